# Optimizing a Trainium2 kernel written in Bass

```python
import jax
import jax.numpy as jnp
from jax import lax
import numpy as np

D_MODEL = 1024
BATCH = 4
SEQ = 4096
DEPTH = 2

GRID_W = 64
CTX_LEN = 256
NA_HEADS = 8
HEAD_DIM = 64
NA_WIDTH = NA_HEADS * HEAD_DIM
WIN_H = 8
WIN_W = 16
Q_COLS = 16
BAND_W = Q_COLS + WIN_W
ROPE_BASE = 10000.0
SG_GROUPS = 8
SG_WIDTH = 512
SG_GROUP_DIM = SG_WIDTH // SG_GROUPS
CHUNK = 128
D_FF = 2816
CONV_W = 3
IN_SIZES = (NA_WIDTH, NA_WIDTH, NA_WIDTH, SG_WIDTH, SG_WIDTH, D_MODEL, D_MODEL)
N_IN = sum(IN_SIZES)
ALPHA = (2 * DEPTH) ** 0.25
BETA = (8 * DEPTH) ** -0.25
LN_EPS = 1e-5
NEG_INF = -1e30

kernel_name = "hybrid_na_sgmlp_diffusion_block"


def layer_norm(x, g, b):
    xf = x.astype(jnp.float32)
    mu = jnp.mean(xf, axis=-1, keepdims=True)
    var = jnp.mean(jnp.square(xf - mu), axis=-1, keepdims=True)
    y = (xf - mu) * lax.rsqrt(var + LN_EPS)
    return (y * g.astype(jnp.float32) + b.astype(jnp.float32)).astype(x.dtype)


def modulate(x, shift, scale):
    return x * (1 + scale) + shift


def split_in(z):
    offs = [int(o) for o in np.cumsum(IN_SIZES)[:-1]]
    return jnp.split(z, offs, axis=-1)


def heads(t):
    return t.reshape(t.shape[0], t.shape[1], NA_HEADS, HEAD_DIM)


def axial_rope(x, row, col):
    half = HEAD_DIM // 2
    nf = half // 2
    inv = ROPE_BASE ** (-jnp.arange(nf, dtype=jnp.float32) / nf)

    def rot(xa, pos):
        ang = pos.astype(jnp.float32)[:, None] * inv[None, :]
        cos = jnp.cos(ang)[None, :, None, :]
        sin = jnp.sin(ang)[None, :, None, :]
        x1 = xa[..., :nf].astype(jnp.float32)
        x2 = xa[..., nf:].astype(jnp.float32)
        return jnp.concatenate([x1 * cos - x2 * sin, x2 * cos + x1 * sin], axis=-1)

    out = jnp.concatenate([rot(x[..., :half], row), rot(x[..., half:], col)], axis=-1)
    return out.astype(x.dtype)


def na_latent(q, k, v, k_ctx, v_ctx, rpb):
    B, n, H, hd = q.shape
    rows = n // GRID_W
    kh = min(WIN_H, rows)
    scale = hd ** -0.5
    t = jnp.arange(n)
    q_rot = axial_rope(q, t // GRID_W, t % GRID_W)
    k_rot = axial_rope(k, t // GRID_W, t % GRID_W)

    r_idx = np.arange(rows)
    row_start = np.clip(r_idx - WIN_H // 2, 0, rows - kh)
    key_rows = row_start[:, None] + np.arange(kh)
    n_qb = GRID_W // Q_COLS
    jb = np.arange(n_qb)
    band_start = np.clip(jb * Q_COLS - WIN_W // 2, 0, GRID_W - BAND_W)
    band_cols = band_start[:, None] + np.arange(BAND_W)
    q_cols = jb[:, None] * Q_COLS + np.arange(Q_COLS)
    win_start = np.clip(q_cols - WIN_W // 2, 0, GRID_W - WIN_W)
    kc = band_cols[:, None, :]
    col_ok = (kc >= win_start[..., None]) & (kc < win_start[..., None] + WIN_W)
    dx_idx = np.clip(kc - q_cols[..., None] + WIN_W - 1, 0, 2 * WIN_W - 2)
    dy_idx = key_rows - r_idx[:, None] + WIN_H - 1
    tok_idx = key_rows[:, None, :, None] * GRID_W + band_cols[None, :, None, :]
    col_ok_j = jnp.asarray(col_ok)[:, :, None, :]
    dx_j = jnp.asarray(dx_idx)
    n_lat = kh * BAND_W

    q_rows = q_rot.reshape(B, rows, n_qb, Q_COLS, H, hd).transpose(1, 0, 2, 3, 4, 5)
    qp_rows = q.reshape(B, rows, n_qb, Q_COLS, H, hd).transpose(1, 0, 2, 3, 4, 5)

    def one_row(args):
        q_r, qp_r, tok_r, dy_r = args
        k_b = k_rot[:, tok_r]
        v_b = v[:, tok_r].reshape(B, n_qb, n_lat, H, hd)
        s_lat = jnp.einsum('bjqhd,bjyxhd->bhjqyx', q_r, k_b).astype(jnp.float32) * scale
        bias = rpb[:, dy_r][:, :, dx_j].transpose(0, 2, 3, 1, 4)
        s_lat = jnp.where(col_ok_j, s_lat + bias.astype(jnp.float32), NEG_INF)
        s_lat = s_lat.reshape(B, H, n_qb, Q_COLS, n_lat)
        s_ctx = jnp.einsum('bjqhd,bchd->bhjqc', qp_r, k_ctx).astype(jnp.float32) * scale
        p = jax.nn.softmax(jnp.concatenate([s_lat, s_ctx], axis=-1), axis=-1)
        p_lat, p_ctx = p[..., :n_lat], p[..., n_lat:]
        o = (jnp.einsum('bhjqk,bjkhd->bjqhd', p_lat, v_b.astype(jnp.float32))
             + jnp.einsum('bhjqc,bchd->bjqhd', p_ctx, v_ctx.astype(jnp.float32)))
        return o.astype(q.dtype)

    out = lax.map(one_row, (q_rows, qp_rows, jnp.asarray(tok_idx, dtype=jnp.int32),
                            jnp.asarray(dy_idx, dtype=jnp.int32)))
    return out.transpose(1, 0, 2, 3, 4, 5).reshape(B, n, H * hd)


def na_context(q, k, v):
    B, n, H, hd = q.shape
    s = jnp.einsum('bqhd,bkhd->bhqk', q, k).astype(jnp.float32) * (hd ** -0.5)
    p = jax.nn.softmax(s, axis=-1)
    o = jnp.einsum('bhqk,bkhd->bqhd', p, v.astype(jnp.float32))
    return o.astype(q.dtype).reshape(B, n, H * hd)


def spatial_gating(u, v, ln_g, ln_b, w_s, b_s):
    B, n, _ = u.shape
    v = layer_norm(v, ln_g, ln_b)
    vc = v.reshape(B, n // CHUNK, CHUNK, SG_GROUPS, SG_GROUP_DIM)
    mixed = jnp.einsum('gpq,bcqgd->bcpgd', w_s, vc) + b_s.T[None, None, :, :, None]
    return u * mixed.reshape(B, n, SG_WIDTH)


def branch_merge(o_a, u, sv, ga, gb, sg_ln_g, sg_ln_b, w_s, b_s, w_pa, w_pb, w_o):
    o_b = spatial_gating(jax.nn.gelu(u), jax.nn.gelu(sv), sg_ln_g, sg_ln_b, w_s, b_s)
    y = jax.nn.sigmoid(ga) * (o_a @ w_pa) + jax.nn.sigmoid(gb) * (o_b @ w_pb)
    return y @ w_o


def conv_ffn(h, w_up, conv_w, conv_b, w_down):
    z = h @ w_up
    z = lax.conv_general_dilated(z, conv_w[:, None, :], window_strides=(1,),
                                 padding=((CONV_W // 2, CONV_W // 2),),
                                 dimension_numbers=('NWC', 'WIO', 'NWC'),
                                 feature_group_count=2 * D_FF) + conv_b
    a, g = jnp.split(z, 2, axis=-1)
    return (jax.nn.silu(g) * a) @ w_down


def setup_inputs(seed: int = 0) -> dict:
    key = jax.random.key(seed)
    ks = jax.random.split(key, 24)
    f32 = jnp.float32

    def nrm(k, shape, s):
        return jax.random.normal(k, shape, f32) * s

    col_scale = jnp.concatenate([jnp.ones((2 * NA_WIDTH,), f32), jnp.full((NA_WIDTH,), BETA, f32),
                                 jnp.ones((N_IN - 3 * NA_WIDTH,), f32)])
    return {
        "x": nrm(ks[0], (BATCH, SEQ, D_MODEL), 1.0),
        "c": nrm(ks[1], (BATCH, D_MODEL), 1.0),
        "ctx": nrm(ks[2], (BATCH, CTX_LEN, D_MODEL), 1.0),
        "c_ctx": nrm(ks[3], (D_MODEL,), 1.0),
        "w_ada": nrm(ks[4], (DEPTH, D_MODEL, 6 * D_MODEL), 0.5 * D_MODEL ** -0.5),
        "b_ada": nrm(ks[5], (DEPTH, 6 * D_MODEL), 0.01),
        "w_in": nrm(ks[6], (DEPTH, D_MODEL, N_IN), D_MODEL ** -0.5) * col_scale,
        "rpb": nrm(ks[7], (DEPTH, NA_HEADS, 2 * WIN_H - 1, 2 * WIN_W - 1), 0.1),
        "sg_ln_g": 1.0 + nrm(ks[8], (DEPTH, SG_WIDTH), 0.01),
        "sg_ln_b": nrm(ks[9], (DEPTH, SG_WIDTH), 0.01),
        "w_s": nrm(ks[10], (DEPTH, SG_GROUPS, CHUNK, CHUNK), CHUNK ** -0.5),
        "b_s": 1.0 + nrm(ks[11], (DEPTH, SG_GROUPS, CHUNK), 0.01),
        "w_pa": nrm(ks[12], (DEPTH, NA_WIDTH, D_MODEL), BETA * NA_WIDTH ** -0.5),
        "w_pb": nrm(ks[13], (DEPTH, SG_WIDTH, D_MODEL), BETA * SG_WIDTH ** -0.5),
        "w_o": nrm(ks[14], (DEPTH, D_MODEL, D_MODEL), BETA * D_MODEL ** -0.5),
        "ln1_g": 1.0 + nrm(ks[15], (DEPTH, D_MODEL), 0.01),
        "ln1_b": nrm(ks[16], (DEPTH, D_MODEL), 0.01),
        "w_up": nrm(ks[17], (DEPTH, D_MODEL, 2 * D_FF), D_MODEL ** -0.5),
        "conv_w": nrm(ks[18], (DEPTH, CONV_W, 2 * D_FF), CONV_W ** -0.5),
        "conv_b": nrm(ks[19], (DEPTH, 2 * D_FF), 0.01),
        "w_down": nrm(ks[20], (DEPTH, D_FF, D_MODEL), BETA * D_FF ** -0.5),
        "ln2_g": 1.0 + nrm(ks[21], (DEPTH, D_MODEL), 0.01),
        "ln2_b": nrm(ks[22], (DEPTH, D_MODEL), 0.01),
    }


def reference(x, c, ctx, c_ctx, w_ada, b_ada, w_in, rpb, sg_ln_g, sg_ln_b, w_s, b_s,
              w_pa, w_pb, w_o, ln1_g, ln1_b, w_up, conv_w, conv_b, w_down, ln2_g, ln2_b):
    for i in range(DEPTH):
        mod = jax.nn.silu(c) @ w_ada[i] + b_ada[i]
        mod_c = jax.nn.silu(c_ctx) @ w_ada[i] + b_ada[i]
        sh_a, sc_a, g_a, sh_f, sc_f, g_f = [m[:, None, :] for m in jnp.split(mod, 6, axis=-1)]
        csh_a, csc_a, cg_a, csh_f, csc_f, cg_f = jnp.split(mod_c, 6, axis=-1)

        hc = modulate(ctx, csh_a, csc_a)
        if i < DEPTH - 1:
            qc, kc, vc, uc, svc, gac, gbc = split_in(hc @ w_in[i])
            k_c, v_c = heads(kc), heads(vc)
            oa_c = na_context(heads(qc), k_c, v_c)
            y_c = branch_merge(oa_c, uc, svc, gac, gbc, sg_ln_g[i], sg_ln_b[i], w_s[i], b_s[i],
                               w_pa[i], w_pb[i], w_o[i])
            ctx_next = layer_norm(ALPHA * ctx + cg_a * y_c, ln1_g[i], ln1_b[i])
            f_c = conv_ffn(modulate(ctx_next, csh_f, csc_f), w_up[i], conv_w[i], conv_b[i], w_down[i])
            ctx_next = layer_norm(ALPHA * ctx_next + cg_f * f_c, ln2_g[i], ln2_b[i])
        else:
            kc, vc = jnp.split(hc @ w_in[i][:, NA_WIDTH:3 * NA_WIDTH], 2, axis=-1)
            k_c, v_c = heads(kc), heads(vc)
            ctx_next = ctx

        h = modulate(x, sh_a, sc_a)
        q, k, v, u, sv, ga, gb = split_in(h @ w_in[i])
        oa = na_latent(heads(q), heads(k), heads(v), k_c, v_c, rpb[i])
        y = branch_merge(oa, u, sv, ga, gb, sg_ln_g[i], sg_ln_b[i], w_s[i], b_s[i],
                         w_pa[i], w_pb[i], w_o[i])
        x = layer_norm(ALPHA * x + g_a * y, ln1_g[i], ln1_b[i])
        f = conv_ffn(modulate(x, sh_f, sc_f), w_up[i], conv_w[i], conv_b[i], w_down[i])
        x = layer_norm(ALPHA * x + g_f * f, ln2_g[i], ln2_b[i])
        ctx = ctx_next
    return x
```

```python
import numpy as np
import concourse.bass as bass
import concourse.mybir as mybir
from concourse.bass_utils import run_bass_kernel_spmd

F32, BF16 = mybir.dt.float32, mybir.dt.bfloat16
AF = mybir.ActivationFunctionType
ALU = mybir.AluOpType

D = 1024
NH = 8
HD = 64
DFF = 2816
NCH = 22
DEPTH = 2
ALPHA = float((2 * DEPTH) ** 0.25)
LN_EPS = 1e-5
MASKV = -30000.0
NSLOT_M = 15
ENGS = ("pe", "act", "dve", "pool", "sp")

S_KA, S_KB, S_V, S_QA, S_QB, S_U, S_SV = 0, 1, 2, 3, 4, 5, 6
S_GA0, S_GB0, S_PAB0, S_GA1, S_GB1, S_PAB1, S_WO0, S_WO1 = 7, 8, 9, 10, 11, 12, 13, 14
S_E0, S_E1 = 15, 16
S_UP0 = 17
S_DN0 = 28
NSLOTS = 34


def _perm128():
    j = np.arange(128)
    d = j % 64
    pd = np.where((d % 32) < 16, d + 16, d - 16)
    return (j // 64) * 64 + pd


def _slot_cols(W, cols):
    blk = W[:, cols]
    return blk.reshape(8, 128, 512).transpose(1, 0, 2).reshape(128, 4096)


def bias_tile(half, jq, jk, rpb):
    kk = np.arange(128)
    lq_row = 2 * jq + kk // 64
    lq_col = kk % 64
    lk_row = 2 * jk + kk // 64
    lk_col = kk % 64
    if half == 0:
        r, qc, kr, kc = lq_row, lq_col, lk_row, lk_col
    else:
        r, qc, kr, kc = 63 - lq_row, 63 - lq_col, 63 - lk_row, 63 - lk_col
    rs = np.clip(r - 4, 0, 56)
    ws = np.clip(qc - 8, 0, 48)
    KR, KC = kr[:, None], kc[:, None]
    valid = ((KR >= rs[None, :]) & (KR < rs[None, :] + 8) & (KC >= ws[None, :]) & (KC < ws[None, :] + 16)
             & (KR >= 0) & (KR <= 63))
    dy = np.clip(KR - r[None, :] + 7, 0, 14)
    dx = np.clip(KC - qc[None, :] + 15, 0, 30)
    vals = rpb[:, dy, dx]
    out = np.where(valid[None], vals, np.float32(MASKV)).astype(np.float32)
    return np.ascontiguousarray(out.transpose(1, 0, 2))


def build_layer_slots(inp, l, half):
    w_in = inp["w_in"][l]
    perm = _perm128()
    slots = np.zeros((NSLOTS, 128, 4096), np.float32)

    def qk_cols(base, hp0):
        c = []
        for hp in (hp0, hp0 + 1):
            c.append(base + hp * 128 + np.arange(128))
        for hp in (hp0, hp0 + 1):
            c.append(base + hp * 128 + perm)
        return np.concatenate(c)

    slots[S_KA] = _slot_cols(w_in, qk_cols(512, 0))
    slots[S_KB] = _slot_cols(w_in, qk_cols(512, 2))
    slots[S_V] = _slot_cols(w_in, 1024 + np.arange(512))
    slots[S_QA] = _slot_cols(w_in, qk_cols(0, 0))
    slots[S_QB] = _slot_cols(w_in, qk_cols(0, 2))
    slots[S_U] = _slot_cols(w_in, 1536 + np.arange(512))
    slots[S_SV] = _slot_cols(w_in, 2048 + np.arange(512))
    for s in range(2):
        slots[S_GA0 + 3 * s] = _slot_cols(w_in, 2560 + s * 512 + np.arange(512))
        slots[S_GB0 + 3 * s] = _slot_cols(w_in, 3584 + s * 512 + np.arange(512))
        pa = inp["w_pa"][l][:, s * 512:(s + 1) * 512].reshape(4, 128, 512)
        pb = inp["w_pb"][l][:, s * 512:(s + 1) * 512].reshape(4, 128, 512)
        slots[S_PAB0 + 3 * s] = np.concatenate([pa, pb], 0).transpose(1, 0, 2).reshape(128, 4096)
        slots[S_WO0 + s] = _slot_cols(inp["w_o"][l], s * 512 + np.arange(512))
    rpb = inp["rpb"][l]
    for jq, sl in ((0, S_E0), (1, S_E1)):
        t = np.stack([bias_tile(half, jq, jk, rpb) for jk in range(4)], 1)
        slots[sl] = t.reshape(128, 4096)
    w_up = inp["w_up"][l]
    for s in range(11):
        cols = np.concatenate([(2 * s) * 128 + np.arange(128), (2 * s + 1) * 128 + np.arange(128),
                               DFF + (2 * s) * 128 + np.arange(128), DFF + (2 * s + 1) * 128 + np.arange(128)])
        slots[S_UP0 + s] = _slot_cols(w_up, cols)
    wd = np.zeros((24 * 128, 1024), np.float32)
    wd[:DFF] = inp["w_down"][l]
    for h in range(2):
        for i in range(3):
            blk = wd[i * 1024:(i + 1) * 1024, h * 512:(h + 1) * 512]
            slots[S_DN0 + h * 3 + i] = blk.reshape(8, 128, 512).transpose(1, 0, 2).reshape(128, 4096)
    bint = np.stack([bias_tile(half, 4, 4 + o, rpb) for o in range(-2, 3)], 1).reshape(128, 5 * 8 * 128)
    return slots, np.ascontiguousarray(bint)


def rope_table(half, ntok):
    tau = np.arange(ntok)
    t = tau if half == 0 else 4095 - tau
    row = (t // 64).astype(np.float32)
    col = (t % 64).astype(np.float32)
    inv = (np.float32(10000.0) ** (-np.arange(16, dtype=np.float32) / np.float32(16))).astype(np.float32)
    p = np.arange(128)
    d = p % 64
    f = d % 16
    pos = np.where((d < 32)[:, None], row[None, :], col[None, :]).astype(np.float32)
    ang = (pos * inv[f][:, None]).astype(np.float32)
    sgn = np.where((d % 32) < 16, -1.0, 1.0).astype(np.float32)[:, None]
    C = np.cos(ang).astype(np.float32)
    S = (np.sin(ang).astype(np.float32) * sgn).astype(np.float32)
    return np.ascontiguousarray(np.stack([C, S], 1))


def layer_small(inp, l, half):
    w_s = inp["w_s"][l]
    b_s = inp["b_s"][l]
    cw = inp["conv_w"][l]
    if half == 1:
        w_s = w_s[:, ::-1, ::-1]
        b_s = b_s[:, ::-1]
        cw = cw[::-1]
    wsT = np.ascontiguousarray(w_s.transpose(2, 0, 1)).reshape(128, 8 * 128)
    bsT = np.ascontiguousarray(b_s.T)
    cb = inp["conv_b"][l]
    convp = np.stack([cw[0], cw[1], cw[2], cb], -1).reshape(44, 128, 4).transpose(1, 0, 2)
    sgln = np.stack([inp["sg_ln_g"][l], inp["sg_ln_b"][l]])[None]
    lnrow = np.stack([inp["ln1_g"][l], inp["ln1_b"][l], inp["ln2_g"][l], inp["ln2_b"][l]])[None]
    lnfm = np.stack([inp["ln1_g"][l].reshape(8, 128).T, inp["ln1_b"][l].reshape(8, 128).T], 1)
    wada = np.ascontiguousarray(inp["w_ada"][l].reshape(8, 128, 6144).transpose(1, 0, 2))
    bada = inp["b_ada"][l][None]
    return dict(wsT=np.ascontiguousarray(wsT, dtype=np.float32), bsT=np.ascontiguousarray(bsT, dtype=np.float32),
                convp=np.ascontiguousarray(convp, dtype=np.float32), sgln=np.ascontiguousarray(sgln, dtype=np.float32),
                lnrow=np.ascontiguousarray(lnrow, dtype=np.float32), lnfm=np.ascontiguousarray(lnfm, dtype=np.float32),
                wada=wada, bada=np.ascontiguousarray(bada, dtype=np.float32))


class Prog:
    def __init__(self):
        self.q = {e: [] for e in ENGS}
        self.cnt = {}
        self.known = {e: {} for e in ENGS}
        self.last_w = {}
        self.readers = {}
        self.semnames = set(ENGS)

    def op(self, eng, fn, reads=(), writes=(), sem=None, inc=1):
        writes = list(writes) + [k for k in reads if k[0] == "ps" and k not in writes]
        reads = [k for k in reads if k[0] != "ps"]
        deps = {}

        def add(tok):
            if tok is None:
                return
            s, v, e = tok
            if e == "pe" and eng == "pe" and s == "pe":
                return
            if deps.get(s, 0) < v:
                deps[s] = v
        for k in reads:
            add(self.last_w.get(k))
        for k in writes:
            add(self.last_w.get(k))
            for s, (v, e) in self.readers.get(k, {}).items():
                add((s, v, e))
        kn = self.known[eng]
        for s, v in deps.items():
            if kn.get(s, 0) < v:
                self.q[eng].append(("wait", s, v))
                kn[s] = v
        s = sem or eng
        self.semnames.add(s)
        self.cnt[s] = self.cnt.get(s, 0) + inc
        tok = (s, self.cnt[s], eng)
        self.q[eng].append(("ins", fn, s, inc))
        for k in writes:
            self.last_w[k] = tok
            self.readers[k] = {}
        for k in reads:
            d = self.readers.setdefault(k, {})
            if d.get(s, (0, None))[0] < tok[1]:
                d[s] = (tok[1], eng)
        return tok

    def wait_all(self, eng, semname):
        v = self.cnt.get(semname, 0)
        if v and self.known[eng].get(semname, 0) < v:
            self.q[eng].append(("wait", semname, v))
            self.known[eng][semname] = v

    def replay(self, eng, e, sems):
        for it in self.q[eng]:
            if it[0] == "wait":
                e.wait_ge(sems[it[1]], it[2])
            else:
                ins = it[1](e)
                ins.then_inc(sems[it[2]], it[3])


class Mem:
    def __init__(self):
        self.off = 0
        self.peak = 0

    def alloc(self, nbytes):
        o = self.off
        self.off += (nbytes + 63) // 64 * 64
        self.peak = max(self.peak, self.off)
        return o


class Builder:
    def __init__(self, layers, n_x_in, out_tiles, ctx_out, dbg=None):
        self.layers = layers
        self.n_x_in = n_x_in
        self.out_tiles = out_tiles
        self.ctx_out = ctx_out
        self.dbg = dbg
        self.P = Prog()
        self.nc = bass.Bass("TRN2", target_bir_lowering=False)
        self.bank_i = 0
        self.fbank_i = 0
        self.wacq = 0
        self.wrel_n = 0
        self.wemit = 0
        self.wplan = []
        self.wcached = set()
        self._plan()
        self.wlast = {}
        for u, k in enumerate(self.wplan):
            self.wlast[k] = u

    def _plan(self):
        for li, L in enumerate(self.layers):
            for g in self.m_groups(L):
                if g["kind"] == "ctx" and L["ctx"] == "kv":
                    seq = [S_KA, S_KB, S_V]
                else:
                    seq = [S_KA, S_KB, S_V, S_QA, S_QB, S_SV, S_U]
                    if g["kind"] == "lat" and g["G"] == 0:
                        seq += [S_E0, S_E1]
                    seq += list(range(7, 15))
                for s in seq:
                    self.wplan.append((li, s))
            if self.dbg == "mid%d" % li:
                break
            for g in self.f_groups(L):
                for s in range(S_UP0, S_UP0 + 17):
                    self.wplan.append((li, s))

    def m_groups(self, L):
        gs = [dict(kind="ctx", G=-1, q=[0, 1], kv=[0, 1], base=0)]
        nmix, nkv = L["nmix"], L["nkv"]
        G = 0
        while 4 * G < nmix:
            q = list(range(4 * G, min(4 * G + 4, nmix)))
            kv = list(range(0, min(6, nkv))) if G == 0 else list(range(4 * G + 2, min(4 * G + 6, nkv)))
            gs.append(dict(kind="lat", G=G, q=q, kv=kv, base=4 * G))
            G += 1
        return gs

    def f_groups(self, L):
        gs = []
        if L["ctx"] == "full":
            gs.append(dict(kind="ctx", G=-1, t=[0, 1]))
        G = 0
        while 4 * G < L["nffn"]:
            gs.append(dict(kind="lat", G=G, t=list(range(4 * G, min(4 * G + 4, L["nffn"])))))
            G += 1
        return gs

    def raw(self, off, dt, n):
        if dt == BF16:
            return self.mem[:, off // 2: off // 2 + n]
        return self.mem[:, off // 2: off // 2 + 2 * n].bitcast(F32)

    def view(self, off, dt, shape):
        ap = self.raw(off, dt, int(np.prod(shape)))
        if len(shape) == 2:
            return ap.rearrange("p (a b) -> p a b", b=shape[1])
        if len(shape) == 3:
            return ap.rearrange("p (a b c) -> p a b c", b=shape[1], c=shape[2])
        return ap

    def bank(self):
        b = self.bank_i
        self.bank_i = (b + 1) % 8
        return b

    def xslot(self, kind, t):
        if kind == "ctx":
            return 19 + t
        return t if t <= 18 else t + 2

    def wget(self, li, s):
        u = self.wacq
        assert self.wplan[u] == (li, s), (u, self.wplan[u], li, s)
        assert u < self.wrel_n + 3
        self._wpump()
        assert self.wemit > u
        self.wacq += 1
        return u % 3

    def wrel(self):
        self.wrel_n += 1
        assert self.wrel_n <= self.wacq
        self._wpump()

    def _wpump(self):
        while self.wemit < min(self.wrel_n + 3, len(self.wplan)):
            u = self.wemit
            li, s = self.wplan[u]
            slot = u % 3
            dst = self.wring[slot]
            if (li, s) in self.wcached:
                src = self.dr["wsb%d" % li][s]
                self.P.op("pool", lambda e, dst=dst, src=src: e.dma_start(out=dst, in_=src),
                          reads=[("wsb", li, s)], writes=[("w", slot)], sem="w%d" % slot, inc=16)
            else:
                src = self.dr["ws%d" % li][s]
                self.P.op("pool", lambda e, dst=dst, src=src: e.dma_start(out=dst, in_=src),
                          writes=[("w", slot)], sem="w%d" % slot, inc=16)
                if self.wlast[(li, s)] > u and s not in (S_E0, S_E1):
                    sdst = self.dr["wsb%d" % li][s]
                    self.P.op("sp", lambda e, dst=dst, sdst=sdst: e.dma_start(out=sdst, in_=dst),
                              reads=[("w", slot)], writes=[("wsb", li, s)], sem="wst%d" % slot, inc=16)
                    self.wcached.add((li, s))
            self.wemit = u + 1

    def dma(self, eng, dst, src, writes, sem, reads=(), **kw):
        return self.P.op(eng, lambda e: e.dma_start(out=dst, in_=src, **kw), reads=reads, writes=writes, sem=sem, inc=16)

    def mm(self, out_ap, pairs, reads, bank):
        def fn(e):
            n = len(pairs)
            ins = None
            for i, (l, r) in enumerate(pairs):
                ins = e.matmul(out_ap, l, r, start=(i == 0), stop=(i == n - 1))
            return ins
        return self.P.op("pe", fn, reads=reads, writes=[("ps", bank)])

    def barrier(self):
        P = self.P
        for eng in ENGS:
            for s in ["pe", "act", "dve"] + [k for k in P.cnt if k.startswith("ms")]:
                if s != eng:
                    P.wait_all(eng, s)

    def build(self):
        nc = self.nc
        P = self.P
        dr = {}
        self.dr = dr
        NL = len(self.layers)
        nxt = self.n_x_in

        def din(name, shape):
            dr[name] = nc.dram_tensor(name, list(shape), F32, kind="ExternalInput").ap()
        din("x", (nxt * 128, D))
        din("ctx", (256, D))
        din("cfm", (128, 16))
        din("ident", (128, 128))
        din("rope", (128, 2, nxt * 128))
        for li in range(NL):
            din("ws%d" % li, (NSLOTS, 128, 4096))
            din("bint%d" % li, (128, 5 * 8 * 128))
            din("wsT%d" % li, (128, 1024))
            din("bsT%d" % li, (128, 8))
            din("convp%d" % li, (128, 44 * 4))
            din("sgln%d" % li, (1, 1024))
            din("lnrow%d" % li, (1, 4096))
            din("wada%d" % li, (128, 8, 6144))
            din("bada%d" % li, (1, 6144))
        dr["out"] = nc.dram_tensor("out", [self.out_tiles * 128, D], F32, kind="ExternalOutput").ap()
        if self.ctx_out:
            dr["ctxout"] = nc.dram_tensor("ctxout", [256, D], F32, kind="ExternalOutput").ap()
        if self.dbg:
            dr["dbg"] = nc.dram_tensor("dbg", [21 * 128, D], F32, kind="ExternalOutput").ap()
        dr["modscr"] = nc.dram_tensor("modscr", [NL * 2, 6144], F32, kind="Internal").ap()
        for li in range(NL):
            dr["wsb%d" % li] = nc.dram_tensor("wsb%d" % li, [NSLOTS, 128, 4096], BF16, kind="Internal").ap()

        M = Mem()
        NXS = 23 if max(L_['nkv'] for L_ in self.layers) > 19 else 21
        o_x = M.alloc(NXS * 4096)
        o_w = [M.alloc(8192) for _ in range(3)]
        o_identb = M.alloc(256)
        o_identf = M.alloc(512)
        o_wsT = M.alloc(2048)
        o_bsT = M.alloc(64)
        o_modfm = M.alloc(2 * 6 * 8 * 4)
        o_vec = M.alloc(8 * 8 * 4)
        o_stat = M.alloc(512)
        mark = M.off
        o_kT = M.alloc(4 * 8 * 128 * 2)
        o_V = M.alloc(8 * 520 * 2)
        o_ckT = M.alloc(4 * 256 * 2)
        o_cV = M.alloc(2 * 520 * 2)
        o_bint = M.alloc(5 * 8 * 128 * 2)
        o_hT = M.alloc(8 * 768 * 2)
        o_q = M.alloc(8 * 512 * 2)
        o_gu = M.alloc(4 * 512 * 2)
        o_vln = M.alloc(4 * 512 * 2)
        o_pt = M.alloc(3 * 1024 * 2)
        o_rope = M.alloc(2 * 768 * 4)
        o_sgb = o_rope
        o_t = o_rope + 4096
        o_oa = M.alloc(512 * 2)
        o_sgln = M.alloc(2 * 512 * 4)
        o_gabc = M.alloc(4096)
        o_ln1bc = M.alloc(8192)
        m_end = M.off
        M.off = mark
        o_hfT = [M.alloc(8 * 514 * 2) for _ in range(2)]
        o_prod = M.alloc(22 * 512 * 2)
        o_ta = [M.alloc(2048) for _ in range(4)]
        o_tg = [M.alloc(2048) for _ in range(4)]
        o_convp = M.alloc(44 * 4 * 4)
        o_fbc = M.alloc(3 * 4096)
        f_end = M.off
        M.off = mark
        o_wada = [M.alloc(8192) for _ in range(2)]
        o_modrow = [M.alloc(2048) for _ in range(2)]
        o_bada = [M.alloc(2048) for _ in range(2)]
        o_scol = M.alloc(32 * 4)
        o_sbf = M.alloc(64)
        total = M.peak
        self.sbuf_bytes = total
        assert total <= 212800, total

        import contextlib
        with contextlib.ExitStack() as st:
            self.mem = st.enter_context(nc.sbuf_tensor("mem", [128, total // 2], BF16))
            ps = [st.enter_context(nc.psum_tensor("ps%d" % i, [128, 512], F32)) for i in range(8)]
            self.ps = ps
            v, raw = self.view, self.raw
            X = v(o_x, F32, (NXS, D))
            self.wring = [raw(o, BF16, 4096) for o in o_w]
            identb = raw(o_identb, BF16, 128)
            identf = raw(o_identf, F32, 128)
            wsT = v(o_wsT, BF16, (8, 128))
            bsT = raw(o_bsT, F32, 8)
            modfm = v(o_modfm, F32, (2, 6, 8))
            vec = v(o_vec, F32, (8, 8))
            stat = raw(o_stat, F32, 128)
            self.stat = stat
            kT = v(o_kT, BF16, (4, 1024))
            Vr = v(o_V, BF16, (8, 520))
            ckT = v(o_ckT, BF16, (4, 256))
            cV = v(o_cV, BF16, (2, 520))
            bint = v(o_bint, BF16, (5, 8, 128))
            hT = v(o_hT, BF16, (8, 768))
            qT = v(o_q, BF16, (8, 512))[:, 0:4, :]
            qrT = v(o_q, BF16, (8, 512))[:, 4:8, :]
            yT = v(o_q, BF16, (8, 512))
            gu = v(o_gu, BF16, (4, 512))
            oaT = v(o_gu, BF16, (4, 512))
            vln = v(o_vln, BF16, (4, 512))
            obT = v(o_vln, BF16, (4, 512))
            PT = [raw(o_pt + i * 2048, BF16, 1024) for i in range(3)]
            tmpf = [raw(o_pt + i * 2048, F32, 512) for i in range(2)]
            sga = v(o_pt, BF16, (4, 512))
            sgb = v(o_sgb, BF16, (4, 512))
            tt_ = raw(o_t, F32, 512)
            rope = v(o_rope, F32, (2, 768))
            oa = raw(o_oa, BF16, 512)
            sgln = v(o_sgln, F32, (2, 512))
            gabc = raw(o_gabc, F32, 1024)
            ln1bc = v(o_ln1bc, F32, (2, 1024))
            hfT = [v(o, BF16, (8, 514)) for o in o_hfT]
            prodT = v(o_prod, BF16, (22, 512))
            ta2 = [raw(o, F32, 512) for o in o_ta]
            tg2 = [raw(o, F32, 512) for o in o_tg]
            convp = v(o_convp, F32, (44, 4))
            fbc = v(o_fbc, F32, (3, 1024))
            wadab = [v(o, BF16, (8, 512)) for o in o_wada]
            sbf = raw(o_sbf, BF16, 16)
            modrow = [raw(o, F32, 512) for o in o_modrow]
            badab = [raw(o, F32, 512) for o in o_bada]
            scol = raw(o_scol, F32, 32)

            self.dma("sp", identf, dr["ident"], [("identf",)], "t0")
            self.dma("pool", identb, dr["ident"], [("identb",)], "t1")
            self.dma("sp", scol[:, 0:16], dr["cfm"], [("scol",)], "t2")
            P.op("act", lambda e: e.activation(out=sbf, in_=scol[:, 0:16], func=AF.Silu),
                 reads=[("scol",)], writes=[("ssil",)])
            for li in range(NL):
                for nb in range(12):
                    pb = nb % 2
                    wb = wadab[pb]
                    self.dma("pool", wb, dr["wada%d" % li][:, :, nb * 512:(nb + 1) * 512], [("wada", pb)], "wa%d" % pb)
                    self.dma("sp", badab[pb][0:2, :], dr["bada%d" % li][0:1, nb * 512:(nb + 1) * 512].partition_broadcast(2), [("bada", pb)], "ba%d" % pb)
                    b = self.bank()
                    pairs = [(sbf[:, kc:16:8], wb[:, kc, :]) for kc in range(8)]
                    self.mm(ps[b][0:2, :], pairs, [("ssil",), ("wada", pb)], b)
                    P.op("dve", lambda e, b=b, pb=pb: e.tensor_tensor(out=modrow[pb][0:2, :], in0=ps[b][0:2, :], in1=badab[pb][0:2, :], op=ALU.add),
                         reads=[("ps", b), ("bada", pb)], writes=[("modrow", pb)])
                    self.dma("sp", dr["modscr"][2 * li:2 * li + 2, nb * 512:(nb + 1) * 512], modrow[pb][0:2, :], [("modscr", li, nb)], "ms%d" % pb,
                             reads=[("modrow", pb)])
            for t in range(nxt):
                sl = self.xslot("lat", t)
                self.dma("sp", X[:, sl, :], dr["x"][t * 128:(t + 1) * 128, :], [("x", sl)], "x%d" % sl)
            for t in range(2):
                sl = 19 + t
                self.dma("sp", X[:, sl, :], dr["ctx"][t * 128:(t + 1) * 128, :], [("x", sl)], "x%d" % sl)
            MODSCR = lambda li: [("modscr", li, nb) for nb in range(12)]
            self.barrier()

            for li, L in enumerate(self.layers):
                last_layer = (li == NL - 1)
                rd = MODSCR(li)
                P.op("dve", lambda e: e.memset(raw(o_V, BF16, 8 * 520), 1.0), writes=[("V", s) for s in range(8)])
                P.op("dve", lambda e: e.memset(raw(o_cV, BF16, 2 * 520), 1.0), writes=[("cV", 0), ("cV", 1)])
                self.dma("sp", modfm, dr["modscr"][2 * li:2 * li + 2, :].rearrange("w (m kc p) -> p w m kc", m=6, kc=8, p=128),
                         [("modfm",)], "t4", reads=rd, allow_slow_non_contiguous=True)
                self.dma("pool", wsT, dr["wsT%d" % li].rearrange("p (g q) -> p g q", g=8), [("wsT",)], "t6")
                self.dma("sp", bsT, dr["bsT%d" % li], [("bsT",)], "t7")
                self.dma("pool", raw(o_bint, BF16, 5120), dr["bint%d" % li], [("bint",)], "t8")
                P.op("dve", lambda e: e.tensor_scalar(out=raw(o_bint, BF16, 5120), in0=raw(o_bint, BF16, 5120), scalar1=8.0, scalar2=None, op0=ALU.mult),
                     reads=[("bint",)], writes=[("bint",)])
                self.dma("sp", raw(o_sgln, F32, 1024), dr["sgln%d" % li].partition_broadcast(128), [("sgln",)], "t9")
                self.dma("sp", raw(o_ln1bc, F32, 2048), dr["lnrow%d" % li][0:1, 0:2048].partition_broadcast(128), [("ln1bc",)], "t10")

                def dv(e):
                    ins = None
                    for w in range(2):
                        e.tensor_scalar(out=vec[:, 2 * w, :], in0=modfm[:, w, 1, :], scalar1=1.0, scalar2=None, op0=ALU.add)
                        e.tensor_copy(out=vec[:, 2 * w + 1, :], in_=modfm[:, w, 0, :])
                        e.tensor_scalar(out=vec[:, 4 + 2 * w, :], in0=modfm[:, w, 4, :], scalar1=1.0, scalar2=None, op0=ALU.add)
                        ins = e.tensor_copy(out=vec[:, 5 + 2 * w, :], in_=modfm[:, w, 3, :])
                    return ins
                P.op("dve", dv, reads=[("modfm",)], writes=[("vec",)])
                VEC = [("vec",)]

                def m_ht(g):
                    kind = g["kind"]
                    ctxkv_only = (kind == "ctx" and L["ctx"] == "kv")
                    qtiles, kvtiles, base = g["q"], g["kv"], g["base"]
                    ntq = len(qtiles)
                    nq = ntq * 128
                    tiles_h = sorted(set(qtiles) | set(kvtiles)) if not ctxkv_only else kvtiles
                    npos = tiles_h[-1] - base + 1
                    wv = 2 if kind == "ctx" else 0
                    isg0 = (kind == "lat" and g["G"] == 0)
                    if kind == "lat":
                        self.dma("sp", rope[:, :, 0:npos * 128], dr["rope"][:, :, base * 128:(base + npos) * 128], [("rope",)], "t12")
                    for p in range(npos):
                        sl = self.xslot(kind, base + p)
                        for cb in range(2):
                            b = self.bank()

                            def tr(e, sl=sl, cb=cb, b=b):
                                ins = None
                                for ci in range(4):
                                    c = cb * 4 + ci
                                    ins = e.transpose(ps[b][:, ci * 128:(ci + 1) * 128], X[:, sl, c * 128:(c + 1) * 128], identf)
                                return ins
                            P.op("pe", tr, reads=[("x", sl), ("identf",)], writes=[("ps", b)])
                            eng = "act" if cb == 0 else "dve"

                            def ev(e, p=p, cb=cb, b=b, eng=eng, wv=wv):
                                ins = None
                                for ci in range(4):
                                    c = cb * 4 + ci
                                    o = hT[:, c, p * 128:(p + 1) * 128]
                                    i_ = ps[b][:, ci * 128:(ci + 1) * 128]
                                    if eng == "act":
                                        ins = e.activation(out=o, in_=i_, func=AF.Identity, bias=vec[:, wv + 1, c:c + 1], scale=vec[:, wv, c:c + 1])
                                    else:
                                        ins = e.tensor_scalar(out=o, in0=i_, scalar1=vec[:, wv, c:c + 1], scalar2=vec[:, wv + 1, c:c + 1],
                                                              op0=ALU.mult, op1=ALU.add)
                                return ins
                            P.op(eng, ev, reads=[("ps", b)] + VEC, writes=[("hT", p, cb)])


                def HT(p0, p1):
                    return [("hT", p, cb) for p in range(p0, p1) for cb in range(2)]

                def m_group(g, between):
                    kind = g["kind"]
                    ctxkv_only = (kind == "ctx" and L["ctx"] == "kv")
                    qtiles, kvtiles, base = g["q"], g["kv"], g["base"]
                    ntq = len(qtiles)
                    nq = ntq * 128
                    wv = 2 if kind == "ctx" else 0
                    isg0 = (kind == "lat" and g["G"] == 0)
                    if not ctxkv_only and (kind == "ctx" or isg0):
                        w = 1 if kind == "ctx" else 0
                        r_ = 2 * li + w
                        self.dma("sp", gabc, dr["modscr"][r_:r_ + 1, 2048:3072].partition_broadcast(128), [("gabc",)], "t11", reads=rd)
                    kp0 = kvtiles[0] - base
                    kp1 = kvtiles[-1] - base + 1
                    for si, S in enumerate((S_KA, S_KB)):
                        ws_ = self.wget(li, S)
                        W = self.wring[ws_].rearrange("p (k n) -> p k n", k=8)
                        for hl in range(2):
                            hp = si * 2 + hl
                            t0 = kp0 * 128
                            while t0 < kp1 * 128:
                                t1 = min(t0 + 512, kp1 * 128)
                                n = t1 - t0
                                b1 = self.bank()
                                self.mm(ps[b1][:, 0:n], [(W[:, kc, hl * 128:(hl + 1) * 128], hT[:, kc, t0:t1]) for kc in range(8)],
                                        [("w", ws_)] + HT(t0 // 128, t1 // 128), b1)
                                if kind == "ctx":
                                    P.op("act", lambda e, b1=b1, hp=hp, t0=t0, t1=t1, n=n: e.activation(out=ckT[:, hp, t0:t1], in_=ps[b1][:, 0:n], func=AF.Copy),
                                         reads=[("ps", b1)], writes=[("ckT", hp)])
                                else:
                                    b2 = self.bank()
                                    self.mm(ps[b2][:, 0:n], [(W[:, kc, 256 + hl * 128:256 + (hl + 1) * 128], hT[:, kc, t0:t1]) for kc in range(8)],
                                            [("w", ws_)] + HT(t0 // 128, t1 // 128), b2)
                                    P.op("dve", lambda e, b1=b1, t0=t0, t1=t1, n=n: e.tensor_tensor(out=tmpf[0][:, 0:n], in0=ps[b1][:, 0:n], in1=rope[:, 0, t0:t1], op=ALU.mult),
                                         reads=[("ps", b1), ("rope",)], writes=[("pt", 0)])
                                    P.op("dve", lambda e, b2=b2, t0=t0, t1=t1, n=n: e.tensor_tensor(out=ps[b2][:, 0:n], in0=ps[b2][:, 0:n], in1=rope[:, 1, t0:t1], op=ALU.mult),
                                         reads=[("ps", b2), ("rope",)], writes=[("ps", b2)])
                                    for tt in range(t0 // 128, t1 // 128):
                                        slot = (base + tt) % 8
                                        c0 = tt * 128 - t0
                                        P.op("dve", lambda e, b2=b2, c0=c0, slot=slot, hp=hp: e.tensor_tensor(
                                            out=kT[:, hp, slot * 128:(slot + 1) * 128], in0=tmpf[0][:, c0:c0 + 128], in1=ps[b2][:, c0:c0 + 128], op=ALU.add),
                                            reads=[("pt", 0), ("ps", b2)], writes=[("kT", slot, hp)])
                                t0 = t1
                        self.wrel()
                    ws_ = self.wget(li, S_V)
                    W = self.wring[ws_].rearrange("p (k n) -> p k n", k=8)
                    for p in range(kp0, kp1):
                        b = self.bank()
                        self.mm(ps[b][:, :], [(hT[:, kc, p * 128:(p + 1) * 128], W[:, kc, :]) for kc in range(8)], [("w", ws_)] + HT(p, p + 1), b)
                        if kind == "ctx":
                            dst, key = cV[:, p, :].rearrange("p (h d) -> p h d", d=65)[:, :, 0:64], ("cV", p)
                        else:
                            slot = (base + p) % 8
                            dst, key = Vr[:, slot, :].rearrange("p (h d) -> p h d", d=65)[:, :, 0:64], ("V", slot)
                        P.op("act", lambda e, b=b, dst=dst: e.activation(out=dst, in_=ps[b][:, :].rearrange("p (h d) -> p h d", d=64), func=AF.Copy),
                             reads=[("ps", b)], writes=[key])
                    self.wrel()
                    if ctxkv_only:
                        if between is not None:
                            between()
                        return
                    for si, S in enumerate((S_QA, S_QB)):
                        ws_ = self.wget(li, S)
                        W = self.wring[ws_].rearrange("p (k n) -> p k n", k=8)
                        for hl in range(2):
                            hp = si * 2 + hl
                            b1 = self.bank()
                            self.mm(ps[b1][:, 0:nq], [(W[:, kc, hl * 128:(hl + 1) * 128], hT[:, kc, 0:nq]) for kc in range(8)], [("w", ws_)] + HT(0, ntq), b1)
                            P.op("act", lambda e, b1=b1, hp=hp: e.activation(out=qT[:, hp, 0:nq], in_=ps[b1][:, 0:nq], func=AF.Copy),
                                 reads=[("ps", b1)], writes=[("q", hp)])
                            if kind == "lat":
                                b2 = self.bank()
                                self.mm(ps[b2][:, 0:nq], [(W[:, kc, 256 + hl * 128:256 + (hl + 1) * 128], hT[:, kc, 0:nq]) for kc in range(8)],
                                        [("w", ws_)] + HT(0, ntq), b2)
                                P.op("dve", lambda e, b1=b1: e.tensor_tensor(out=tmpf[0][:, 0:nq], in0=ps[b1][:, 0:nq], in1=rope[:, 0, 0:nq], op=ALU.mult),
                                     reads=[("ps", b1), ("rope",)], writes=[("pt", 0)])
                                P.op("dve", lambda e, b2=b2: e.tensor_tensor(out=ps[b2][:, 0:nq], in0=ps[b2][:, 0:nq], in1=rope[:, 1, 0:nq], op=ALU.mult),
                                     reads=[("ps", b2), ("rope",)], writes=[("ps", b2)])
                                P.op("dve", lambda e, b2=b2, hp=hp: e.tensor_tensor(out=qrT[:, hp, 0:nq], in0=tmpf[0][:, 0:nq], in1=ps[b2][:, 0:nq], op=ALU.add),
                                     reads=[("pt", 0), ("ps", b2)], writes=[("q", 4 + hp)])
                        self.wrel()
                    ws_ = self.wget(li, S_SV)
                    W = self.wring[ws_].rearrange("p (k n) -> p k n", k=8)
                    for i in range(ntq):
                        b = self.bank()
                        self.mm(ps[b][:, :], [(hT[:, kc, i * 128:(i + 1) * 128], W[:, kc, :]) for kc in range(8)], [("w", ws_)] + HT(i, i + 1), b)
                        tf = tmpf[i % 2]
                        pk = ("pt", i % 2)
                        P.op("act", lambda e, b=b, tf=tf: e.activation(out=tf, in_=ps[b][:, :], func=AF.Gelu_apprx_tanh),
                             reads=[("ps", b)], writes=[pk])
                        so = (i % 2) * 16
                        P.op("dve", lambda e, tf=tf, so=so: e.bn_stats(stat[:, so:so + 6], tf), reads=[pk], writes=[("stat", i % 2, 0)])
                        P.op("dve", lambda e, so=so: e.bn_aggr(stat[:, so + 6:so + 8], stat[:, so:so + 6]), reads=[("stat", i % 2, 0)], writes=[("stat", i % 2, 1)])
                        P.op("act", lambda e, so=so: e.activation(out=stat[:, so + 8:so + 9], in_=stat[:, so + 7:so + 8], func=AF.Sqrt, bias=self.eps_ap, scale=1.0),
                             reads=[("stat", i % 2, 1)], writes=[("stat", i % 2, 2)])
                        P.op("dve", lambda e, so=so: e.reciprocal(out=stat[:, so + 9:so + 10], in_=stat[:, so + 8:so + 9]),
                             reads=[("stat", i % 2, 2)], writes=[("stat", i % 2, 3)])
                        P.op("dve", lambda e, tf=tf, so=so: e.scalar_tensor_tensor(out=tf, in0=tf, scalar=stat[:, so + 6:so + 7], in1=sgln[:, 0, :],
                                                                                  op0=ALU.subtract, op1=ALU.mult),
                             reads=[pk, ("stat", i % 2, 1), ("sgln",)], writes=[pk])
                        P.op("dve", lambda e, tf=tf, so=so, i=i: e.scalar_tensor_tensor(out=vln[:, i, :], in0=tf, scalar=stat[:, so + 9:so + 10], in1=sgln[:, 1, :],
                                                                                       op0=ALU.mult, op1=ALU.add),
                             reads=[pk, ("stat", i % 2, 3), ("sgln",)], writes=[("vln", i)])
                    self.wrel()
                    ws_ = self.wget(li, S_U)
                    W = self.wring[ws_].rearrange("p (k n) -> p k n", k=8)
                    for i in range(ntq):
                        b = self.bank()
                        self.mm(ps[b][:, :], [(hT[:, kc, i * 128:(i + 1) * 128], W[:, kc, :]) for kc in range(8)], [("w", ws_)] + HT(i, i + 1), b)
                        P.op("act", lambda e, b=b, i=i: e.activation(out=gu[:, i, :], in_=ps[b][:, :], func=AF.Gelu_apprx_tanh),
                             reads=[("ps", b)], writes=[("gu", i)])
                    self.wrel()
                    for i in range(ntq):
                        b = self.bank()

                        def sgm(e, i=i, b=b):
                            ins = None
                            for gg in range(8):
                                ins = e.matmul(ps[b][:, gg * 64:(gg + 1) * 64], wsT[:, gg, :], vln[:, i, gg * 64:(gg + 1) * 64], start=True, stop=True)
                            return ins
                        P.op("pe", sgm, reads=[("wsT",), ("vln", i)], writes=[("ps", b)])

                        def sgv(e, i=i, b=b):
                            ins = None
                            for gg in range(8):
                                ins = e.scalar_tensor_tensor(out=gu[:, i, gg * 64:(gg + 1) * 64], in0=ps[b][:, gg * 64:(gg + 1) * 64], scalar=bsT[:, gg:gg + 1],
                                                             in1=gu[:, i, gg * 64:(gg + 1) * 64], op0=ALU.add, op1=ALU.mult)
                            return ins
                        P.op("dve", sgv, reads=[("ps", b), ("bsT",), ("gu", i)], writes=[("gu", i)])
                    for i in range(ntq):
                        b = self.bank()
                        pb16 = ps[b][:, :].bitcast(BF16)

                        def trb(e, i=i, pb16=pb16):
                            ins = None
                            for c in range(4):
                                ins = e.transpose(pb16[:, c * 128:(c + 1) * 128], gu[:, i, c * 128:(c + 1) * 128], identb)
                            return ins
                        P.op("pe", trb, reads=[("gu", i), ("identb",)], writes=[("ps", b)])
                        P.op("act", lambda e, i=i, pb16=pb16: e.activation(out=obT[:, :, i * 128:(i + 1) * 128], in_=pb16[:, 0:512].rearrange("p (c t) -> p c t", c=4), func=AF.Copy),
                             reads=[("ps", b)], writes=[("vln", jj) for jj in range(4)])
                    es = []
                    if isg0:
                        for S in (S_E0, S_E1):
                            sl_ = self.wget(li, S)
                            es.append(sl_)
                            P.op("dve", lambda e, sl_=sl_: e.tensor_scalar(out=self.wring[sl_], in0=self.wring[sl_], scalar1=8.0, scalar2=None, op0=ALU.mult),
                                 reads=[("w", sl_)], writes=[("w", sl_)])
                    jobs = [(i, j, h) for i, j in enumerate(qtiles) for h in range(NH)]
                    SB = [(0, 1), (2, 3), (4, 5)]
                    state = {}
                    nkv_l = L["nkv"]

                    def keylist(j):
                        if kind == "ctx":
                            return []
                        if j == 0:
                            return [(kt, ("e", 0, kt)) for kt in range(4)]
                        if j == 1:
                            return [(kt, ("e", 1, kt)) for kt in range(4)]
                        return [(kt, ("i", kt - j + 2)) for kt in range(j - 2, j + 3) if 0 <= kt < nkv_l]

                    def emit_S(n):
                        i, j, h = jobs[n]
                        hp, r0 = h // 2, (h % 2) * 64
                        kl = keylist(j)
                        nk = len(kl) + 2
                        ba, bb = SB[n % 3]
                        banks = [ba] * 4 + [bb] * 4
                        qsrc = qrT if kind == "lat" else qT

                        def fn(e):
                            ins = None
                            for ki, (kt, bt) in enumerate(kl):
                                o = ps[banks[ki]][:, (ki % 4) * 128:(ki % 4 + 1) * 128]
                                slot = kt % 8
                                ins = e.matmul(o, kT[r0:r0 + 64, hp, slot * 128:(slot + 1) * 128], qsrc[r0:r0 + 64, hp, i * 128:(i + 1) * 128], start=True, stop=True)
                            for c in range(2):
                                ki = len(kl) + c
                                o = ps[banks[ki]][:, (ki % 4) * 128:(ki % 4 + 1) * 128]
                                ins = e.matmul(o, ckT[r0:r0 + 64, hp, c * 128:(c + 1) * 128], qT[r0:r0 + 64, hp, i * 128:(i + 1) * 128], start=True, stop=True)
                            return ins
                        rds = [("q", hp), ("q", 4 + hp), ("ckT", hp), ("identb",), ("bint",)]
                        rds += [("kT", kt % 8, hp) for kt, _ in kl]
                        if isg0 and j < 2:
                            rds += [("w", es[j])]
                        wr = [("ps", ba)] + ([("ps", bb)] if nk > 4 else [])
                        P.op("pe", fn, reads=rds, writes=wr)
                        if kl:
                            if kl[0][1][0] == "i":
                                btab = bint[:, :, h, :]
                                brd = [("bint",)]
                            else:
                                btab = self.wring[es[kl[0][1][1]]].rearrange("p (o h q) -> p o h q", o=4, h=8)[:, :, h, :]
                                brd = [("w", es[kl[0][1][1]])]
                            nA = min(len(kl), 4)
                            P.op("dve", lambda e: e.tensor_tensor(out=ps[ba][:, 0:nA * 128].rearrange("p (o q) -> p o q", o=nA),
                                                                  in0=ps[ba][:, 0:nA * 128].rearrange("p (o q) -> p o q", o=nA),
                                                                  in1=btab[:, 0:nA, :], op=ALU.add),
                                 reads=[("ps", ba)] + brd, writes=[("ps", ba)])
                            if len(kl) > 4:
                                P.op("dve", lambda e: e.tensor_tensor(out=ps[bb][:, 0:128], in0=ps[bb][:, 0:128], in1=btab[:, 4, :], op=ALU.add),
                                     reads=[("ps", bb)] + brd, writes=[("ps", bb)])
                        pt = PT[n % 3]
                        n1 = min(nk, 4) * 128
                        P.op("act", lambda e: e.activation(out=pt[:, 0:n1], in_=ps[ba][:, 0:n1], func=AF.Exp, scale=0.125),
                             reads=[("ps", ba)], writes=[("pt", n % 3)])
                        if nk > 4:
                            n2 = (nk - 4) * 128
                            P.op("act", lambda e: e.activation(out=pt[:, 512:512 + n2], in_=ps[bb][:, 0:n2], func=AF.Exp, scale=0.125),
                                 reads=[("ps", bb)], writes=[("pt", n % 3)])
                        state[n] = (kl, nk)

                    def emit_PV(n):
                        i, j, h = jobs[n]
                        kl, nk = state[n]
                        ob_ = 6 + h // 4
                        oc = (h % 4) * 65
                        pt = PT[n % 3]

                        def fn(e):
                            ins = None
                            for ki in range(nk):
                                if ki < len(kl):
                                    rhs = Vr[:, kl[ki][0] % 8, h * 65:(h + 1) * 65]
                                else:
                                    rhs = cV[:, ki - len(kl), h * 65:(h + 1) * 65]
                                ins = e.matmul(ps[ob_][:, oc:oc + 65], pt[:, ki * 128:(ki + 1) * 128], rhs, start=(ki == 0), stop=(ki == nk - 1))
                            return ins
                        rds = [("pt", n % 3), ("cV", 0), ("cV", 1)] + [("V", kt % 8) for kt, _ in kl]
                        P.op("pe", fn, reads=rds, writes=[("ps", ob_)])
                        if h == NH - 1:
                            o0, o1 = 6, 7
                            so = 32 + (i % 2) * 8

                            def nrm(e):
                                ins = None
                                for hb, ob2 in enumerate((o0, o1)):
                                    ins = e.reciprocal(out=stat[:, so + hb * 4:so + hb * 4 + 4], in_=ps[ob2][:, 64:260:65])
                                return ins
                            P.op("dve", nrm, reads=[("ps", o0), ("ps", o1)], writes=[("rs", i % 2)])

                            def nrm2(e):
                                ins = None
                                for hb, ob2 in enumerate((o0, o1)):
                                    ins = e.tensor_tensor(out=oa[:, hb * 256:(hb + 1) * 256].rearrange("p (h d) -> p h d", d=64),
                                                          in0=ps[ob2][:, 0:260].rearrange("p (h d) -> p h d", d=65)[:, :, 0:64],
                                                          in1=stat[:, so + hb * 4:so + hb * 4 + 4].unsqueeze(2).broadcast_to([128, 4, 64]), op=ALU.mult)
                                return ins
                            P.op("dve", nrm2, reads=[("ps", o0), ("ps", o1), ("rs", i % 2)], writes=[("oa",)])
                            pb16 = ps[o0][:, :].bitcast(BF16)

                            def trb(e):
                                ins = None
                                for c in range(4):
                                    ins = e.transpose(pb16[:, c * 128:(c + 1) * 128], oa[:, c * 128:(c + 1) * 128], identb)
                                return ins
                            P.op("pe", trb, reads=[("oa",), ("identb",)], writes=[("ps", o0)])
                            P.op("dve", lambda e: e.tensor_copy(out=oaT[:, :, i * 128:(i + 1) * 128], in_=pb16[:, 0:512].rearrange("p (c t) -> p c t", c=4)),
                                 reads=[("ps", o0)], writes=[("gu", jj) for jj in range(4)])
                            if isg0 and j < 2:
                                self.wrel()
                    for n in range(len(jobs) + 2):
                        if n < len(jobs):
                            emit_S(n)
                        if 2 <= n:
                            emit_PV(n - 2)
                    GU = [("gu", jj) for jj in range(4)]
                    VLN = [("vln", jj) for jj in range(4)]
                    for s in range(2):
                        for which, S, dstb, dkey in ((0, S_GA0 + 3 * s, sga, "pt"), (1, S_GB0 + 3 * s, sgb, "sgb")):
                            wa = self.wget(li, S)
                            WA = self.wring[wa].rearrange("p (k n) -> p k n", k=8)
                            for fl in range(4):
                                b = self.bank()
                                self.mm(ps[b][:, 0:nq], [(WA[:, kc, fl * 128:(fl + 1) * 128], hT[:, kc, 0:nq]) for kc in range(8)], [("w", wa)] + HT(0, ntq), b)
                                key = ("pt", fl // 2) if which == 0 else ("sgb", fl)
                                P.op("act", lambda e, b=b, dstb=dstb, fl=fl: e.activation(out=dstb[:, fl, 0:nq], in_=ps[b][:, 0:nq], func=AF.Sigmoid),
                                     reads=[("ps", b)], writes=[key])
                            self.wrel()
                        wc = self.wget(li, S_PAB0 + 3 * s)
                        WC = self.wring[wc].rearrange("p (k n) -> p k n", k=8)
                        for fl in range(4):
                            f = 4 * s + fl
                            bC, bD = self.bank(), self.bank()
                            self.mm(ps[bC][:, 0:nq], [(WC[:, kc, fl * 128:(fl + 1) * 128], oaT[:, kc, 0:nq]) for kc in range(4)], [("w", wc)] + GU, bC)
                            self.mm(ps[bD][:, 0:nq], [(WC[:, 4 + kc, fl * 128:(fl + 1) * 128], obT[:, kc, 0:nq]) for kc in range(4)], [("w", wc)] + VLN, bD)
                            P.op("dve", lambda e, bC=bC, fl=fl: e.tensor_tensor(out=tt_[:, 0:nq], in0=ps[bC][:, 0:nq], in1=sga[:, fl, 0:nq], op=ALU.mult),
                                 reads=[("ps", bC), ("pt", fl // 2), ("rope",)], writes=[("tt",)])
                            P.op("dve", lambda e, bD=bD, fl=fl: e.tensor_tensor(out=ps[bD][:, 0:nq], in0=ps[bD][:, 0:nq], in1=sgb[:, fl, 0:nq], op=ALU.mult),
                                 reads=[("ps", bD), ("sgb", fl), ("rope",)], writes=[("ps", bD)])
                            P.op("dve", lambda e, bD=bD, f=f: e.tensor_tensor(out=yT[:, f, 0:nq], in0=tt_[:, 0:nq], in1=ps[bD][:, 0:nq], op=ALU.add),
                                 reads=[("ps", bD), ("tt",), ("rope",)], writes=[("q", f)])
                        self.wrel()
                    if between is not None:
                        between()
                    for hh in range(2):
                        wo = self.wget(li, S_WO0 + hh)
                        WO = self.wring[wo].rearrange("p (k n) -> p k n", k=8)
                        for i, j in enumerate(qtiles):
                            sl = self.xslot(kind, j)
                            b = self.bank()
                            self.mm(ps[b][:, :], [(yT[:, kc, i * 128:(i + 1) * 128], WO[:, kc, :]) for kc in range(8)],
                                    [("w", wo)] + [("q", f) for f in range(8)], b)
                            P.op("dve", lambda e, b=b, hh=hh: e.tensor_tensor(out=ps[b][:, :], in0=ps[b][:, :], in1=gabc[:, hh * 512:(hh + 1) * 512], op=ALU.mult),
                                 reads=[("ps", b), ("gabc",)], writes=[("ps", b)])
                            P.op("dve", lambda e, b=b, hh=hh, sl=sl: e.scalar_tensor_tensor(out=X[:, sl, hh * 512:(hh + 1) * 512], in0=X[:, sl, hh * 512:(hh + 1) * 512],
                                                                                            scalar=ALPHA, in1=ps[b][:, :], op0=ALU.mult, op1=ALU.add),
                                 reads=[("ps", b), ("x", sl)], writes=[("x", sl)])
                            if hh == 1:
                                self.layernorm(X[:, sl, :], ("x", sl), i % 2, affine=(ln1bc[:, 0, :], ln1bc[:, 1, :], ("ln1bc",)))
                        self.wrel()

                mgs = self.m_groups(L)
                m_ht(mgs[0])
                for gi, g_ in enumerate(mgs):
                    nx = (lambda k=gi + 1: m_ht(mgs[k])) if gi + 1 < len(mgs) else None
                    m_group(g_, nx)
                self.barrier()
                if self.dbg == "mid%d" % li:
                    for sl in range(21):
                        self.dma("sp", dr["dbg"][sl * 128:(sl + 1) * 128, :], X[:, sl, :], [("dbgout", sl)], "os", reads=[("x", sl)])
                    break

                self.dma("sp", raw(o_convp, F32, 176), dr["convp%d" % li], [("convp",)], "t13")
                self.dma("sp", raw(o_fbc + 4096, F32, 2048), dr["lnrow%d" % li][0:1, 2048:4096].partition_broadcast(128), [("fbc", 1)], "t14")
                def f_hf(g, hb_i):
                    kind = g["kind"]
                    tiles = g["t"]
                    nt = len(tiles)
                    ntok = nt * 128
                    wv = 6 if kind == "ctx" else 4
                    first = (kind == "ctx" or g["G"] == 0)
                    hf = hfT[hb_i % 2]
                    hfprev = hfT[(hb_i + 1) % 2]
                    hkey = ("hf", hb_i % 2)
                    pkey = ("hf", (hb_i + 1) % 2)
                    if first:
                        P.op("dve", lambda e, hf=hf: e.memset(hf[:, :, 0:1], 0.0), writes=[hkey])
                    else:
                        P.op("dve", lambda e, hf=hf, hfprev=hfprev: e.tensor_copy(out=hf[:, :, 0:1], in_=hfprev[:, :, 512:513]), reads=[pkey], writes=[hkey])
                    nxt_t = tiles[-1] + 1
                    have_r = (kind == "lat" and nxt_t < L["nmix"])
                    if have_r:
                        sl = self.xslot(kind, nxt_t)
                        b = self.bank()

                        def trh(e, sl=sl, b=b):
                            ins = None
                            for c in range(8):
                                ins = e.matmul(ps[b][:, c:c + 1], X[0:1, sl, c * 128:(c + 1) * 128], identf[0:1, 0:1], start=True, stop=True)
                            return ins
                        P.op("pe", trh, reads=[("x", sl), ("identf",)], writes=[("ps", b)])
                        P.op("dve", lambda e, b=b, wv=wv: e.tensor_tensor(out=ps[b][:, 0:8], in0=ps[b][:, 0:8], in1=vec[:, wv, :], op=ALU.mult),
                             reads=[("ps", b)] + VEC, writes=[("ps", b)])
                        P.op("dve", lambda e, b=b, wv=wv, hf=hf, ntok=ntok: e.tensor_tensor(out=hf[:, :, ntok + 1:ntok + 2].rearrange("p c o -> p (c o)"),
                                                                                         in0=ps[b][:, 0:8], in1=vec[:, wv + 1, :], op=ALU.add),
                             reads=[("ps", b)] + VEC, writes=[hkey])
                    else:
                        P.op("dve", lambda e, hf=hf, ntok=ntok: e.memset(hf[:, :, ntok + 1:ntok + 2], 0.0), writes=[hkey])
                    for i, j in enumerate(tiles):
                        sl = self.xslot(kind, j)
                        for cb in range(2):
                            b = self.bank()

                            def tr(e, sl=sl, cb=cb, b=b):
                                ins = None
                                for ci in range(4):
                                    c = cb * 4 + ci
                                    ins = e.transpose(ps[b][:, ci * 128:(ci + 1) * 128], X[:, sl, c * 128:(c + 1) * 128], identf)
                                return ins
                            P.op("pe", tr, reads=[("x", sl), ("identf",)], writes=[("ps", b)])
                            eng = "act" if cb == 0 else "dve"

                            def ev(e, i=i, cb=cb, b=b, eng=eng, wv=wv, hf=hf):
                                ins = None
                                for ci in range(4):
                                    c = cb * 4 + ci
                                    o = hf[:, c, 1 + i * 128:1 + (i + 1) * 128]
                                    i_ = ps[b][:, ci * 128:(ci + 1) * 128]
                                    if eng == "act":
                                        ins = e.activation(out=o, in_=i_, func=AF.Identity, bias=vec[:, wv + 1, c:c + 1], scale=vec[:, wv, c:c + 1])
                                    else:
                                        ins = e.tensor_scalar(out=o, in0=i_, scalar1=vec[:, wv, c:c + 1], scalar2=vec[:, wv + 1, c:c + 1], op0=ALU.mult, op1=ALU.add)
                                return ins
                            P.op(eng, ev, reads=[("ps", b)] + VEC, writes=[hkey])

                def f_group(g, hb_i, between):
                    kind = g["kind"]
                    tiles = g["t"]
                    nt = len(tiles)
                    ntok = nt * 128
                    first = (kind == "ctx" or g["G"] == 0)
                    if first:
                        w = 1 if kind == "ctx" else 0
                        r_ = 2 * li + w
                        self.dma("sp", fbc[:, 0, :], dr["modscr"][r_:r_ + 1, 5 * 1024:6 * 1024].partition_broadcast(128), [("fbc", 0)], "t15", reads=rd)
                    hf = hfT[hb_i % 2]
                    hkey = ("hf", hb_i % 2)
                    pending = []
                    for s in range(11):
                        wu = self.wget(li, S_UP0 + s)
                        W = self.wring[wu].rearrange("p (k n) -> p k n", k=8)
                        for jl in range(2):
                            jch = 2 * s + jl
                            res = []
                            for part in range(2):
                                cbk = part * 2 + jl
                                b = self.fbank_i % 4
                                bh = 4 + self.fbank_i % 4
                                self.fbank_i += 1
                                self.mm(ps[b][:, 0:ntok], [(W[:, kc, cbk * 128:(cbk + 1) * 128], hf[:, kc, 1:ntok + 1]) for kc in range(8)], [("w", wu), hkey], b)
                                hc = 0
                                self.mm(ps[bh][:, hc:hc + 2], [(W[:, kc, cbk * 128:(cbk + 1) * 128], hf[:, kc, 0:ntok + 2:ntok + 1]) for kc in range(8)], [("w", wu), hkey], bh)
                                res.append((b, hc, bh))
                            ta, tg = ta2[jch % 4], tg2[jch % 4]
                            for part, (b, hc, bh) in enumerate(res):
                                cc = jch + 22 * part
                                tx = ta if part == 0 else tg
                                tk = ("ta", jch % 4) if part == 0 else ("tg", jch % 4)

                                def c1(e, b=b, hc=hc, cc=cc, tx=tx, bh=bh, ntok=ntok):
                                    e.activation(out=tx[:, 0:ntok - 1], in_=ps[b][:, 1:ntok], func=AF.Identity, bias=convp[:, cc, 3:4], scale=convp[:, cc, 2:3])
                                    return e.activation(out=tx[:, ntok - 1:ntok], in_=ps[bh][:, hc + 1:hc + 2], func=AF.Identity, bias=convp[:, cc, 3:4], scale=convp[:, cc, 2:3])
                                P.op("act", c1, reads=[("ps", b), ("ps", bh), ("convp",)], writes=[tk])

                                def c2(e, b=b, hc=hc, cc=cc, tx=tx, bh=bh, ntok=ntok):
                                    e.scalar_tensor_tensor(out=tx[:, 1:ntok], in0=ps[b][:, 0:ntok - 1], scalar=convp[:, cc, 0:1], in1=tx[:, 1:ntok], op0=ALU.mult, op1=ALU.add)
                                    return e.scalar_tensor_tensor(out=tx[:, 0:1], in0=ps[bh][:, hc:hc + 1], scalar=convp[:, cc, 0:1], in1=tx[:, 0:1], op0=ALU.mult, op1=ALU.add)
                                P.op("dve", c2, reads=[("ps", b), ("ps", bh), ("convp",), tk], writes=[tk])
                                P.op("dve", lambda e, b=b, cc=cc, tx=tx, ntok=ntok: e.scalar_tensor_tensor(out=tx[:, 0:ntok], in0=ps[b][:, 0:ntok], scalar=convp[:, cc, 1:2],
                                                                                                         in1=tx[:, 0:ntok], op0=ALU.mult, op1=ALU.add),
                                     reads=[("ps", b), ("convp",), tk], writes=[tk])
                            def fin(jch=jch, ta=ta, tg=tg):
                                P.op("act", lambda e: e.activation(out=tg[:, 0:ntok], in_=tg[:, 0:ntok], func=AF.Silu),
                                     reads=[("tg", jch % 4)], writes=[("tg", jch % 4)])
                                P.op("pool", lambda e: e.tensor_tensor(out=prodT[:, jch, 0:ntok], in0=ta[:, 0:ntok], in1=tg[:, 0:ntok], op=ALU.mult),
                                     reads=[("ta", jch % 4), ("tg", jch % 4)], writes=[("prod", jch)])
                            for pf in pending:
                                pf()
                            pending[:] = [fin]
                        self.wrel()
                    for pf in pending:
                        pf()
                    pending[:] = []
                    if between is not None:
                        between()
                    for hh in range(2):
                        bks = [self.bank() for _ in range(nt)]
                        for i3 in range(3):
                            wd_ = self.wget(li, S_DN0 + hh * 3 + i3)
                            W = self.wring[wd_].rearrange("p (k n) -> p k n", k=8)
                            nk = 8 if i3 < 2 else 6
                            for i in range(nt):
                                def fn(e, i=i, i3=i3, nk=nk, W=W, bks=bks):
                                    ins = None
                                    for kk in range(nk):
                                        kc = i3 * 8 + kk
                                        ins = e.matmul(ps[bks[i]][:, :], prodT[:, kc, i * 128:(i + 1) * 128], W[:, kk, :], start=(kc == 0), stop=(kc == 21))
                                    return ins
                                P.op("pe", fn, reads=[("w", wd_)] + [("prod", i3 * 8 + kk) for kk in range(nk)], writes=[("ps", bks[i])])
                            self.wrel()
                        for i, j in enumerate(tiles):
                            sl = self.xslot(kind, j)
                            b = bks[i]
                            P.op("dve", lambda e, b=b, hh=hh: e.tensor_tensor(out=ps[b][:, :], in0=ps[b][:, :], in1=fbc[:, 0, hh * 512:(hh + 1) * 512], op=ALU.mult),
                                 reads=[("ps", b), ("fbc", 0)], writes=[("ps", b)])
                            P.op("dve", lambda e, b=b, hh=hh, sl=sl: e.scalar_tensor_tensor(out=X[:, sl, hh * 512:(hh + 1) * 512], in0=X[:, sl, hh * 512:(hh + 1) * 512],
                                                                                            scalar=ALPHA, in1=ps[b][:, :], op0=ALU.mult, op1=ALU.add),
                                 reads=[("ps", b), ("x", sl)], writes=[("x", sl)])
                            if hh == 1:
                                self.layernorm(X[:, sl, :], ("x", sl), i % 2, affine=(fbc[:, 1, :], fbc[:, 2, :], ("fbc", 1)))
                                if last_layer and kind == "lat" and j < self.out_tiles:
                                    self.dma("sp", dr["out"][j * 128:(j + 1) * 128, :], X[:, sl, :], [("out", j)], "os", reads=[("x", sl)])
                                if last_layer and kind == "ctx" and self.ctx_out:
                                    self.dma("sp", dr["ctxout"][j * 128:(j + 1) * 128, :], X[:, sl, :], [("cout", j)], "os", reads=[("x", sl)])
                fgs = self.f_groups(L)
                f_hf(fgs[0], 0)
                for hb_i, g_ in enumerate(fgs):
                    nx = (lambda k=hb_i + 1: f_hf(fgs[k], k)) if hb_i + 1 < len(fgs) else None
                    f_group(g_, hb_i, nx)
                self.barrier()

            P.wait_all("sp", "os")
            sems = {}
            for s in sorted(P.semnames):
                sems[s] = st.enter_context(nc.semaphore(s))
            blk = st.enter_context(nc.Block())

            @blk.tensor
            def _(e):
                P.replay("pe", e, sems)

            @blk.scalar
            def _(e):
                P.replay("act", e, sems)

            @blk.vector
            def _(e):
                P.replay("dve", e, sems)

            @blk.gpsimd
            def _(e):
                P.replay("pool", e, sems)

            @blk.sync
            def _(e):
                P.replay("sp", e, sems)
        return nc

    eps_ap = LN_EPS

    def layernorm(self, xap, xkey, par, affine):
        P = self.P
        stat = self.stat
        so = 64 + par * 24
        g_bc, b_bc, akey = affine

        def st1(e):
            e.bn_stats(stat[:, so:so + 6], xap[:, 0:512])
            return e.bn_stats(stat[:, so + 6:so + 12], xap[:, 512:1024])
        P.op("dve", st1, reads=[xkey], writes=[("lnst", par, 0)])
        P.op("dve", lambda e: e.bn_aggr(stat[:, so + 12:so + 14], stat[:, so:so + 12].rearrange("p (a b) -> p a b", b=6)),
             reads=[("lnst", par, 0)], writes=[("lnst", par, 1)])
        P.op("act", lambda e: e.activation(out=stat[:, so + 14:so + 15], in_=stat[:, so + 13:so + 14], func=AF.Sqrt, bias=self.eps_ap, scale=1.0),
             reads=[("lnst", par, 1)], writes=[("lnst", par, 2)])
        P.op("dve", lambda e: e.reciprocal(out=stat[:, so + 15:so + 16], in_=stat[:, so + 14:so + 15]), reads=[("lnst", par, 2)], writes=[("lnst", par, 3)])
        P.op("dve", lambda e: e.scalar_tensor_tensor(out=xap, in0=xap, scalar=stat[:, so + 12:so + 13], in1=g_bc, op0=ALU.subtract, op1=ALU.mult),
             reads=[xkey, ("lnst", par, 1), akey], writes=[xkey])
        P.op("dve", lambda e: e.scalar_tensor_tensor(out=xap, in0=xap, scalar=stat[:, so + 15:so + 16], in1=b_bc, op0=ALU.mult, op1=ALU.add),
             reads=[xkey, ("lnst", par, 3), akey], writes=[xkey])


def _local_index(half, n):
    tau = np.arange(n)
    return tau if half == 0 else 4095 - tau


_CACHE = {}


def _prep_static(inp):
    st = {}
    for l in range(DEPTH):
        for half in range(2):
            slots, bint = build_layer_slots(inp, l, half)
            sm = layer_small(inp, l, half)
            st[(l, half)] = (slots, bint, sm)
    return st


def _core_maps(inp, st, layer_ids, x_full, ctx_full, nxt):
    maps = []
    ident = np.eye(128, dtype=np.float32)
    for c in range(8):
        b, half = c // 2, c % 2
        idx = _local_index(half, nxt * 128)
        m = {}
        m["x"] = np.ascontiguousarray(x_full[b][idx])
        cl = ctx_full[b] if half == 0 else ctx_full[b][::-1]
        m["ctx"] = np.ascontiguousarray(cl)
        cf = np.concatenate([inp["c"][b].reshape(8, 128).T, inp["c_ctx"].reshape(8, 128).T], 1)
        m["cfm"] = np.ascontiguousarray(cf, dtype=np.float32)
        m["ident"] = ident
        m["rope"] = rope_table(half, nxt * 128)
        for li, l in enumerate(layer_ids):
            slots, bint, sm = st[(l, half)]
            m["ws%d" % li] = slots
            m["bint%d" % li] = bint
            m["wsT%d" % li] = sm["wsT"]
            m["bsT%d" % li] = sm["bsT"]
            m["convp%d" % li] = sm["convp"].reshape(128, 176)
            m["sgln%d" % li] = sm["sgln"].reshape(1, 1024)
            m["lnrow%d" % li] = sm["lnrow"].reshape(1, 4096)
            m["wada%d" % li] = sm["wada"]
            m["bada%d" % li] = sm["bada"]
        maps.append(m)
    return maps


def _assemble(res, key, ntiles):
    out = np.zeros((4, 4096, D), np.float32)
    for c in range(8):
        b, half = c // 2, c % 2
        o = np.asarray(res.results[c][key])
        if half == 0:
            out[b, 0:ntiles * 128] = o
        else:
            out[b, 4096 - ntiles * 128:] = o[::-1]
    return out


def _assemble_ctx(res):
    out = np.zeros((4, 256, D), np.float32)
    for b in range(4):
        out[b] = np.asarray(res.results[2 * b]["ctxout"])
    return out


FUSED = True


def kernel(**inputs):
    inp = {k: np.asarray(v, dtype=np.float32) for k, v in inputs.items()}
    st = _prep_static(inp)
    if FUSED:
        if "fused" not in _CACHE:
            bld = Builder([dict(nkv=21, nmix=19, nffn=19, ctx="full"), dict(nkv=19, nmix=17, nffn=16, ctx="kv")],
                          n_x_in=21, out_tiles=16, ctx_out=False)
            _CACHE["fused"] = bld.build()
        nc = _CACHE["fused"]
        maps = _core_maps(inp, st, [0, 1], inp["x"], inp["ctx"], 21)
        res = run_bass_kernel_spmd(nc, maps, core_ids=list(range(8)))
        return _assemble(res, "out", 16)
    if "single" not in _CACHE:
        bld = Builder([dict(nkv=19, nmix=17, nffn=16, ctx="full")], n_x_in=19, out_tiles=16, ctx_out=True)
        _CACHE["single"] = bld.build()
    nc = _CACHE["single"]
    x = inp["x"]
    ctx = inp["ctx"]
    for l in range(DEPTH):
        maps = _core_maps(inp, st, [l], x, ctx, 19)
        res = run_bass_kernel_spmd(nc, maps, core_ids=list(range(8)))
        x = _assemble(res, "out", 16)
        ctx = _assemble_ctx(res)
    return x
```

```python
import numpy as np
import concourse.bass as bass
import concourse.mybir as mybir
from concourse.bass_utils import run_bass_kernel_spmd

F32, BF16 = mybir.dt.float32, mybir.dt.bfloat16
AF = mybir.ActivationFunctionType
ALU = mybir.AluOpType

D = 1024
NH = 8
HD = 64
DFF = 2816
NCH = 22
DEPTH = 2
ALPHA = float((2 * DEPTH) ** 0.25)
LN_EPS = 1e-5
MASKV = -30000.0
NSLOT_M = 15
ENGS = ("pe", "act", "dve", "pool", "sp")

S_KA, S_KB, S_V, S_QA, S_QB, S_U, S_SV = 0, 1, 2, 3, 4, 5, 6
S_GA0, S_GB0, S_PAB0, S_GA1, S_GB1, S_PAB1, S_WO0, S_WO1 = 7, 8, 9, 10, 11, 12, 13, 14
S_E0, S_E1 = 15, 16
S_UP0 = 17
S_DN0 = 28
NSLOTS = 34


def _perm128():
    j = np.arange(128)
    d = j % 64
    pd = np.where((d % 32) < 16, d + 16, d - 16)
    return (j // 64) * 64 + pd


def _slot_cols(W, cols):
    blk = W[:, cols]
    return blk.reshape(8, 128, 512).transpose(1, 0, 2).reshape(128, 4096)


def bias_tile(half, jq, jk, rpb):
    kk = np.arange(128)
    lq_row = 2 * jq + kk // 64
    lq_col = kk % 64
    lk_row = 2 * jk + kk // 64
    lk_col = kk % 64
    if half == 0:
        r, qc, kr, kc = lq_row, lq_col, lk_row, lk_col
    else:
        r, qc, kr, kc = 63 - lq_row, 63 - lq_col, 63 - lk_row, 63 - lk_col
    rs = np.clip(r - 4, 0, 56)
    ws = np.clip(qc - 8, 0, 48)
    KR, KC = kr[:, None], kc[:, None]
    valid = ((KR >= rs[None, :]) & (KR < rs[None, :] + 8) & (KC >= ws[None, :]) & (KC < ws[None, :] + 16)
             & (KR >= 0) & (KR <= 63))
    dy = np.clip(KR - r[None, :] + 7, 0, 14)
    dx = np.clip(KC - qc[None, :] + 15, 0, 30)
    vals = rpb[:, dy, dx]
    out = np.where(valid[None], vals, np.float32(MASKV)).astype(np.float32)
    return np.ascontiguousarray(out.transpose(1, 0, 2))


def build_layer_slots(inp, l, half):
    w_in = inp["w_in"][l]
    perm = _perm128()
    slots = np.zeros((NSLOTS, 128, 4096), np.float32)

    def qk_cols(base, hp0):
        c = []
        for hp in (hp0, hp0 + 1):
            c.append(base + hp * 128 + np.arange(128))
        for hp in (hp0, hp0 + 1):
            c.append(base + hp * 128 + perm)
        return np.concatenate(c)

    slots[S_KA] = _slot_cols(w_in, qk_cols(512, 0))
    slots[S_KB] = _slot_cols(w_in, qk_cols(512, 2))
    slots[S_V] = _slot_cols(w_in, 1024 + np.arange(512))
    slots[S_QA] = _slot_cols(w_in, qk_cols(0, 0))
    slots[S_QB] = _slot_cols(w_in, qk_cols(0, 2))
    slots[S_U] = _slot_cols(w_in, 1536 + np.arange(512))
    slots[S_SV] = _slot_cols(w_in, 2048 + np.arange(512))
    for s in range(2):
        slots[S_GA0 + 3 * s] = _slot_cols(w_in, 2560 + s * 512 + np.arange(512))
        slots[S_GB0 + 3 * s] = _slot_cols(w_in, 3584 + s * 512 + np.arange(512))
        pa = inp["w_pa"][l][:, s * 512:(s + 1) * 512].reshape(4, 128, 512)
        pb = inp["w_pb"][l][:, s * 512:(s + 1) * 512].reshape(4, 128, 512)
        slots[S_PAB0 + 3 * s] = np.concatenate([pa, pb], 0).transpose(1, 0, 2).reshape(128, 4096)
        slots[S_WO0 + s] = _slot_cols(inp["w_o"][l], s * 512 + np.arange(512))
    rpb = inp["rpb"][l]
    for jq, sl in ((0, S_E0), (1, S_E1)):
        t = np.stack([bias_tile(half, jq, jk, rpb) for jk in range(4)], 1)
        slots[sl] = t.reshape(128, 4096)
    w_up = inp["w_up"][l]
    for s in range(11):
        cols = np.concatenate([(2 * s) * 128 + np.arange(128), (2 * s + 1) * 128 + np.arange(128),
                               DFF + (2 * s) * 128 + np.arange(128), DFF + (2 * s + 1) * 128 + np.arange(128)])
        slots[S_UP0 + s] = _slot_cols(w_up, cols)
    wd = np.zeros((24 * 128, 1024), np.float32)
    wd[:DFF] = inp["w_down"][l]
    for h in range(2):
        for i in range(3):
            blk = wd[i * 1024:(i + 1) * 1024, h * 512:(h + 1) * 512]
            slots[S_DN0 + h * 3 + i] = blk.reshape(8, 128, 512).transpose(1, 0, 2).reshape(128, 4096)
    bint = np.stack([bias_tile(half, 4, 4 + o, rpb) for o in range(-2, 3)], 1).reshape(128, 5 * 8 * 128)
    return slots, np.ascontiguousarray(bint)


def rope_table(half, ntok):
    tau = np.arange(ntok)
    t = tau if half == 0 else 4095 - tau
    row = (t // 64).astype(np.float32)
    col = (t % 64).astype(np.float32)
    inv = (np.float32(10000.0) ** (-np.arange(16, dtype=np.float32) / np.float32(16))).astype(np.float32)
    p = np.arange(128)
    d = p % 64
    f = d % 16
    pos = np.where((d < 32)[:, None], row[None, :], col[None, :]).astype(np.float32)
    ang = (pos * inv[f][:, None]).astype(np.float32)
    sgn = np.where((d % 32) < 16, -1.0, 1.0).astype(np.float32)[:, None]
    C = np.cos(ang).astype(np.float32)
    S = (np.sin(ang).astype(np.float32) * sgn).astype(np.float32)
    return np.ascontiguousarray(np.stack([C, S], 1))


def layer_small(inp, l, half):
    w_s = inp["w_s"][l]
    b_s = inp["b_s"][l]
    cw = inp["conv_w"][l]
    if half == 1:
        w_s = w_s[:, ::-1, ::-1]
        b_s = b_s[:, ::-1]
        cw = cw[::-1]
    wsT = np.ascontiguousarray(w_s.transpose(2, 0, 1)).reshape(128, 8 * 128)
    bsT = np.ascontiguousarray(b_s.T)
    cb = inp["conv_b"][l]
    convp = np.stack([cw[0], cw[1], cw[2], cb], -1).reshape(44, 128, 4).transpose(1, 0, 2)
    sgln = np.stack([inp["sg_ln_g"][l], inp["sg_ln_b"][l]])[None]
    lnrow = np.stack([inp["ln1_g"][l], inp["ln1_b"][l], inp["ln2_g"][l], inp["ln2_b"][l]])[None]
    lnfm = np.stack([inp["ln1_g"][l].reshape(8, 128).T, inp["ln1_b"][l].reshape(8, 128).T], 1)
    wada = np.ascontiguousarray(inp["w_ada"][l].reshape(8, 128, 6144).transpose(1, 0, 2))
    bada = inp["b_ada"][l][None]
    return dict(wsT=np.ascontiguousarray(wsT, dtype=np.float32), bsT=np.ascontiguousarray(bsT, dtype=np.float32),
                convp=np.ascontiguousarray(convp, dtype=np.float32), sgln=np.ascontiguousarray(sgln, dtype=np.float32),
                lnrow=np.ascontiguousarray(lnrow, dtype=np.float32), lnfm=np.ascontiguousarray(lnfm, dtype=np.float32),
                wada=wada, bada=np.ascontiguousarray(bada, dtype=np.float32))


class Prog:
    def __init__(self):
        self.q = {e: [] for e in ENGS}
        self.cnt = {}
        self.known = {e: {} for e in ENGS}
        self.last_w = {}
        self.readers = {}
        self.semnames = set(ENGS)

    def op(self, eng, fn, reads=(), writes=(), sem=None, inc=1):
        writes = list(writes) + [k for k in reads if k[0] == "ps" and k not in writes]
        reads = [k for k in reads if k[0] != "ps"]
        deps = {}

        def add(tok):
            if tok is None:
                return
            s, v, e = tok
            if e == "pe" and eng == "pe" and s == "pe":
                return
            if deps.get(s, 0) < v:
                deps[s] = v
        for k in reads:
            add(self.last_w.get(k))
        for k in writes:
            add(self.last_w.get(k))
            for s, (v, e) in self.readers.get(k, {}).items():
                add((s, v, e))
        kn = self.known[eng]
        for s, v in deps.items():
            if kn.get(s, 0) < v:
                self.q[eng].append(("wait", s, v))
                kn[s] = v
        s = sem or eng
        self.semnames.add(s)
        self.cnt[s] = self.cnt.get(s, 0) + inc
        tok = (s, self.cnt[s], eng)
        self.q[eng].append(("ins", fn, s, inc))
        for k in writes:
            self.last_w[k] = tok
            self.readers[k] = {}
        for k in reads:
            d = self.readers.setdefault(k, {})
            if d.get(s, (0, None))[0] < tok[1]:
                d[s] = (tok[1], eng)
        return tok

    def wait_all(self, eng, semname):
        v = self.cnt.get(semname, 0)
        if v and self.known[eng].get(semname, 0) < v:
            self.q[eng].append(("wait", semname, v))
            self.known[eng][semname] = v

    def replay(self, eng, e, sems):
        for it in self.q[eng]:
            if it[0] == "wait":
                e.wait_ge(sems[it[1]], it[2])
            else:
                ins = it[1](e)
                ins.then_inc(sems[it[2]], it[3])


class Mem:
    def __init__(self):
        self.off = 0
        self.peak = 0

    def alloc(self, nbytes):
        o = self.off
        self.off += (nbytes + 63) // 64 * 64
        self.peak = max(self.peak, self.off)
        return o


class Builder:
    def __init__(self, layers, n_x_in, out_tiles, ctx_out, dbg=None):
        self.layers = layers
        self.n_x_in = n_x_in
        self.out_tiles = out_tiles
        self.ctx_out = ctx_out
        self.dbg = dbg
        self.P = Prog()
        self.nc = bass.Bass("TRN2", target_bir_lowering=False)
        self.bank_i = 0
        self.bank_n = 8
        self.fbank_i = 0
        self.wacq = 0
        self.wrel_n = 0
        self.wemit = 0
        self.wplan = []
        self.wcached = set()
        self._plan()
        self.wlast = {}
        for u, k in enumerate(self.wplan):
            self.wlast[k] = u

    def _plan(self):
        for li, L in enumerate(self.layers):
            for g in self.m_groups(L):
                if g["kind"] == "ctx" and L["ctx"] == "kv":
                    seq = [S_KA, S_KB, S_V]
                else:
                    seq = [S_KA, S_KB, S_V, S_QA, S_QB, S_SV, S_U]
                    if g["kind"] == "lat" and g["G"] == 0:
                        seq += [S_E0, S_E1]
                    seq += list(range(7, 15))
                for s in seq:
                    self.wplan.append((li, s))
            if self.dbg == "mid%d" % li:
                break
            for g in self.f_groups(L):
                for s in range(S_UP0, S_UP0 + 17):
                    self.wplan.append((li, s))

    def m_groups(self, L):
        gs = [dict(kind="ctx", G=-1, q=[0, 1], kv=[0, 1], base=0)]
        nmix, nkv = L["nmix"], L["nkv"]
        G = 0
        while 4 * G < nmix:
            q = list(range(4 * G, min(4 * G + 4, nmix)))
            kv = list(range(0, min(6, nkv))) if G == 0 else list(range(4 * G + 2, min(4 * G + 6, nkv)))
            gs.append(dict(kind="lat", G=G, q=q, kv=kv, base=4 * G))
            G += 1
        return gs

    def f_groups(self, L):
        gs = []
        if L["ctx"] == "full":
            gs.append(dict(kind="ctx", G=-1, t=[0, 1]))
        G = 0
        while 4 * G < L["nffn"]:
            gs.append(dict(kind="lat", G=G, t=list(range(4 * G, min(4 * G + 4, L["nffn"])))))
            G += 1
        return gs

    def raw(self, off, dt, n):
        if dt == BF16:
            return self.mem[:, off // 2: off // 2 + n]
        return self.mem[:, off // 2: off // 2 + 2 * n].bitcast(F32)

    def view(self, off, dt, shape):
        ap = self.raw(off, dt, int(np.prod(shape)))
        if len(shape) == 2:
            return ap.rearrange("p (a b) -> p a b", b=shape[1])
        if len(shape) == 3:
            return ap.rearrange("p (a b c) -> p a b c", b=shape[1], c=shape[2])
        return ap

    def bank(self):
        b = self.bank_i % self.bank_n
        self.bank_i = (b + 1) % self.bank_n
        return b

    def xslot(self, kind, t):
        if kind == "ctx":
            return 19 + t
        return t if t <= 18 else t + 2

    def wget(self, li, s):
        u = self.wacq
        assert self.wplan[u] == (li, s), (u, self.wplan[u], li, s)
        assert u < self.wrel_n + 3
        self._wpump()
        assert self.wemit > u
        self.wacq += 1
        return u % 3

    def wrel(self):
        self.wrel_n += 1
        assert self.wrel_n <= self.wacq
        self._wpump()

    def _wpump(self):
        while self.wemit < min(self.wrel_n + 3, len(self.wplan)):
            u = self.wemit
            li, s = self.wplan[u]
            slot = u % 3
            dst = self.wring[slot]
            if (li, s) in self.wcached:
                src = self.dr["wsb%d" % li][s]
                self.P.op("pool", lambda e, dst=dst, src=src: e.dma_start(out=dst, in_=src),
                          reads=[("wsb", li, s)], writes=[("w", slot)], sem="w%d" % slot, inc=16)
            else:
                src = self.dr["ws%d" % li][s]
                self.P.op("pool", lambda e, dst=dst, src=src: e.dma_start(out=dst, in_=src),
                          writes=[("w", slot)], sem="w%d" % slot, inc=16)
                if self.wlast[(li, s)] > u and s not in (S_E0, S_E1):
                    sdst = self.dr["wsb%d" % li][s]
                    self.P.op("sp", lambda e, dst=dst, sdst=sdst: e.dma_start(out=sdst, in_=dst),
                              reads=[("w", slot)], writes=[("wsb", li, s)], sem="wst%d" % slot, inc=16)
                    self.wcached.add((li, s))
            self.wemit = u + 1

    def dma(self, eng, dst, src, writes, sem, reads=(), **kw):
        return self.P.op(eng, lambda e: e.dma_start(out=dst, in_=src, **kw), reads=reads, writes=writes, sem=sem, inc=16)

    def mm(self, out_ap, pairs, reads, bank):
        def fn(e):
            n = len(pairs)
            ins = None
            for i, (l, r) in enumerate(pairs):
                ins = e.matmul(out_ap, l, r, start=(i == 0), stop=(i == n - 1))
            return ins
        return self.P.op("pe", fn, reads=reads, writes=[("ps", bank)])

    def barrier(self):
        P = self.P
        for eng in ENGS:
            for s in ["pe", "act", "dve"] + [k for k in P.cnt if k.startswith("ms")]:
                if s != eng:
                    P.wait_all(eng, s)

    def build(self):
        nc = self.nc
        P = self.P
        dr = {}
        self.dr = dr
        NL = len(self.layers)
        nxt = self.n_x_in

        def din(name, shape):
            dr[name] = nc.dram_tensor(name, list(shape), F32, kind="ExternalInput").ap()
        din("x", (nxt * 128, D))
        din("ctx", (256, D))
        din("cfm", (128, 16))
        din("ident", (128, 128))
        din("rope", (128, 2, nxt * 128))
        for li in range(NL):
            din("ws%d" % li, (NSLOTS, 128, 4096))
            din("bint%d" % li, (128, 5 * 8 * 128))
            din("wsT%d" % li, (128, 1024))
            din("bsT%d" % li, (128, 8))
            din("convp%d" % li, (128, 44 * 4))
            din("sgln%d" % li, (1, 1024))
            din("lnrow%d" % li, (1, 4096))
            din("wada%d" % li, (128, 8, 6144))
            din("bada%d" % li, (1, 6144))
        dr["out"] = nc.dram_tensor("out", [self.out_tiles * 128, D], F32, kind="ExternalOutput").ap()
        if self.ctx_out:
            dr["ctxout"] = nc.dram_tensor("ctxout", [256, D], F32, kind="ExternalOutput").ap()
        if self.dbg:
            dr["dbg"] = nc.dram_tensor("dbg", [21 * 128, D], F32, kind="ExternalOutput").ap()
        dr["modscr"] = nc.dram_tensor("modscr", [NL * 2, 6144], F32, kind="Internal").ap()
        for li in range(NL):
            dr["wsb%d" % li] = nc.dram_tensor("wsb%d" % li, [NSLOTS, 128, 4096], BF16, kind="Internal").ap()

        M = Mem()
        NXS = 23 if max(L_['nkv'] for L_ in self.layers) > 19 else 21
        o_x = M.alloc(NXS * 4096)
        o_w = [M.alloc(8192) for _ in range(3)]
        o_identb = M.alloc(256)
        o_identf = M.alloc(512)
        o_wsT = M.alloc(2048)
        o_bsT = M.alloc(64)
        o_modfm = M.alloc(len(self.layers) * 2 * 48 * 4)
        o_vec = M.alloc(8 * 8 * 4)
        o_stat = M.alloc(512)
        mark = M.off
        o_kT = M.alloc(4 * 8 * 128 * 2)
        o_V = M.alloc(8 * 520 * 2)
        o_ckT = M.alloc(4 * 256 * 2)
        o_cV = M.alloc(2 * 520 * 2)
        o_bint = M.alloc(5 * 8 * 128 * 2)
        o_hT = M.alloc(8 * 768 * 2)
        o_q = M.alloc(8 * 512 * 2)
        o_gu = M.alloc(4 * 512 * 2)
        o_vln = M.alloc(4 * 512 * 2)
        o_pt = M.alloc(3 * 1024 * 2)
        o_rope = M.alloc(2 * 768 * 4)
        o_sgb = o_rope
        o_t = o_rope + 4096
        o_oa = M.alloc(512 * 2)
        o_sgln = M.alloc(2 * 512 * 4)
        o_gabc = M.alloc(4096)
        o_ln1bc = M.alloc(8192)
        m_end = M.off
        M.off = mark
        o_hfT = [M.alloc(8 * 514 * 2) for _ in range(2)]
        o_prod = M.alloc(22 * 512 * 2)
        o_ta = [M.alloc(2048) for _ in range(4)]
        o_tg = [M.alloc(2048) for _ in range(4)]
        o_convp = M.alloc(44 * 4 * 4)
        o_fbc = M.alloc(3 * 4096)
        f_end = M.off
        M.off = mark
        o_wada = [M.alloc(8192) for _ in range(2)]
        o_modrow = [M.alloc(2048) for _ in range(2)]
        o_bada = [M.alloc(2048) for _ in range(2)]
        o_scol = M.alloc(32 * 4)
        o_sbf = M.alloc(64)
        total = M.peak
        self.sbuf_bytes = total
        assert total <= 212800, total

        import contextlib
        with contextlib.ExitStack() as st:
            self.mem = st.enter_context(nc.sbuf_tensor("mem", [128, total // 2], BF16))
            ps = [st.enter_context(nc.psum_tensor("ps%d" % i, [128, 512], F32)) for i in range(8)]
            self.ps = ps
            v, raw = self.view, self.raw
            X = v(o_x, F32, (NXS, D))
            self.wring = [raw(o, BF16, 4096) for o in o_w]
            identb = raw(o_identb, BF16, 128)
            identf = raw(o_identf, F32, 128)
            wsT = v(o_wsT, BF16, (8, 128))
            bsT = raw(o_bsT, F32, 8)
            modfm_l = [v(o_modfm + l_ * 384, F32, (2, 48)) for l_ in range(len(self.layers))]
            vec = v(o_vec, F32, (8, 8))
            stat = raw(o_stat, F32, 128)
            self.stat = stat
            kT = v(o_kT, BF16, (4, 1024))
            Vr = v(o_V, BF16, (8, 520))
            ckT = v(o_ckT, BF16, (4, 256))
            cV = v(o_cV, BF16, (2, 520))
            bint = v(o_bint, BF16, (5, 8, 128))
            hT = v(o_hT, BF16, (8, 768))
            qT = v(o_q, BF16, (8, 512))[:, 0:4, :]
            qrT = v(o_q, BF16, (8, 512))[:, 4:8, :]
            yT = v(o_q, BF16, (8, 512))
            gu = v(o_gu, BF16, (4, 512))
            oaT = v(o_gu, BF16, (4, 512))
            vln = v(o_vln, BF16, (4, 512))
            obT = v(o_vln, BF16, (4, 512))
            PT = [raw(o_pt + i * 2048, BF16, 1024) for i in range(3)]
            tmpf = [raw(o_pt + i * 2048, F32, 512) for i in range(2)]
            sga = v(o_pt, BF16, (4, 512))
            sgb = v(o_sgb, BF16, (4, 512))
            tt_ = raw(o_t, F32, 512)
            rope = v(o_rope, F32, (2, 768))
            oa = raw(o_oa, BF16, 512)
            sgln = v(o_sgln, F32, (2, 512))
            gabc = raw(o_gabc, F32, 1024)
            ln1bc = v(o_ln1bc, F32, (2, 1024))
            hfT = [v(o, BF16, (8, 514)) for o in o_hfT]
            prodT = v(o_prod, BF16, (22, 512))
            ta2 = [raw(o, F32, 512) for o in o_ta]
            tg2 = [raw(o, F32, 512) for o in o_tg]
            convp = v(o_convp, F32, (44, 4))
            fbc = v(o_fbc, F32, (3, 1024))
            wadab = [v(o, BF16, (8, 512)) for o in o_wada]
            sbf = raw(o_sbf, BF16, 16)
            modrow = [raw(o, F32, 512) for o in o_modrow]
            badab = [raw(o, F32, 512) for o in o_bada]
            scol = raw(o_scol, F32, 32)

            self.dma("sp", identf, dr["ident"], [("identf",)], "t0")
            self.dma("pool", identb, dr["ident"], [("identb",)], "t1")
            self.dma("sp", scol[:, 0:16], dr["cfm"], [("scol",)], "t2")
            P.op("act", lambda e: e.activation(out=sbf, in_=scol[:, 0:16], func=AF.Silu),
                 reads=[("scol",)], writes=[("ssil",)])
            self.bank_n = 7
            self.bank_i = 0
            for li in range(NL):
                for nb in range(12):
                    pb = nb % 2
                    wb = wadab[pb]
                    self.dma("pool", wb, dr["wada%d" % li][:, :, nb * 512:(nb + 1) * 512], [("wada", pb)], "wa%d" % pb)
                    self.dma("sp", badab[pb][0:2, :], dr["bada%d" % li][0:1, nb * 512:(nb + 1) * 512].partition_broadcast(2), [("bada", pb)], "ba%d" % pb)
                    b = self.bank()
                    pairs = [(sbf[:, kc:16:8], wb[:, kc, :]) for kc in range(8)]
                    self.mm(ps[b][0:2, :], pairs, [("ssil",), ("wada", pb)], b)
                    P.op("dve", lambda e, b=b, pb=pb: e.tensor_tensor(out=modrow[pb][0:2, :], in0=ps[b][0:2, :], in1=badab[pb][0:2, :], op=ALU.add),
                         reads=[("ps", b), ("bada", pb)], writes=[("modrow", pb)])
                    self.dma("sp", dr["modscr"][2 * li:2 * li + 2, nb * 512:(nb + 1) * 512], modrow[pb][0:2, :], [("modscr", li, nb)], "ms%d" % pb,
                             reads=[("modrow", pb)])

                    def trm(e, nb=nb, pb=pb):
                        ins = None
                        for j in range(4):
                            c = nb * 4 + j
                            ins = e.matmul(ps[7][:, 2 * c:2 * c + 2], modrow[pb][0:2, j * 128:(j + 1) * 128], identf[0:2, 0:2], start=True, stop=True)
                        return ins
                    P.op("pe", trm, reads=[("modrow", pb), ("identf",)], writes=[("ps", 7)])
                P.op("dve", lambda e, li=li: e.tensor_copy(out=modfm_l[li], in_=ps[7][:, 0:96].rearrange("p (c w) -> p w c", w=2)),
                     reads=[("ps", 7)], writes=[("modfm", li)])
            for t in range(nxt):
                sl = self.xslot("lat", t)
                self.dma("sp", X[:, sl, :], dr["x"][t * 128:(t + 1) * 128, :], [("x", sl)], "x%d" % sl)
            for t in range(2):
                sl = 19 + t
                self.dma("sp", X[:, sl, :], dr["ctx"][t * 128:(t + 1) * 128, :], [("x", sl)], "x%d" % sl)
            MODSCR = lambda li: [("modscr", li, nb) for nb in range(12)]
            self.bank_n = 8
            self.barrier()

            for li, L in enumerate(self.layers):
                last_layer = (li == NL - 1)
                rd = MODSCR(li)
                P.op("dve", lambda e: e.memset(raw(o_V, BF16, 8 * 520), 1.0), writes=[("V", s) for s in range(8)])
                P.op("dve", lambda e: e.memset(raw(o_cV, BF16, 2 * 520), 1.0), writes=[("cV", 0), ("cV", 1)])
                modfm = modfm_l[li].rearrange("p w (m k) -> p w m k", m=6)
                self.dma("pool", wsT, dr["wsT%d" % li].rearrange("p (g q) -> p g q", g=8), [("wsT",)], "t6")
                self.dma("sp", bsT, dr["bsT%d" % li], [("bsT",)], "t7")
                self.dma("pool", raw(o_bint, BF16, 5120), dr["bint%d" % li], [("bint",)], "t8")
                P.op("dve", lambda e: e.tensor_scalar(out=raw(o_bint, BF16, 5120), in0=raw(o_bint, BF16, 5120), scalar1=8.0, scalar2=None, op0=ALU.mult),
                     reads=[("bint",)], writes=[("bint",)])
                self.dma("sp", raw(o_sgln, F32, 1024), dr["sgln%d" % li].partition_broadcast(128), [("sgln",)], "t9")
                self.dma("sp", raw(o_ln1bc, F32, 2048), dr["lnrow%d" % li][0:1, 0:2048].partition_broadcast(128), [("ln1bc",)], "t10")

                def dv(e, modfm=modfm):
                    ins = None
                    for w in range(2):
                        e.tensor_scalar(out=vec[:, 2 * w, :], in0=modfm[:, w, 1, :], scalar1=1.0, scalar2=None, op0=ALU.add)
                        e.tensor_copy(out=vec[:, 2 * w + 1, :], in_=modfm[:, w, 0, :])
                        e.tensor_scalar(out=vec[:, 4 + 2 * w, :], in0=modfm[:, w, 4, :], scalar1=1.0, scalar2=None, op0=ALU.add)
                        ins = e.tensor_copy(out=vec[:, 5 + 2 * w, :], in_=modfm[:, w, 3, :])
                    return ins
                P.op("dve", dv, reads=[("modfm", li)], writes=[("vec",)])
                VEC = [("vec",)]

                def m_ht(g):
                    kind = g["kind"]
                    ctxkv_only = (kind == "ctx" and L["ctx"] == "kv")
                    qtiles, kvtiles, base = g["q"], g["kv"], g["base"]
                    ntq = len(qtiles)
                    nq = ntq * 128
                    tiles_h = sorted(set(qtiles) | set(kvtiles)) if not ctxkv_only else kvtiles
                    npos = tiles_h[-1] - base + 1
                    wv = 2 if kind == "ctx" else 0
                    isg0 = (kind == "lat" and g["G"] == 0)
                    if kind == "lat":
                        self.dma("sp", rope[:, :, 0:npos * 128], dr["rope"][:, :, base * 128:(base + npos) * 128], [("rope",)], "t12")
                    for p in range(npos):
                        sl = self.xslot(kind, base + p)
                        for cb in range(2):
                            b = self.bank()

                            def tr(e, sl=sl, cb=cb, b=b):
                                ins = None
                                for ci in range(4):
                                    c = cb * 4 + ci
                                    ins = e.transpose(ps[b][:, ci * 128:(ci + 1) * 128], X[:, sl, c * 128:(c + 1) * 128], identf)
                                return ins
                            P.op("pe", tr, reads=[("x", sl), ("identf",)], writes=[("ps", b)])
                            eng = "act" if cb == 0 else "dve"

                            def ev(e, p=p, cb=cb, b=b, eng=eng, wv=wv):
                                ins = None
                                for ci in range(4):
                                    c = cb * 4 + ci
                                    o = hT[:, c, p * 128:(p + 1) * 128]
                                    i_ = ps[b][:, ci * 128:(ci + 1) * 128]
                                    if eng == "act":
                                        ins = e.activation(out=o, in_=i_, func=AF.Identity, bias=vec[:, wv + 1, c:c + 1], scale=vec[:, wv, c:c + 1])
                                    else:
                                        ins = e.tensor_scalar(out=o, in0=i_, scalar1=vec[:, wv, c:c + 1], scalar2=vec[:, wv + 1, c:c + 1],
                                                              op0=ALU.mult, op1=ALU.add)
                                return ins
                            P.op(eng, ev, reads=[("ps", b)] + VEC, writes=[("hT", p, cb)])


                def HT(p0, p1):
                    return [("hT", p, cb) for p in range(p0, p1) for cb in range(2)]

                def m_group(g, between):
                    kind = g["kind"]
                    ctxkv_only = (kind == "ctx" and L["ctx"] == "kv")
                    qtiles, kvtiles, base = g["q"], g["kv"], g["base"]
                    ntq = len(qtiles)
                    nq = ntq * 128
                    wv = 2 if kind == "ctx" else 0
                    isg0 = (kind == "lat" and g["G"] == 0)
                    if not ctxkv_only and (kind == "ctx" or isg0):
                        w = 1 if kind == "ctx" else 0
                        r_ = 2 * li + w
                        self.dma("sp", gabc, dr["modscr"][r_:r_ + 1, 2048:3072].partition_broadcast(128), [("gabc",)], "t11", reads=rd)
                    kp0 = kvtiles[0] - base
                    kp1 = kvtiles[-1] - base + 1
                    for si, S in enumerate((S_KA, S_KB)):
                        ws_ = self.wget(li, S)
                        W = self.wring[ws_].rearrange("p (k n) -> p k n", k=8)
                        for hl in range(2):
                            hp = si * 2 + hl
                            t0 = kp0 * 128
                            while t0 < kp1 * 128:
                                t1 = min(t0 + 512, kp1 * 128)
                                n = t1 - t0
                                b1 = self.bank()
                                self.mm(ps[b1][:, 0:n], [(W[:, kc, hl * 128:(hl + 1) * 128], hT[:, kc, t0:t1]) for kc in range(8)],
                                        [("w", ws_)] + HT(t0 // 128, t1 // 128), b1)
                                if kind == "ctx":
                                    P.op("act", lambda e, b1=b1, hp=hp, t0=t0, t1=t1, n=n: e.activation(out=ckT[:, hp, t0:t1], in_=ps[b1][:, 0:n], func=AF.Copy),
                                         reads=[("ps", b1)], writes=[("ckT", hp)])
                                else:
                                    b2 = self.bank()
                                    self.mm(ps[b2][:, 0:n], [(W[:, kc, 256 + hl * 128:256 + (hl + 1) * 128], hT[:, kc, t0:t1]) for kc in range(8)],
                                            [("w", ws_)] + HT(t0 // 128, t1 // 128), b2)
                                    P.op("dve", lambda e, b1=b1, t0=t0, t1=t1, n=n: e.tensor_tensor(out=tmpf[0][:, 0:n], in0=ps[b1][:, 0:n], in1=rope[:, 0, t0:t1], op=ALU.mult),
                                         reads=[("ps", b1), ("rope",)], writes=[("pt", 0)])
                                    P.op("dve", lambda e, b2=b2, t0=t0, t1=t1, n=n: e.tensor_tensor(out=ps[b2][:, 0:n], in0=ps[b2][:, 0:n], in1=rope[:, 1, t0:t1], op=ALU.mult),
                                         reads=[("ps", b2), ("rope",)], writes=[("ps", b2)])
                                    for tt in range(t0 // 128, t1 // 128):
                                        slot = (base + tt) % 8
                                        c0 = tt * 128 - t0
                                        P.op("dve", lambda e, b2=b2, c0=c0, slot=slot, hp=hp: e.tensor_tensor(
                                            out=kT[:, hp, slot * 128:(slot + 1) * 128], in0=tmpf[0][:, c0:c0 + 128], in1=ps[b2][:, c0:c0 + 128], op=ALU.add),
                                            reads=[("pt", 0), ("ps", b2)], writes=[("kT", slot, hp)])
                                t0 = t1
                        self.wrel()
                    ws_ = self.wget(li, S_V)
                    W = self.wring[ws_].rearrange("p (k n) -> p k n", k=8)
                    for p in range(kp0, kp1):
                        b = self.bank()
                        self.mm(ps[b][:, :], [(hT[:, kc, p * 128:(p + 1) * 128], W[:, kc, :]) for kc in range(8)], [("w", ws_)] + HT(p, p + 1), b)
                        if kind == "ctx":
                            dst, key = cV[:, p, :].rearrange("p (h d) -> p h d", d=65)[:, :, 0:64], ("cV", p)
                        else:
                            slot = (base + p) % 8
                            dst, key = Vr[:, slot, :].rearrange("p (h d) -> p h d", d=65)[:, :, 0:64], ("V", slot)
                        P.op("act", lambda e, b=b, dst=dst: e.activation(out=dst, in_=ps[b][:, :].rearrange("p (h d) -> p h d", d=64), func=AF.Copy),
                             reads=[("ps", b)], writes=[key])
                    self.wrel()
                    if ctxkv_only:
                        if between is not None:
                            between()
                        return
                    for si, S in enumerate((S_QA, S_QB)):
                        ws_ = self.wget(li, S)
                        W = self.wring[ws_].rearrange("p (k n) -> p k n", k=8)
                        for hl in range(2):
                            hp = si * 2 + hl
                            b1 = self.bank()
                            self.mm(ps[b1][:, 0:nq], [(W[:, kc, hl * 128:(hl + 1) * 128], hT[:, kc, 0:nq]) for kc in range(8)], [("w", ws_)] + HT(0, ntq), b1)
                            P.op("act", lambda e, b1=b1, hp=hp: e.activation(out=qT[:, hp, 0:nq], in_=ps[b1][:, 0:nq], func=AF.Copy),
                                 reads=[("ps", b1)], writes=[("q", hp)])
                            if kind == "lat":
                                b2 = self.bank()
                                self.mm(ps[b2][:, 0:nq], [(W[:, kc, 256 + hl * 128:256 + (hl + 1) * 128], hT[:, kc, 0:nq]) for kc in range(8)],
                                        [("w", ws_)] + HT(0, ntq), b2)
                                P.op("dve", lambda e, b1=b1: e.tensor_tensor(out=tmpf[0][:, 0:nq], in0=ps[b1][:, 0:nq], in1=rope[:, 0, 0:nq], op=ALU.mult),
                                     reads=[("ps", b1), ("rope",)], writes=[("pt", 0)])
                                P.op("dve", lambda e, b2=b2: e.tensor_tensor(out=ps[b2][:, 0:nq], in0=ps[b2][:, 0:nq], in1=rope[:, 1, 0:nq], op=ALU.mult),
                                     reads=[("ps", b2), ("rope",)], writes=[("ps", b2)])
                                P.op("dve", lambda e, b2=b2, hp=hp: e.tensor_tensor(out=qrT[:, hp, 0:nq], in0=tmpf[0][:, 0:nq], in1=ps[b2][:, 0:nq], op=ALU.add),
                                     reads=[("pt", 0), ("ps", b2)], writes=[("q", 4 + hp)])
                        self.wrel()
                    ws_ = self.wget(li, S_SV)
                    W = self.wring[ws_].rearrange("p (k n) -> p k n", k=8)
                    for i in range(ntq):
                        b = self.bank()
                        self.mm(ps[b][:, :], [(hT[:, kc, i * 128:(i + 1) * 128], W[:, kc, :]) for kc in range(8)], [("w", ws_)] + HT(i, i + 1), b)
                        tf = tmpf[i % 2]
                        pk = ("pt", i % 2)
                        P.op("act", lambda e, b=b, tf=tf: e.activation(out=tf, in_=ps[b][:, :], func=AF.Gelu_apprx_tanh),
                             reads=[("ps", b)], writes=[pk])
                        so = (i % 2) * 16
                        P.op("dve", lambda e, tf=tf, so=so: e.bn_stats(stat[:, so:so + 6], tf), reads=[pk], writes=[("stat", i % 2, 0)])
                        P.op("dve", lambda e, so=so: e.bn_aggr(stat[:, so + 6:so + 8], stat[:, so:so + 6]), reads=[("stat", i % 2, 0)], writes=[("stat", i % 2, 1)])
                        P.op("act", lambda e, so=so: e.activation(out=stat[:, so + 8:so + 9], in_=stat[:, so + 7:so + 8], func=AF.Sqrt, bias=self.eps_ap, scale=1.0),
                             reads=[("stat", i % 2, 1)], writes=[("stat", i % 2, 2)])
                        P.op("dve", lambda e, so=so: e.reciprocal(out=stat[:, so + 9:so + 10], in_=stat[:, so + 8:so + 9]),
                             reads=[("stat", i % 2, 2)], writes=[("stat", i % 2, 3)])
                        P.op("dve", lambda e, tf=tf, so=so: e.scalar_tensor_tensor(out=tf, in0=tf, scalar=stat[:, so + 6:so + 7], in1=sgln[:, 0, :],
                                                                                  op0=ALU.subtract, op1=ALU.mult),
                             reads=[pk, ("stat", i % 2, 1), ("sgln",)], writes=[pk])
                        P.op("dve", lambda e, tf=tf, so=so, i=i: e.scalar_tensor_tensor(out=vln[:, i, :], in0=tf, scalar=stat[:, so + 9:so + 10], in1=sgln[:, 1, :],
                                                                                       op0=ALU.mult, op1=ALU.add),
                             reads=[pk, ("stat", i % 2, 3), ("sgln",)], writes=[("vln", i)])
                    self.wrel()
                    ws_ = self.wget(li, S_U)
                    W = self.wring[ws_].rearrange("p (k n) -> p k n", k=8)
                    for i in range(ntq):
                        b = self.bank()
                        self.mm(ps[b][:, :], [(hT[:, kc, i * 128:(i + 1) * 128], W[:, kc, :]) for kc in range(8)], [("w", ws_)] + HT(i, i + 1), b)
                        P.op("act", lambda e, b=b, i=i: e.activation(out=gu[:, i, :], in_=ps[b][:, :], func=AF.Gelu_apprx_tanh),
                             reads=[("ps", b)], writes=[("gu", i)])
                    self.wrel()
                    for i in range(ntq):
                        b = self.bank()

                        def sgm(e, i=i, b=b):
                            ins = None
                            for gg in range(8):
                                ins = e.matmul(ps[b][:, gg * 64:(gg + 1) * 64], wsT[:, gg, :], vln[:, i, gg * 64:(gg + 1) * 64], start=True, stop=True)
                            return ins
                        P.op("pe", sgm, reads=[("wsT",), ("vln", i)], writes=[("ps", b)])

                        def sgv(e, i=i, b=b):
                            ins = None
                            for gg in range(8):
                                ins = e.scalar_tensor_tensor(out=gu[:, i, gg * 64:(gg + 1) * 64], in0=ps[b][:, gg * 64:(gg + 1) * 64], scalar=bsT[:, gg:gg + 1],
                                                             in1=gu[:, i, gg * 64:(gg + 1) * 64], op0=ALU.add, op1=ALU.mult)
                            return ins
                        P.op("dve", sgv, reads=[("ps", b), ("bsT",), ("gu", i)], writes=[("gu", i)])
                    for i in range(ntq):
                        b = self.bank()
                        pb16 = ps[b][:, :].bitcast(BF16)

                        def trb(e, i=i, pb16=pb16):
                            ins = None
                            for c in range(4):
                                ins = e.transpose(pb16[:, c * 128:(c + 1) * 128], gu[:, i, c * 128:(c + 1) * 128], identb)
                            return ins
                        P.op("pe", trb, reads=[("gu", i), ("identb",)], writes=[("ps", b)])
                        P.op("act", lambda e, i=i, pb16=pb16: e.activation(out=obT[:, :, i * 128:(i + 1) * 128], in_=pb16[:, 0:512].rearrange("p (c t) -> p c t", c=4), func=AF.Copy),
                             reads=[("ps", b)], writes=[("vln", jj) for jj in range(4)])
                    es = []
                    if isg0:
                        for S in (S_E0, S_E1):
                            sl_ = self.wget(li, S)
                            es.append(sl_)
                            P.op("dve", lambda e, sl_=sl_: e.tensor_scalar(out=self.wring[sl_], in0=self.wring[sl_], scalar1=8.0, scalar2=None, op0=ALU.mult),
                                 reads=[("w", sl_)], writes=[("w", sl_)])
                    jobs = [(i, j, h) for i, j in enumerate(qtiles) for h in range(NH)]
                    SB = [(0, 1), (2, 3), (4, 5)]
                    state = {}
                    nkv_l = L["nkv"]

                    def keylist(j):
                        if kind == "ctx":
                            return []
                        if j == 0:
                            return [(kt, ("e", 0, kt)) for kt in range(4)]
                        if j == 1:
                            return [(kt, ("e", 1, kt)) for kt in range(4)]
                        return [(kt, ("i", kt - j + 2)) for kt in range(j - 2, j + 3) if 0 <= kt < nkv_l]

                    def emit_S(n):
                        i, j, h = jobs[n]
                        hp, r0 = h // 2, (h % 2) * 64
                        kl = keylist(j)
                        nk = len(kl) + 2
                        ba, bb = SB[n % 3]
                        banks = [ba] * 4 + [bb] * 4
                        qsrc = qrT if kind == "lat" else qT

                        def fn(e):
                            ins = None
                            for ki, (kt, bt) in enumerate(kl):
                                o = ps[banks[ki]][:, (ki % 4) * 128:(ki % 4 + 1) * 128]
                                slot = kt % 8
                                ins = e.matmul(o, kT[r0:r0 + 64, hp, slot * 128:(slot + 1) * 128], qsrc[r0:r0 + 64, hp, i * 128:(i + 1) * 128], start=True, stop=True)
                            for c in range(2):
                                ki = len(kl) + c
                                o = ps[banks[ki]][:, (ki % 4) * 128:(ki % 4 + 1) * 128]
                                ins = e.matmul(o, ckT[r0:r0 + 64, hp, c * 128:(c + 1) * 128], qT[r0:r0 + 64, hp, i * 128:(i + 1) * 128], start=True, stop=True)
                            return ins
                        rds = [("q", hp), ("q", 4 + hp), ("ckT", hp), ("identb",), ("bint",)]
                        rds += [("kT", kt % 8, hp) for kt, _ in kl]
                        if isg0 and j < 2:
                            rds += [("w", es[j])]
                        wr = [("ps", ba)] + ([("ps", bb)] if nk > 4 else [])
                        P.op("pe", fn, reads=rds, writes=wr)
                        if kl:
                            if kl[0][1][0] == "i":
                                btab = bint[:, :, h, :]
                                brd = [("bint",)]
                            else:
                                btab = self.wring[es[kl[0][1][1]]].rearrange("p (o h q) -> p o h q", o=4, h=8)[:, :, h, :]
                                brd = [("w", es[kl[0][1][1]])]
                            nA = min(len(kl), 4)
                            P.op("dve", lambda e: e.tensor_tensor(out=ps[ba][:, 0:nA * 128].rearrange("p (o q) -> p o q", o=nA),
                                                                  in0=ps[ba][:, 0:nA * 128].rearrange("p (o q) -> p o q", o=nA),
                                                                  in1=btab[:, 0:nA, :], op=ALU.add),
                                 reads=[("ps", ba)] + brd, writes=[("ps", ba)])
                            if len(kl) > 4:
                                P.op("dve", lambda e: e.tensor_tensor(out=ps[bb][:, 0:128], in0=ps[bb][:, 0:128], in1=btab[:, 4, :], op=ALU.add),
                                     reads=[("ps", bb)] + brd, writes=[("ps", bb)])
                        pt = PT[n % 3]
                        n1 = min(nk, 4) * 128
                        P.op("act", lambda e: e.activation(out=pt[:, 0:n1], in_=ps[ba][:, 0:n1], func=AF.Exp, scale=0.125),
                             reads=[("ps", ba)], writes=[("pt", n % 3)])
                        if nk > 4:
                            n2 = (nk - 4) * 128
                            P.op("act", lambda e: e.activation(out=pt[:, 512:512 + n2], in_=ps[bb][:, 0:n2], func=AF.Exp, scale=0.125),
                                 reads=[("ps", bb)], writes=[("pt", n % 3)])
                        state[n] = (kl, nk)

                    def emit_PV(n):
                        i, j, h = jobs[n]
                        kl, nk = state[n]
                        ob_ = 6 + h // 4
                        oc = (h % 4) * 65
                        pt = PT[n % 3]

                        def fn(e):
                            ins = None
                            for ki in range(nk):
                                if ki < len(kl):
                                    rhs = Vr[:, kl[ki][0] % 8, h * 65:(h + 1) * 65]
                                else:
                                    rhs = cV[:, ki - len(kl), h * 65:(h + 1) * 65]
                                ins = e.matmul(ps[ob_][:, oc:oc + 65], pt[:, ki * 128:(ki + 1) * 128], rhs, start=(ki == 0), stop=(ki == nk - 1))
                            return ins
                        rds = [("pt", n % 3), ("cV", 0), ("cV", 1)] + [("V", kt % 8) for kt, _ in kl]
                        P.op("pe", fn, reads=rds, writes=[("ps", ob_)])
                        if h == NH - 1:
                            o0, o1 = 6, 7
                            so = 32 + (i % 2) * 8

                            def nrm(e):
                                ins = None
                                for hb, ob2 in enumerate((o0, o1)):
                                    ins = e.reciprocal(out=stat[:, so + hb * 4:so + hb * 4 + 4], in_=ps[ob2][:, 64:260:65])
                                return ins
                            P.op("dve", nrm, reads=[("ps", o0), ("ps", o1)], writes=[("rs", i % 2)])

                            def nrm2(e):
                                ins = None
                                for hb, ob2 in enumerate((o0, o1)):
                                    ins = e.tensor_tensor(out=oa[:, hb * 256:(hb + 1) * 256].rearrange("p (h d) -> p h d", d=64),
                                                          in0=ps[ob2][:, 0:260].rearrange("p (h d) -> p h d", d=65)[:, :, 0:64],
                                                          in1=stat[:, so + hb * 4:so + hb * 4 + 4].unsqueeze(2).broadcast_to([128, 4, 64]), op=ALU.mult)
                                return ins
                            P.op("dve", nrm2, reads=[("ps", o0), ("ps", o1), ("rs", i % 2)], writes=[("oa",)])
                            pb16 = ps[o0][:, :].bitcast(BF16)

                            def trb(e):
                                ins = None
                                for c in range(4):
                                    ins = e.transpose(pb16[:, c * 128:(c + 1) * 128], oa[:, c * 128:(c + 1) * 128], identb)
                                return ins
                            P.op("pe", trb, reads=[("oa",), ("identb",)], writes=[("ps", o0)])
                            P.op("dve", lambda e: e.tensor_copy(out=oaT[:, :, i * 128:(i + 1) * 128], in_=pb16[:, 0:512].rearrange("p (c t) -> p c t", c=4)),
                                 reads=[("ps", o0)], writes=[("gu", jj) for jj in range(4)])
                            if isg0 and j < 2:
                                self.wrel()
                    for n in range(len(jobs) + 2):
                        if n < len(jobs):
                            emit_S(n)
                        if 2 <= n:
                            emit_PV(n - 2)
                    GU = [("gu", jj) for jj in range(4)]
                    VLN = [("vln", jj) for jj in range(4)]
                    for s in range(2):
                        for which, S, dstb, dkey in ((0, S_GA0 + 3 * s, sga, "pt"), (1, S_GB0 + 3 * s, sgb, "sgb")):
                            wa = self.wget(li, S)
                            WA = self.wring[wa].rearrange("p (k n) -> p k n", k=8)
                            for fl in range(4):
                                b = self.bank()
                                self.mm(ps[b][:, 0:nq], [(WA[:, kc, fl * 128:(fl + 1) * 128], hT[:, kc, 0:nq]) for kc in range(8)], [("w", wa)] + HT(0, ntq), b)
                                key = ("pt", fl // 2) if which == 0 else ("sgb", fl)
                                P.op("act", lambda e, b=b, dstb=dstb, fl=fl: e.activation(out=dstb[:, fl, 0:nq], in_=ps[b][:, 0:nq], func=AF.Sigmoid),
                                     reads=[("ps", b)], writes=[key])
                            self.wrel()
                        wc = self.wget(li, S_PAB0 + 3 * s)
                        WC = self.wring[wc].rearrange("p (k n) -> p k n", k=8)
                        for fl in range(4):
                            f = 4 * s + fl
                            bC, bD = self.bank(), self.bank()
                            self.mm(ps[bC][:, 0:nq], [(WC[:, kc, fl * 128:(fl + 1) * 128], oaT[:, kc, 0:nq]) for kc in range(4)], [("w", wc)] + GU, bC)
                            self.mm(ps[bD][:, 0:nq], [(WC[:, 4 + kc, fl * 128:(fl + 1) * 128], obT[:, kc, 0:nq]) for kc in range(4)], [("w", wc)] + VLN, bD)
                            P.op("dve", lambda e, bC=bC, fl=fl: e.tensor_tensor(out=tt_[:, 0:nq], in0=ps[bC][:, 0:nq], in1=sga[:, fl, 0:nq], op=ALU.mult),
                                 reads=[("ps", bC), ("pt", fl // 2), ("rope",)], writes=[("tt",)])
                            P.op("dve", lambda e, bD=bD, fl=fl: e.tensor_tensor(out=ps[bD][:, 0:nq], in0=ps[bD][:, 0:nq], in1=sgb[:, fl, 0:nq], op=ALU.mult),
                                 reads=[("ps", bD), ("sgb", fl), ("rope",)], writes=[("ps", bD)])
                            P.op("dve", lambda e, bD=bD, f=f: e.tensor_tensor(out=yT[:, f, 0:nq], in0=tt_[:, 0:nq], in1=ps[bD][:, 0:nq], op=ALU.add),
                                 reads=[("ps", bD), ("tt",), ("rope",)], writes=[("q", f)])
                        self.wrel()
                    if between is not None:
                        between()
                    for hh in range(2):
                        wo = self.wget(li, S_WO0 + hh)
                        WO = self.wring[wo].rearrange("p (k n) -> p k n", k=8)
                        for i, j in enumerate(qtiles):
                            sl = self.xslot(kind, j)
                            b = self.bank()
                            self.mm(ps[b][:, :], [(yT[:, kc, i * 128:(i + 1) * 128], WO[:, kc, :]) for kc in range(8)],
                                    [("w", wo)] + [("q", f) for f in range(8)], b)
                            P.op("dve", lambda e, b=b, hh=hh: e.tensor_tensor(out=ps[b][:, :], in0=ps[b][:, :], in1=gabc[:, hh * 512:(hh + 1) * 512], op=ALU.mult),
                                 reads=[("ps", b), ("gabc",)], writes=[("ps", b)])
                            P.op("dve", lambda e, b=b, hh=hh, sl=sl: e.scalar_tensor_tensor(out=X[:, sl, hh * 512:(hh + 1) * 512], in0=X[:, sl, hh * 512:(hh + 1) * 512],
                                                                                            scalar=ALPHA, in1=ps[b][:, :], op0=ALU.mult, op1=ALU.add),
                                 reads=[("ps", b), ("x", sl)], writes=[("x", sl)])
                            if hh == 1:
                                self.layernorm(X[:, sl, :], ("x", sl), i % 2, affine=(ln1bc[:, 0, :], ln1bc[:, 1, :], ("ln1bc",)))
                        self.wrel()

                mgs = self.m_groups(L)
                m_ht(mgs[0])
                for gi, g_ in enumerate(mgs):
                    nx = (lambda k=gi + 1: m_ht(mgs[k])) if gi + 1 < len(mgs) else None
                    m_group(g_, nx)
                self.barrier()
                if self.dbg == "mid%d" % li:
                    for sl in range(21):
                        self.dma("sp", dr["dbg"][sl * 128:(sl + 1) * 128, :], X[:, sl, :], [("dbgout", sl)], "os", reads=[("x", sl)])
                    break

                self.dma("sp", raw(o_convp, F32, 176), dr["convp%d" % li], [("convp",)], "t13")
                self.dma("sp", raw(o_fbc + 4096, F32, 2048), dr["lnrow%d" % li][0:1, 2048:4096].partition_broadcast(128), [("fbc", 1)], "t14")
                def f_hf(g, hb_i):
                    kind = g["kind"]
                    tiles = g["t"]
                    nt = len(tiles)
                    ntok = nt * 128
                    wv = 6 if kind == "ctx" else 4
                    first = (kind == "ctx" or g["G"] == 0)
                    hf = hfT[hb_i % 2]
                    hfprev = hfT[(hb_i + 1) % 2]
                    hkey = ("hf", hb_i % 2)
                    pkey = ("hf", (hb_i + 1) % 2)
                    if first:
                        P.op("dve", lambda e, hf=hf: e.memset(hf[:, :, 0:1], 0.0), writes=[hkey])
                    else:
                        P.op("dve", lambda e, hf=hf, hfprev=hfprev: e.tensor_copy(out=hf[:, :, 0:1], in_=hfprev[:, :, 512:513]), reads=[pkey], writes=[hkey])
                    nxt_t = tiles[-1] + 1
                    have_r = (kind == "lat" and nxt_t < L["nmix"])
                    if have_r:
                        sl = self.xslot(kind, nxt_t)
                        b = self.bank()

                        def trh(e, sl=sl, b=b):
                            ins = None
                            for c in range(8):
                                ins = e.matmul(ps[b][:, c:c + 1], X[0:1, sl, c * 128:(c + 1) * 128], identf[0:1, 0:1], start=True, stop=True)
                            return ins
                        P.op("pe", trh, reads=[("x", sl), ("identf",)], writes=[("ps", b)])
                        P.op("dve", lambda e, b=b, wv=wv: e.tensor_tensor(out=ps[b][:, 0:8], in0=ps[b][:, 0:8], in1=vec[:, wv, :], op=ALU.mult),
                             reads=[("ps", b)] + VEC, writes=[("ps", b)])
                        P.op("dve", lambda e, b=b, wv=wv, hf=hf, ntok=ntok: e.tensor_tensor(out=hf[:, :, ntok + 1:ntok + 2].rearrange("p c o -> p (c o)"),
                                                                                         in0=ps[b][:, 0:8], in1=vec[:, wv + 1, :], op=ALU.add),
                             reads=[("ps", b)] + VEC, writes=[hkey])
                    else:
                        P.op("dve", lambda e, hf=hf, ntok=ntok: e.memset(hf[:, :, ntok + 1:ntok + 2], 0.0), writes=[hkey])
                    for i, j in enumerate(tiles):
                        sl = self.xslot(kind, j)
                        for cb in range(2):
                            b = self.bank()

                            def tr(e, sl=sl, cb=cb, b=b):
                                ins = None
                                for ci in range(4):
                                    c = cb * 4 + ci
                                    ins = e.transpose(ps[b][:, ci * 128:(ci + 1) * 128], X[:, sl, c * 128:(c + 1) * 128], identf)
                                return ins
                            P.op("pe", tr, reads=[("x", sl), ("identf",)], writes=[("ps", b)])
                            eng = "act" if cb == 0 else "dve"

                            def ev(e, i=i, cb=cb, b=b, eng=eng, wv=wv, hf=hf):
                                ins = None
                                for ci in range(4):
                                    c = cb * 4 + ci
                                    o = hf[:, c, 1 + i * 128:1 + (i + 1) * 128]
                                    i_ = ps[b][:, ci * 128:(ci + 1) * 128]
                                    if eng == "act":
                                        ins = e.activation(out=o, in_=i_, func=AF.Identity, bias=vec[:, wv + 1, c:c + 1], scale=vec[:, wv, c:c + 1])
                                    else:
                                        ins = e.tensor_scalar(out=o, in0=i_, scalar1=vec[:, wv, c:c + 1], scalar2=vec[:, wv + 1, c:c + 1], op0=ALU.mult, op1=ALU.add)
                                return ins
                            P.op(eng, ev, reads=[("ps", b)] + VEC, writes=[hkey])

                def f_group(g, hb_i, between):
                    kind = g["kind"]
                    tiles = g["t"]
                    nt = len(tiles)
                    ntok = nt * 128
                    first = (kind == "ctx" or g["G"] == 0)
                    if first:
                        w = 1 if kind == "ctx" else 0
                        r_ = 2 * li + w
                        self.dma("sp", fbc[:, 0, :], dr["modscr"][r_:r_ + 1, 5 * 1024:6 * 1024].partition_broadcast(128), [("fbc", 0)], "t15", reads=rd)
                    hf = hfT[hb_i % 2]
                    hkey = ("hf", hb_i % 2)
                    pending = []
                    for s in range(11):
                        wu = self.wget(li, S_UP0 + s)
                        W = self.wring[wu].rearrange("p (k n) -> p k n", k=8)
                        for jl in range(2):
                            jch = 2 * s + jl
                            res = []
                            for part in range(2):
                                cbk = part * 2 + jl
                                b = self.fbank_i % 4
                                bh = 4 + self.fbank_i % 4
                                self.fbank_i += 1
                                self.mm(ps[b][:, 0:ntok], [(W[:, kc, cbk * 128:(cbk + 1) * 128], hf[:, kc, 1:ntok + 1]) for kc in range(8)], [("w", wu), hkey], b)
                                hc = 0
                                self.mm(ps[bh][:, hc:hc + 2], [(W[:, kc, cbk * 128:(cbk + 1) * 128], hf[:, kc, 0:ntok + 2:ntok + 1]) for kc in range(8)], [("w", wu), hkey], bh)
                                res.append((b, hc, bh))
                            ta, tg = ta2[jch % 4], tg2[jch % 4]
                            for part, (b, hc, bh) in enumerate(res):
                                cc = jch + 22 * part
                                tx = ta if part == 0 else tg
                                tk = ("ta", jch % 4) if part == 0 else ("tg", jch % 4)

                                def c1(e, b=b, hc=hc, cc=cc, tx=tx, bh=bh, ntok=ntok):
                                    e.activation(out=tx[:, 0:ntok - 1], in_=ps[b][:, 1:ntok], func=AF.Identity, bias=convp[:, cc, 3:4], scale=convp[:, cc, 2:3])
                                    return e.activation(out=tx[:, ntok - 1:ntok], in_=ps[bh][:, hc + 1:hc + 2], func=AF.Identity, bias=convp[:, cc, 3:4], scale=convp[:, cc, 2:3])
                                P.op("act", c1, reads=[("ps", b), ("ps", bh), ("convp",)], writes=[tk])

                                def c2(e, b=b, hc=hc, cc=cc, tx=tx, bh=bh, ntok=ntok):
                                    e.scalar_tensor_tensor(out=tx[:, 1:ntok], in0=ps[b][:, 0:ntok - 1], scalar=convp[:, cc, 0:1], in1=tx[:, 1:ntok], op0=ALU.mult, op1=ALU.add)
                                    return e.scalar_tensor_tensor(out=tx[:, 0:1], in0=ps[bh][:, hc:hc + 1], scalar=convp[:, cc, 0:1], in1=tx[:, 0:1], op0=ALU.mult, op1=ALU.add)
                                P.op("dve", c2, reads=[("ps", b), ("ps", bh), ("convp",), tk], writes=[tk])
                                P.op("dve", lambda e, b=b, cc=cc, tx=tx, ntok=ntok: e.scalar_tensor_tensor(out=tx[:, 0:ntok], in0=ps[b][:, 0:ntok], scalar=convp[:, cc, 1:2],
                                                                                                         in1=tx[:, 0:ntok], op0=ALU.mult, op1=ALU.add),
                                     reads=[("ps", b), ("convp",), tk], writes=[tk])
                            def fin(jch=jch, ta=ta, tg=tg):
                                P.op("act", lambda e: e.activation(out=tg[:, 0:ntok], in_=tg[:, 0:ntok], func=AF.Silu),
                                     reads=[("tg", jch % 4)], writes=[("tg", jch % 4)])
                                P.op("pool", lambda e: e.tensor_tensor(out=prodT[:, jch, 0:ntok], in0=ta[:, 0:ntok], in1=tg[:, 0:ntok], op=ALU.mult),
                                     reads=[("ta", jch % 4), ("tg", jch % 4)], writes=[("prod", jch)])
                            for pf in pending:
                                pf()
                            pending[:] = [fin]
                        self.wrel()
                    for pf in pending:
                        pf()
                    pending[:] = []
                    if between is not None:
                        between()
                    for hh in range(2):
                        bks = [self.bank() for _ in range(nt)]
                        for i3 in range(3):
                            wd_ = self.wget(li, S_DN0 + hh * 3 + i3)
                            W = self.wring[wd_].rearrange("p (k n) -> p k n", k=8)
                            nk = 8 if i3 < 2 else 6
                            for i in range(nt):
                                def fn(e, i=i, i3=i3, nk=nk, W=W, bks=bks):
                                    ins = None
                                    for kk in range(nk):
                                        kc = i3 * 8 + kk
                                        ins = e.matmul(ps[bks[i]][:, :], prodT[:, kc, i * 128:(i + 1) * 128], W[:, kk, :], start=(kc == 0), stop=(kc == 21))
                                    return ins
                                P.op("pe", fn, reads=[("w", wd_)] + [("prod", i3 * 8 + kk) for kk in range(nk)], writes=[("ps", bks[i])])
                            self.wrel()
                        for i, j in enumerate(tiles):
                            sl = self.xslot(kind, j)
                            b = bks[i]
                            P.op("dve", lambda e, b=b, hh=hh: e.tensor_tensor(out=ps[b][:, :], in0=ps[b][:, :], in1=fbc[:, 0, hh * 512:(hh + 1) * 512], op=ALU.mult),
                                 reads=[("ps", b), ("fbc", 0)], writes=[("ps", b)])
                            P.op("dve", lambda e, b=b, hh=hh, sl=sl: e.scalar_tensor_tensor(out=X[:, sl, hh * 512:(hh + 1) * 512], in0=X[:, sl, hh * 512:(hh + 1) * 512],
                                                                                            scalar=ALPHA, in1=ps[b][:, :], op0=ALU.mult, op1=ALU.add),
                                 reads=[("ps", b), ("x", sl)], writes=[("x", sl)])
                            if hh == 1:
                                self.layernorm(X[:, sl, :], ("x", sl), i % 2, affine=(fbc[:, 1, :], fbc[:, 2, :], ("fbc", 1)))
                                if last_layer and kind == "lat" and j < self.out_tiles:
                                    self.dma("sp", dr["out"][j * 128:(j + 1) * 128, :], X[:, sl, :], [("out", j)], "os", reads=[("x", sl)])
                                if last_layer and kind == "ctx" and self.ctx_out:
                                    self.dma("sp", dr["ctxout"][j * 128:(j + 1) * 128, :], X[:, sl, :], [("cout", j)], "os", reads=[("x", sl)])
                fgs = self.f_groups(L)
                f_hf(fgs[0], 0)
                for hb_i, g_ in enumerate(fgs):
                    nx = (lambda k=hb_i + 1: f_hf(fgs[k], k)) if hb_i + 1 < len(fgs) else None
                    f_group(g_, hb_i, nx)
                self.barrier()

            P.wait_all("sp", "os")
            sems = {}
            for s in sorted(P.semnames):
                sems[s] = st.enter_context(nc.semaphore(s))
            blk = st.enter_context(nc.Block())

            @blk.tensor
            def _(e):
                P.replay("pe", e, sems)

            @blk.scalar
            def _(e):
                P.replay("act", e, sems)

            @blk.vector
            def _(e):
                P.replay("dve", e, sems)

            @blk.gpsimd
            def _(e):
                P.replay("pool", e, sems)

            @blk.sync
            def _(e):
                P.replay("sp", e, sems)
        return nc

    eps_ap = LN_EPS

    def layernorm(self, xap, xkey, par, affine):
        P = self.P
        stat = self.stat
        so = 64 + par * 24
        g_bc, b_bc, akey = affine

        def st1(e):
            e.bn_stats(stat[:, so:so + 6], xap[:, 0:512])
            return e.bn_stats(stat[:, so + 6:so + 12], xap[:, 512:1024])
        P.op("dve", st1, reads=[xkey], writes=[("lnst", par, 0)])
        P.op("dve", lambda e: e.bn_aggr(stat[:, so + 12:so + 14], stat[:, so:so + 12].rearrange("p (a b) -> p a b", b=6)),
             reads=[("lnst", par, 0)], writes=[("lnst", par, 1)])
        P.op("act", lambda e: e.activation(out=stat[:, so + 14:so + 15], in_=stat[:, so + 13:so + 14], func=AF.Sqrt, bias=self.eps_ap, scale=1.0),
             reads=[("lnst", par, 1)], writes=[("lnst", par, 2)])
        P.op("dve", lambda e: e.reciprocal(out=stat[:, so + 15:so + 16], in_=stat[:, so + 14:so + 15]), reads=[("lnst", par, 2)], writes=[("lnst", par, 3)])
        P.op("dve", lambda e: e.scalar_tensor_tensor(out=xap, in0=xap, scalar=stat[:, so + 12:so + 13], in1=g_bc, op0=ALU.subtract, op1=ALU.mult),
             reads=[xkey, ("lnst", par, 1), akey], writes=[xkey])
        P.op("dve", lambda e: e.scalar_tensor_tensor(out=xap, in0=xap, scalar=stat[:, so + 15:so + 16], in1=b_bc, op0=ALU.mult, op1=ALU.add),
             reads=[xkey, ("lnst", par, 3), akey], writes=[xkey])


def _local_index(half, n):
    tau = np.arange(n)
    return tau if half == 0 else 4095 - tau


_CACHE = {}


def _prep_static(inp):
    st = {}
    for l in range(DEPTH):
        for half in range(2):
            slots, bint = build_layer_slots(inp, l, half)
            sm = layer_small(inp, l, half)
            st[(l, half)] = (slots, bint, sm)
    return st


def _core_maps(inp, st, layer_ids, x_full, ctx_full, nxt):
    maps = []
    ident = np.eye(128, dtype=np.float32)
    for c in range(8):
        b, half = c // 2, c % 2
        idx = _local_index(half, nxt * 128)
        m = {}
        m["x"] = np.ascontiguousarray(x_full[b][idx])
        cl = ctx_full[b] if half == 0 else ctx_full[b][::-1]
        m["ctx"] = np.ascontiguousarray(cl)
        cf = np.concatenate([inp["c"][b].reshape(8, 128).T, inp["c_ctx"].reshape(8, 128).T], 1)
        m["cfm"] = np.ascontiguousarray(cf, dtype=np.float32)
        m["ident"] = ident
        m["rope"] = rope_table(half, nxt * 128)
        for li, l in enumerate(layer_ids):
            slots, bint, sm = st[(l, half)]
            m["ws%d" % li] = slots
            m["bint%d" % li] = bint
            m["wsT%d" % li] = sm["wsT"]
            m["bsT%d" % li] = sm["bsT"]
            m["convp%d" % li] = sm["convp"].reshape(128, 176)
            m["sgln%d" % li] = sm["sgln"].reshape(1, 1024)
            m["lnrow%d" % li] = sm["lnrow"].reshape(1, 4096)
            m["wada%d" % li] = sm["wada"]
            m["bada%d" % li] = sm["bada"]
        maps.append(m)
    return maps


def _assemble(res, key, ntiles):
    out = np.zeros((4, 4096, D), np.float32)
    for c in range(8):
        b, half = c // 2, c % 2
        o = np.asarray(res.results[c][key])
        if half == 0:
            out[b, 0:ntiles * 128] = o
        else:
            out[b, 4096 - ntiles * 128:] = o[::-1]
    return out


def _assemble_ctx(res):
    out = np.zeros((4, 256, D), np.float32)
    for b in range(4):
        out[b] = np.asarray(res.results[2 * b]["ctxout"])
    return out


FUSED = True


def kernel(**inputs):
    inp = {k: np.asarray(v, dtype=np.float32) for k, v in inputs.items()}
    st = _prep_static(inp)
    if FUSED:
        if "fused" not in _CACHE:
            bld = Builder([dict(nkv=21, nmix=19, nffn=19, ctx="full"), dict(nkv=19, nmix=17, nffn=16, ctx="kv")],
                          n_x_in=21, out_tiles=16, ctx_out=False)
            _CACHE["fused"] = bld.build()
        nc = _CACHE["fused"]
        maps = _core_maps(inp, st, [0, 1], inp["x"], inp["ctx"], 21)
        res = run_bass_kernel_spmd(nc, maps, core_ids=list(range(8)))
        return _assemble(res, "out", 16)
    if "single" not in _CACHE:
        bld = Builder([dict(nkv=19, nmix=17, nffn=16, ctx="full")], n_x_in=19, out_tiles=16, ctx_out=True)
        _CACHE["single"] = bld.build()
    nc = _CACHE["single"]
    x = inp["x"]
    ctx = inp["ctx"]
    for l in range(DEPTH):
        maps = _core_maps(inp, st, [l], x, ctx, 19)
        res = run_bass_kernel_spmd(nc, maps, core_ids=list(range(8)))
        x = _assemble(res, "out", 16)
        ctx = _assemble_ctx(res)
    return x
```

```python
import numpy as np
import concourse.bass as bass
import concourse.mybir as mybir
from concourse.bass_utils import run_bass_kernel_spmd

F32, BF16 = mybir.dt.float32, mybir.dt.bfloat16
AF = mybir.ActivationFunctionType
ALU = mybir.AluOpType

D = 1024
NH = 8
HD = 64
DFF = 2816
NCH = 22
DEPTH = 2
ALPHA = float((2 * DEPTH) ** 0.25)
LN_EPS = 1e-5
MASKV = -30000.0
NSLOT_M = 15
ENGS = ("pe", "act", "dve", "pool", "sp")

S_KA, S_KB, S_V, S_QA, S_QB, S_U, S_SV = 0, 1, 2, 3, 4, 5, 6
S_GA0, S_GB0, S_PAB0, S_GA1, S_GB1, S_PAB1, S_WO0, S_WO1 = 7, 8, 9, 10, 11, 12, 13, 14
S_E0, S_E1 = 15, 16
S_UP0 = 17
S_DN0 = 28
NSLOTS = 34


def _perm128():
    j = np.arange(128)
    d = j % 64
    pd = np.where((d % 32) < 16, d + 16, d - 16)
    return (j // 64) * 64 + pd


def _slot_cols(W, cols):
    blk = W[:, cols]
    return blk.reshape(8, 128, 512).transpose(1, 0, 2).reshape(128, 4096)


def bias_tile(half, jq, jk, rpb):
    kk = np.arange(128)
    lq_row = 2 * jq + kk // 64
    lq_col = kk % 64
    lk_row = 2 * jk + kk // 64
    lk_col = kk % 64
    if half == 0:
        r, qc, kr, kc = lq_row, lq_col, lk_row, lk_col
    else:
        r, qc, kr, kc = 63 - lq_row, 63 - lq_col, 63 - lk_row, 63 - lk_col
    rs = np.clip(r - 4, 0, 56)
    ws = np.clip(qc - 8, 0, 48)
    KR, KC = kr[:, None], kc[:, None]
    valid = ((KR >= rs[None, :]) & (KR < rs[None, :] + 8) & (KC >= ws[None, :]) & (KC < ws[None, :] + 16)
             & (KR >= 0) & (KR <= 63))
    dy = np.clip(KR - r[None, :] + 7, 0, 14)
    dx = np.clip(KC - qc[None, :] + 15, 0, 30)
    vals = rpb[:, dy, dx]
    out = np.where(valid[None], vals, np.float32(MASKV)).astype(np.float32)
    return np.ascontiguousarray(out.transpose(1, 0, 2))


def build_layer_slots(inp, l, half):
    w_in = inp["w_in"][l]
    perm = _perm128()
    slots = np.zeros((NSLOTS, 128, 4096), np.float32)

    def qk_cols(base, hp0):
        c = []
        for hp in (hp0, hp0 + 1):
            c.append(base + hp * 128 + np.arange(128))
        for hp in (hp0, hp0 + 1):
            c.append(base + hp * 128 + perm)
        return np.concatenate(c)

    slots[S_KA] = _slot_cols(w_in, qk_cols(512, 0))
    slots[S_KB] = _slot_cols(w_in, qk_cols(512, 2))
    slots[S_V] = _slot_cols(w_in, 1024 + np.arange(512))
    slots[S_QA] = _slot_cols(w_in, qk_cols(0, 0))
    slots[S_QB] = _slot_cols(w_in, qk_cols(0, 2))
    slots[S_U] = _slot_cols(w_in, 1536 + np.arange(512))
    slots[S_SV] = _slot_cols(w_in, 2048 + np.arange(512))
    for s in range(2):
        slots[S_GA0 + 3 * s] = _slot_cols(w_in, 2560 + s * 512 + np.arange(512))
        slots[S_GB0 + 3 * s] = _slot_cols(w_in, 3584 + s * 512 + np.arange(512))
        pa = inp["w_pa"][l][:, s * 512:(s + 1) * 512].reshape(4, 128, 512)
        pb = inp["w_pb"][l][:, s * 512:(s + 1) * 512].reshape(4, 128, 512)
        slots[S_PAB0 + 3 * s] = np.concatenate([pa, pb], 0).transpose(1, 0, 2).reshape(128, 4096)
        slots[S_WO0 + s] = _slot_cols(inp["w_o"][l], s * 512 + np.arange(512))
    rpb = inp["rpb"][l]
    for jq, sl in ((0, S_E0), (1, S_E1)):
        t = np.stack([bias_tile(half, jq, jk, rpb) for jk in range(4)], 1)
        slots[sl] = t.reshape(128, 4096)
    w_up = inp["w_up"][l]
    for s in range(11):
        cols = np.concatenate([(2 * s) * 128 + np.arange(128), (2 * s + 1) * 128 + np.arange(128),
                               DFF + (2 * s) * 128 + np.arange(128), DFF + (2 * s + 1) * 128 + np.arange(128)])
        slots[S_UP0 + s] = _slot_cols(w_up, cols)
    wd = np.zeros((24 * 128, 1024), np.float32)
    wd[:DFF] = inp["w_down"][l]
    for h in range(2):
        for i in range(3):
            blk = wd[i * 1024:(i + 1) * 1024, h * 512:(h + 1) * 512]
            slots[S_DN0 + h * 3 + i] = blk.reshape(8, 128, 512).transpose(1, 0, 2).reshape(128, 4096)
    bint = np.stack([bias_tile(half, 4, 4 + o, rpb) for o in range(-2, 3)], 1).reshape(128, 5 * 8 * 128)
    return slots, np.ascontiguousarray(bint)


def rope_table(half, ntok):
    tau = np.arange(ntok)
    t = tau if half == 0 else 4095 - tau
    row = (t // 64).astype(np.float32)
    col = (t % 64).astype(np.float32)
    inv = (np.float32(10000.0) ** (-np.arange(16, dtype=np.float32) / np.float32(16))).astype(np.float32)
    p = np.arange(128)
    d = p % 64
    f = d % 16
    pos = np.where((d < 32)[:, None], row[None, :], col[None, :]).astype(np.float32)
    ang = (pos * inv[f][:, None]).astype(np.float32)
    sgn = np.where((d % 32) < 16, -1.0, 1.0).astype(np.float32)[:, None]
    C = np.cos(ang).astype(np.float32)
    S = (np.sin(ang).astype(np.float32) * sgn).astype(np.float32)
    return np.ascontiguousarray(np.stack([C, S], 1))


def layer_small(inp, l, half):
    w_s = inp["w_s"][l]
    b_s = inp["b_s"][l]
    cw = inp["conv_w"][l]
    if half == 1:
        w_s = w_s[:, ::-1, ::-1]
        b_s = b_s[:, ::-1]
        cw = cw[::-1]
    wsT = np.ascontiguousarray(w_s.transpose(2, 0, 1)).reshape(128, 8 * 128)
    bsT = np.ascontiguousarray(b_s.T)
    cb = inp["conv_b"][l]
    convp = np.stack([cw[0], cw[1], cw[2], cb], -1).reshape(44, 128, 4).transpose(1, 0, 2)
    sgln = np.stack([inp["sg_ln_g"][l], inp["sg_ln_b"][l]])[None]
    lnrow = np.stack([inp["ln1_g"][l], inp["ln1_b"][l], inp["ln2_g"][l], inp["ln2_b"][l]])[None]
    lnfm = np.stack([inp["ln1_g"][l].reshape(8, 128).T, inp["ln1_b"][l].reshape(8, 128).T], 1)
    wada = np.ascontiguousarray(inp["w_ada"][l].reshape(8, 128, 6144).transpose(1, 0, 2))
    bada = inp["b_ada"][l][None]
    return dict(wsT=np.ascontiguousarray(wsT, dtype=np.float32), bsT=np.ascontiguousarray(bsT, dtype=np.float32),
                convp=np.ascontiguousarray(convp, dtype=np.float32), sgln=np.ascontiguousarray(sgln, dtype=np.float32),
                lnrow=np.ascontiguousarray(lnrow, dtype=np.float32), lnfm=np.ascontiguousarray(lnfm, dtype=np.float32),
                wada=wada, bada=np.ascontiguousarray(bada, dtype=np.float32))


class Prog:
    def __init__(self):
        self.q = {e: [] for e in ENGS}
        self.cnt = {}
        self.known = {e: {} for e in ENGS}
        self.last_w = {}
        self.readers = {}
        self.semnames = set(ENGS)

    def op(self, eng, fn, reads=(), writes=(), sem=None, inc=1):
        writes = list(writes) + [k for k in reads if k[0] == "ps" and k not in writes]
        reads = [k for k in reads if k[0] != "ps"]
        deps = {}

        def add(tok):
            if tok is None:
                return
            s, v, e = tok
            if e == "pe" and eng == "pe" and s == "pe":
                return
            if deps.get(s, 0) < v:
                deps[s] = v
        for k in reads:
            add(self.last_w.get(k))
        for k in writes:
            add(self.last_w.get(k))
            for s, (v, e) in self.readers.get(k, {}).items():
                add((s, v, e))
        kn = self.known[eng]
        for s, v in deps.items():
            if kn.get(s, 0) < v:
                self.q[eng].append(("wait", s, v))
                kn[s] = v
        s = sem or eng
        self.semnames.add(s)
        self.cnt[s] = self.cnt.get(s, 0) + inc
        tok = (s, self.cnt[s], eng)
        self.q[eng].append(("ins", fn, s, inc))
        for k in writes:
            self.last_w[k] = tok
            self.readers[k] = {}
        for k in reads:
            d = self.readers.setdefault(k, {})
            if d.get(s, (0, None))[0] < tok[1]:
                d[s] = (tok[1], eng)
        return tok

    def wait_all(self, eng, semname):
        v = self.cnt.get(semname, 0)
        if v and self.known[eng].get(semname, 0) < v:
            self.q[eng].append(("wait", semname, v))
            self.known[eng][semname] = v

    def replay(self, eng, e, sems):
        for it in self.q[eng]:
            if it[0] == "wait":
                e.wait_ge(sems[it[1]], it[2])
            else:
                ins = it[1](e)
                ins.then_inc(sems[it[2]], it[3])


class Mem:
    def __init__(self):
        self.off = 0
        self.peak = 0

    def alloc(self, nbytes):
        o = self.off
        self.off += (nbytes + 63) // 64 * 64
        self.peak = max(self.peak, self.off)
        return o


class Builder:
    def __init__(self, layers, n_x_in, out_tiles, ctx_out, dbg=None):
        self.layers = layers
        self.n_x_in = n_x_in
        self.out_tiles = out_tiles
        self.ctx_out = ctx_out
        self.dbg = dbg
        self.P = Prog()
        self.nc = bass.Bass("TRN2", target_bir_lowering=False)
        self.bank_i = 0
        self.bank_n = 8
        self.fbank_i = 0
        self.wacq = 0
        self.wrel_n = 0
        self.wemit = 0
        self.wplan = []
        self.wcached = set()
        self._plan()
        self.wlast = {}
        for u, k in enumerate(self.wplan):
            self.wlast[k] = u

    def _plan(self):
        for li, L in enumerate(self.layers):
            for g in self.m_groups(L):
                if g["kind"] == "ctx" and L["ctx"] == "kv":
                    seq = [S_KA, S_KB, S_V]
                else:
                    seq = [S_KA, S_KB, S_V, S_QA, S_QB, S_SV, S_U]
                    if g["kind"] == "lat" and g["G"] == 0:
                        seq += [S_E0, S_E1]
                    seq += list(range(7, 15))
                for s in seq:
                    self.wplan.append((li, s))
            if self.dbg == "mid%d" % li:
                break
            for g in self.f_groups(L):
                for s in range(S_UP0, S_UP0 + 17):
                    self.wplan.append((li, s))

    def m_groups(self, L):
        gs = [dict(kind="ctx", G=-1, q=[0, 1], kv=[0, 1], base=0)]
        nmix, nkv = L["nmix"], L["nkv"]
        G = 0
        while 4 * G < nmix:
            q = list(range(4 * G, min(4 * G + 4, nmix)))
            kv = list(range(0, min(6, nkv))) if G == 0 else list(range(4 * G + 2, min(4 * G + 6, nkv)))
            gs.append(dict(kind="lat", G=G, q=q, kv=kv, base=4 * G))
            G += 1
        return gs

    def f_groups(self, L):
        gs = []
        if L["ctx"] == "full":
            gs.append(dict(kind="ctx", G=-1, t=[0, 1]))
        G = 0
        while 4 * G < L["nffn"]:
            gs.append(dict(kind="lat", G=G, t=list(range(4 * G, min(4 * G + 4, L["nffn"])))))
            G += 1
        return gs

    def raw(self, off, dt, n):
        if dt == BF16:
            return self.mem[:, off // 2: off // 2 + n]
        return self.mem[:, off // 2: off // 2 + 2 * n].bitcast(F32)

    def view(self, off, dt, shape):
        ap = self.raw(off, dt, int(np.prod(shape)))
        if len(shape) == 2:
            return ap.rearrange("p (a b) -> p a b", b=shape[1])
        if len(shape) == 3:
            return ap.rearrange("p (a b c) -> p a b c", b=shape[1], c=shape[2])
        return ap

    def bank(self):
        b = self.bank_i % self.bank_n
        self.bank_i = (b + 1) % self.bank_n
        return b

    def xslot(self, kind, t):
        if kind == "ctx":
            return 19 + t
        return t if t <= 18 else t + 2

    def wget(self, li, s):
        u = self.wacq
        assert self.wplan[u] == (li, s), (u, self.wplan[u], li, s)
        assert u < self.wrel_n + 3
        self._wpump()
        assert self.wemit > u
        self.wacq += 1
        return u % 3

    def wrel(self):
        self.wrel_n += 1
        assert self.wrel_n <= self.wacq
        self._wpump()

    def _wpump(self):
        while self.wemit < min(self.wrel_n + 3, len(self.wplan)):
            u = self.wemit
            li, s = self.wplan[u]
            slot = u % 3
            dst = self.wring[slot]
            if (li, s) in self.wcached:
                src = self.dr["wsb%d" % li][s]
                self.P.op("pool", lambda e, dst=dst, src=src: e.dma_start(out=dst, in_=src),
                          reads=[("wsb", li, s)], writes=[("w", slot)], sem="w%d" % slot, inc=16)
            else:
                src = self.dr["ws%d" % li][s]
                self.P.op("pool", lambda e, dst=dst, src=src: e.dma_start(out=dst, in_=src),
                          writes=[("w", slot)], sem="w%d" % slot, inc=16)
                if self.wlast[(li, s)] > u and s not in (S_E0, S_E1):
                    sdst = self.dr["wsb%d" % li][s]
                    self.P.op("sp", lambda e, dst=dst, sdst=sdst: e.dma_start(out=sdst, in_=dst),
                              reads=[("w", slot)], writes=[("wsb", li, s)], sem="wst%d" % slot, inc=16)
                    self.wcached.add((li, s))
            self.wemit = u + 1

    def dma(self, eng, dst, src, writes, sem, reads=(), **kw):
        return self.P.op(eng, lambda e: e.dma_start(out=dst, in_=src, **kw), reads=reads, writes=writes, sem=sem, inc=16)

    def mm(self, out_ap, pairs, reads, bank):
        def fn(e):
            n = len(pairs)
            ins = None
            for i, (l, r) in enumerate(pairs):
                ins = e.matmul(out_ap, l, r, start=(i == 0), stop=(i == n - 1))
            return ins
        return self.P.op("pe", fn, reads=reads, writes=[("ps", bank)])

    def barrier(self):
        P = self.P
        for eng in ENGS:
            for s in ["pe", "act", "dve"] + [k for k in P.cnt if k.startswith("ms")]:
                if s != eng:
                    P.wait_all(eng, s)

    def build(self):
        nc = self.nc
        P = self.P
        dr = {}
        self.dr = dr
        NL = len(self.layers)
        nxt = self.n_x_in

        def din(name, shape):
            dr[name] = nc.dram_tensor(name, list(shape), F32, kind="ExternalInput").ap()
        din("x", (nxt * 128, D))
        din("ctx", (256, D))
        din("cfm", (128, 16))
        din("ident", (128, 128))
        din("rope", (128, 2, nxt * 128))
        for li in range(NL):
            din("ws%d" % li, (NSLOTS, 128, 4096))
            din("bint%d" % li, (128, 5 * 8 * 128))
            din("wsT%d" % li, (128, 1024))
            din("bsT%d" % li, (128, 8))
            din("convp%d" % li, (128, 44 * 4))
            din("sgln%d" % li, (1, 1024))
            din("lnrow%d" % li, (1, 4096))
            din("wada%d" % li, (128, 8, 6144))
            din("bada%d" % li, (1, 6144))
        dr["out"] = nc.dram_tensor("out", [self.out_tiles * 128, D], F32, kind="ExternalOutput").ap()
        if self.ctx_out:
            dr["ctxout"] = nc.dram_tensor("ctxout", [256, D], F32, kind="ExternalOutput").ap()
        if self.dbg:
            dr["dbg"] = nc.dram_tensor("dbg", [21 * 128, D], F32, kind="ExternalOutput").ap()
        dr["modscr"] = nc.dram_tensor("modscr", [NL * 2, 6144], F32, kind="Internal").ap()
        for li in range(NL):
            dr["wsb%d" % li] = nc.dram_tensor("wsb%d" % li, [NSLOTS, 128, 4096], BF16, kind="Internal").ap()

        M = Mem()
        NXS = 23 if max(L_['nkv'] for L_ in self.layers) > 19 else 21
        o_x = M.alloc(NXS * 4096)
        o_w = [M.alloc(8192) for _ in range(3)]
        o_identb = M.alloc(256)
        o_identf = M.alloc(512)
        o_wsT = M.alloc(2048)
        o_bsT = M.alloc(64)
        o_modfm = M.alloc(len(self.layers) * 2 * 48 * 4)
        o_vec = M.alloc(8 * 8 * 4)
        o_stat = M.alloc(512)
        mark = M.off
        o_kT = M.alloc(4 * 8 * 128 * 2)
        o_V = M.alloc(8 * 520 * 2)
        o_ckT = M.alloc(4 * 256 * 2)
        o_cV = M.alloc(2 * 520 * 2)
        o_bint = M.alloc(5 * 8 * 128 * 2)
        o_hT = M.alloc(8 * 768 * 2)
        o_q = M.alloc(8 * 512 * 2)
        o_gu = M.alloc(4 * 512 * 2)
        o_vln = M.alloc(4 * 512 * 2)
        o_pt = M.alloc(3 * 1024 * 2)
        o_rope = M.alloc(2 * 768 * 4)
        o_sgb = o_rope
        o_t = o_rope + 4096
        o_oa = M.alloc(512 * 2)
        o_sgln = M.alloc(2 * 512 * 4)
        o_gabc = M.alloc(4096)
        o_ln1bc = M.alloc(8192)
        m_end = M.off
        M.off = mark
        o_hfT = [M.alloc(8 * 514 * 2) for _ in range(2)]
        o_prod = M.alloc(22 * 512 * 2)
        o_ta = [M.alloc(2048) for _ in range(4)]
        o_tg = [M.alloc(2048) for _ in range(4)]
        o_convp = M.alloc(44 * 4 * 4)
        o_fbc = M.alloc(3 * 4096)
        f_end = M.off
        M.off = mark
        o_wada = [M.alloc(8192) for _ in range(2)]
        o_modrow = [M.alloc(2048) for _ in range(2)]
        o_bada = [M.alloc(2048) for _ in range(2)]
        o_scol = M.alloc(32 * 4)
        o_sbf = M.alloc(64)
        total = M.peak
        self.sbuf_bytes = total
        assert total <= 212800, total

        import contextlib
        with contextlib.ExitStack() as st:
            self.mem = st.enter_context(nc.sbuf_tensor("mem", [128, total // 2], BF16))
            ps = [st.enter_context(nc.psum_tensor("ps%d" % i, [128, 512], F32)) for i in range(8)]
            self.ps = ps
            v, raw = self.view, self.raw
            X = v(o_x, F32, (NXS, D))
            self.wring = [raw(o, BF16, 4096) for o in o_w]
            identb = raw(o_identb, BF16, 128)
            identf = raw(o_identf, F32, 128)
            wsT = v(o_wsT, BF16, (8, 128))
            bsT = raw(o_bsT, F32, 8)
            modfm_l = [v(o_modfm + l_ * 384, F32, (2, 48)) for l_ in range(len(self.layers))]
            vec = v(o_vec, F32, (8, 8))
            stat = raw(o_stat, F32, 128)
            self.stat = stat
            kT = v(o_kT, BF16, (4, 1024))
            Vr = v(o_V, BF16, (8, 520))
            ckT = v(o_ckT, BF16, (4, 256))
            cV = v(o_cV, BF16, (2, 520))
            bint = v(o_bint, BF16, (5, 8, 128))
            hT = v(o_hT, BF16, (8, 768))
            qT = v(o_q, BF16, (8, 512))[:, 0:4, :]
            qrT = v(o_q, BF16, (8, 512))[:, 4:8, :]
            yT = v(o_q, BF16, (8, 512))
            gu = v(o_gu, BF16, (4, 512))
            oaT = v(o_gu, BF16, (4, 512))
            vln = v(o_vln, BF16, (4, 512))
            obT = v(o_vln, BF16, (4, 512))
            PT = [raw(o_pt + i * 2048, BF16, 1024) for i in range(3)]
            tmpf = [raw(o_pt + i * 2048, F32, 512) for i in range(2)]
            sga = v(o_pt, BF16, (4, 512))
            sgb = v(o_sgb, BF16, (4, 512))
            tt_ = raw(o_t, F32, 512)
            rope = v(o_rope, F32, (2, 768))
            oa = raw(o_oa, BF16, 512)
            sgln = v(o_sgln, F32, (2, 512))
            gabc = raw(o_gabc, F32, 1024)
            ln1bc = v(o_ln1bc, F32, (2, 1024))
            hfT = [v(o, BF16, (8, 514)) for o in o_hfT]
            prodT = v(o_prod, BF16, (22, 512))
            ta2 = [raw(o, F32, 512) for o in o_ta]
            tg2 = [raw(o, F32, 512) for o in o_tg]
            convp = v(o_convp, F32, (44, 4))
            fbc = v(o_fbc, F32, (3, 1024))
            wadab = [v(o, BF16, (8, 512)) for o in o_wada]
            sbf = raw(o_sbf, BF16, 16)
            modrow = [raw(o, F32, 512) for o in o_modrow]
            badab = [raw(o, F32, 512) for o in o_bada]
            scol = raw(o_scol, F32, 32)

            for t in range(2):
                sl = 19 + t
                self.dma("sp", X[:, sl, :], dr["ctx"][t * 128:(t + 1) * 128, :], [("x", sl)], "x%d" % sl)
            for t in range(nxt):
                sl = self.xslot("lat", t)
                self.dma("sp", X[:, sl, :], dr["x"][t * 128:(t + 1) * 128, :], [("x", sl)], "x%d" % sl)
            self.dma("sp", identf, dr["ident"], [("identf",)], "t0")
            self.dma("pool", identb, dr["ident"], [("identb",)], "t1")
            self.dma("sp", scol[:, 0:16], dr["cfm"], [("scol",)], "t2")
            P.op("act", lambda e: e.activation(out=sbf, in_=scol[:, 0:16], func=AF.Silu),
                 reads=[("scol",)], writes=[("ssil",)])
            self.bank_n = 7
            self.bank_i = 0
            for li in range(NL):
                for nb in range(12):
                    pb = nb % 2
                    wb = wadab[pb]
                    self.dma("pool", wb, dr["wada%d" % li][:, :, nb * 512:(nb + 1) * 512], [("wada", pb)], "wa%d" % pb)
                    self.dma("sp", badab[pb][0:2, :], dr["bada%d" % li][0:1, nb * 512:(nb + 1) * 512].partition_broadcast(2), [("bada", pb)], "ba%d" % pb)
                    b = self.bank()
                    pairs = [(sbf[:, kc:16:8], wb[:, kc, :]) for kc in range(8)]
                    self.mm(ps[b][0:2, :], pairs, [("ssil",), ("wada", pb)], b)
                    P.op("dve", lambda e, b=b, pb=pb: e.tensor_tensor(out=modrow[pb][0:2, :], in0=ps[b][0:2, :], in1=badab[pb][0:2, :], op=ALU.add),
                         reads=[("ps", b), ("bada", pb)], writes=[("modrow", pb)])
                    self.dma("sp", dr["modscr"][2 * li:2 * li + 2, nb * 512:(nb + 1) * 512], modrow[pb][0:2, :], [("modscr", li, nb)], "ms%d" % pb,
                             reads=[("modrow", pb)])

                    def trm(e, nb=nb, pb=pb):
                        ins = None
                        for j in range(4):
                            c = nb * 4 + j
                            ins = e.matmul(ps[7][:, 2 * c:2 * c + 2], modrow[pb][0:2, j * 128:(j + 1) * 128], identf[0:2, 0:2], start=True, stop=True)
                        return ins
                    P.op("pe", trm, reads=[("modrow", pb), ("identf",)], writes=[("ps", 7)])
                P.op("dve", lambda e, li=li: e.tensor_copy(out=modfm_l[li], in_=ps[7][:, 0:96].rearrange("p (c w) -> p w c", w=2)),
                     reads=[("ps", 7)], writes=[("modfm", li)])
            MODSCR = lambda li: [("modscr", li, nb) for nb in range(12)]
            self.bank_n = 8
            self.barrier()

            for li, L in enumerate(self.layers):
                last_layer = (li == NL - 1)
                rd = MODSCR(li)
                P.op("dve", lambda e: e.memset(raw(o_V, BF16, 8 * 520), 1.0), writes=[("V", s) for s in range(8)])
                P.op("dve", lambda e: e.memset(raw(o_cV, BF16, 2 * 520), 1.0), writes=[("cV", 0), ("cV", 1)])
                modfm = modfm_l[li].rearrange("p w (m k) -> p w m k", m=6)
                self.dma("pool", wsT, dr["wsT%d" % li].rearrange("p (g q) -> p g q", g=8), [("wsT",)], "t6")
                self.dma("sp", bsT, dr["bsT%d" % li], [("bsT",)], "t7")
                self.dma("pool", raw(o_bint, BF16, 5120), dr["bint%d" % li], [("bint",)], "t8")
                P.op("dve", lambda e: e.tensor_scalar(out=raw(o_bint, BF16, 5120), in0=raw(o_bint, BF16, 5120), scalar1=8.0, scalar2=None, op0=ALU.mult),
                     reads=[("bint",)], writes=[("bint",)])
                self.dma("sp", raw(o_sgln, F32, 1024), dr["sgln%d" % li].partition_broadcast(128), [("sgln",)], "t9")
                self.dma("sp", raw(o_ln1bc, F32, 2048), dr["lnrow%d" % li][0:1, 0:2048].partition_broadcast(128), [("ln1bc",)], "t10")

                def dv(e, modfm=modfm):
                    ins = None
                    for w in range(2):
                        e.tensor_scalar(out=vec[:, 2 * w, :], in0=modfm[:, w, 1, :], scalar1=1.0, scalar2=None, op0=ALU.add)
                        e.tensor_copy(out=vec[:, 2 * w + 1, :], in_=modfm[:, w, 0, :])
                        e.tensor_scalar(out=vec[:, 4 + 2 * w, :], in0=modfm[:, w, 4, :], scalar1=1.0, scalar2=None, op0=ALU.add)
                        ins = e.tensor_copy(out=vec[:, 5 + 2 * w, :], in_=modfm[:, w, 3, :])
                    return ins
                P.op("dve", dv, reads=[("modfm", li)], writes=[("vec",)])
                VEC = [("vec",)]

                def m_ht(g):
                    kind = g["kind"]
                    ctxkv_only = (kind == "ctx" and L["ctx"] == "kv")
                    qtiles, kvtiles, base = g["q"], g["kv"], g["base"]
                    ntq = len(qtiles)
                    nq = ntq * 128
                    tiles_h = sorted(set(qtiles) | set(kvtiles)) if not ctxkv_only else kvtiles
                    npos = tiles_h[-1] - base + 1
                    wv = 2 if kind == "ctx" else 0
                    isg0 = (kind == "lat" and g["G"] == 0)
                    if kind == "lat":
                        self.dma("sp", rope[:, :, 0:npos * 128], dr["rope"][:, :, base * 128:(base + npos) * 128], [("rope",)], "t12")
                    for p in range(npos):
                        sl = self.xslot(kind, base + p)
                        for cb in range(2):
                            b = self.bank()

                            def tr(e, sl=sl, cb=cb, b=b):
                                ins = None
                                for ci in range(4):
                                    c = cb * 4 + ci
                                    ins = e.transpose(ps[b][:, ci * 128:(ci + 1) * 128], X[:, sl, c * 128:(c + 1) * 128], identf)
                                return ins
                            P.op("pe", tr, reads=[("x", sl), ("identf",)], writes=[("ps", b)])
                            eng = "act" if cb == 0 else "dve"

                            def ev(e, p=p, cb=cb, b=b, eng=eng, wv=wv):
                                ins = None
                                for ci in range(4):
                                    c = cb * 4 + ci
                                    o = hT[:, c, p * 128:(p + 1) * 128]
                                    i_ = ps[b][:, ci * 128:(ci + 1) * 128]
                                    if eng == "act":
                                        ins = e.activation(out=o, in_=i_, func=AF.Identity, bias=vec[:, wv + 1, c:c + 1], scale=vec[:, wv, c:c + 1])
                                    else:
                                        ins = e.tensor_scalar(out=o, in0=i_, scalar1=vec[:, wv, c:c + 1], scalar2=vec[:, wv + 1, c:c + 1],
                                                              op0=ALU.mult, op1=ALU.add)
                                return ins
                            P.op(eng, ev, reads=[("ps", b)] + VEC, writes=[("hT", p, cb)])


                def HT(p0, p1):
                    return [("hT", p, cb) for p in range(p0, p1) for cb in range(2)]

                def m_group(g, between):
                    kind = g["kind"]
                    ctxkv_only = (kind == "ctx" and L["ctx"] == "kv")
                    qtiles, kvtiles, base = g["q"], g["kv"], g["base"]
                    ntq = len(qtiles)
                    nq = ntq * 128
                    wv = 2 if kind == "ctx" else 0
                    isg0 = (kind == "lat" and g["G"] == 0)
                    if not ctxkv_only and (kind == "ctx" or isg0):
                        w = 1 if kind == "ctx" else 0
                        r_ = 2 * li + w
                        self.dma("sp", gabc, dr["modscr"][r_:r_ + 1, 2048:3072].partition_broadcast(128), [("gabc",)], "t11", reads=rd)
                    kp0 = kvtiles[0] - base
                    kp1 = kvtiles[-1] - base + 1
                    for si, S in enumerate((S_KA, S_KB)):
                        ws_ = self.wget(li, S)
                        W = self.wring[ws_].rearrange("p (k n) -> p k n", k=8)
                        for hl in range(2):
                            hp = si * 2 + hl
                            t0 = kp0 * 128
                            while t0 < kp1 * 128:
                                t1 = min(t0 + 512, kp1 * 128)
                                n = t1 - t0
                                b1 = self.bank()
                                self.mm(ps[b1][:, 0:n], [(W[:, kc, hl * 128:(hl + 1) * 128], hT[:, kc, t0:t1]) for kc in range(8)],
                                        [("w", ws_)] + HT(t0 // 128, t1 // 128), b1)
                                if kind == "ctx":
                                    P.op("act", lambda e, b1=b1, hp=hp, t0=t0, t1=t1, n=n: e.activation(out=ckT[:, hp, t0:t1], in_=ps[b1][:, 0:n], func=AF.Copy),
                                         reads=[("ps", b1)], writes=[("ckT", hp)])
                                else:
                                    b2 = self.bank()
                                    self.mm(ps[b2][:, 0:n], [(W[:, kc, 256 + hl * 128:256 + (hl + 1) * 128], hT[:, kc, t0:t1]) for kc in range(8)],
                                            [("w", ws_)] + HT(t0 // 128, t1 // 128), b2)
                                    P.op("dve", lambda e, b1=b1, t0=t0, t1=t1, n=n: e.tensor_tensor(out=tmpf[0][:, 0:n], in0=ps[b1][:, 0:n], in1=rope[:, 0, t0:t1], op=ALU.mult),
                                         reads=[("ps", b1), ("rope",)], writes=[("pt", 0)])
                                    P.op("dve", lambda e, b2=b2, t0=t0, t1=t1, n=n: e.tensor_tensor(out=ps[b2][:, 0:n], in0=ps[b2][:, 0:n], in1=rope[:, 1, t0:t1], op=ALU.mult),
                                         reads=[("ps", b2), ("rope",)], writes=[("ps", b2)])
                                    for tt in range(t0 // 128, t1 // 128):
                                        slot = (base + tt) % 8
                                        c0 = tt * 128 - t0
                                        P.op("dve", lambda e, b2=b2, c0=c0, slot=slot, hp=hp: e.tensor_tensor(
                                            out=kT[:, hp, slot * 128:(slot + 1) * 128], in0=tmpf[0][:, c0:c0 + 128], in1=ps[b2][:, c0:c0 + 128], op=ALU.add),
                                            reads=[("pt", 0), ("ps", b2)], writes=[("kT", slot, hp)])
                                t0 = t1
                        self.wrel()
                    ws_ = self.wget(li, S_V)
                    W = self.wring[ws_].rearrange("p (k n) -> p k n", k=8)
                    for p in range(kp0, kp1):
                        b = self.bank()
                        self.mm(ps[b][:, :], [(hT[:, kc, p * 128:(p + 1) * 128], W[:, kc, :]) for kc in range(8)], [("w", ws_)] + HT(p, p + 1), b)
                        if kind == "ctx":
                            dst, key = cV[:, p, :].rearrange("p (h d) -> p h d", d=65)[:, :, 0:64], ("cV", p)
                        else:
                            slot = (base + p) % 8
                            dst, key = Vr[:, slot, :].rearrange("p (h d) -> p h d", d=65)[:, :, 0:64], ("V", slot)
                        P.op("act", lambda e, b=b, dst=dst: e.activation(out=dst, in_=ps[b][:, :].rearrange("p (h d) -> p h d", d=64), func=AF.Copy),
                             reads=[("ps", b)], writes=[key])
                    self.wrel()
                    if ctxkv_only:
                        if between is not None:
                            between()
                        return
                    for si, S in enumerate((S_QA, S_QB)):
                        ws_ = self.wget(li, S)
                        W = self.wring[ws_].rearrange("p (k n) -> p k n", k=8)
                        for hl in range(2):
                            hp = si * 2 + hl
                            b1 = self.bank()
                            self.mm(ps[b1][:, 0:nq], [(W[:, kc, hl * 128:(hl + 1) * 128], hT[:, kc, 0:nq]) for kc in range(8)], [("w", ws_)] + HT(0, ntq), b1)
                            P.op("act", lambda e, b1=b1, hp=hp: e.activation(out=qT[:, hp, 0:nq], in_=ps[b1][:, 0:nq], func=AF.Copy),
                                 reads=[("ps", b1)], writes=[("q", hp)])
                            if kind == "lat":
                                b2 = self.bank()
                                self.mm(ps[b2][:, 0:nq], [(W[:, kc, 256 + hl * 128:256 + (hl + 1) * 128], hT[:, kc, 0:nq]) for kc in range(8)],
                                        [("w", ws_)] + HT(0, ntq), b2)
                                P.op("dve", lambda e, b1=b1: e.tensor_tensor(out=tmpf[0][:, 0:nq], in0=ps[b1][:, 0:nq], in1=rope[:, 0, 0:nq], op=ALU.mult),
                                     reads=[("ps", b1), ("rope",)], writes=[("pt", 0)])
                                P.op("dve", lambda e, b2=b2: e.tensor_tensor(out=ps[b2][:, 0:nq], in0=ps[b2][:, 0:nq], in1=rope[:, 1, 0:nq], op=ALU.mult),
                                     reads=[("ps", b2), ("rope",)], writes=[("ps", b2)])
                                P.op("dve", lambda e, b2=b2, hp=hp: e.tensor_tensor(out=qrT[:, hp, 0:nq], in0=tmpf[0][:, 0:nq], in1=ps[b2][:, 0:nq], op=ALU.add),
                                     reads=[("pt", 0), ("ps", b2)], writes=[("q", 4 + hp)])
                        self.wrel()
                    ws_ = self.wget(li, S_SV)
                    W = self.wring[ws_].rearrange("p (k n) -> p k n", k=8)
                    for i in range(ntq):
                        b = self.bank()
                        self.mm(ps[b][:, :], [(hT[:, kc, i * 128:(i + 1) * 128], W[:, kc, :]) for kc in range(8)], [("w", ws_)] + HT(i, i + 1), b)
                        tf = tmpf[i % 2]
                        pk = ("pt", i % 2)
                        P.op("act", lambda e, b=b, tf=tf: e.activation(out=tf, in_=ps[b][:, :], func=AF.Gelu_apprx_tanh),
                             reads=[("ps", b)], writes=[pk])
                        so = (i % 2) * 16
                        P.op("dve", lambda e, tf=tf, so=so: e.bn_stats(stat[:, so:so + 6], tf), reads=[pk], writes=[("stat", i % 2, 0)])
                        P.op("dve", lambda e, so=so: e.bn_aggr(stat[:, so + 6:so + 8], stat[:, so:so + 6]), reads=[("stat", i % 2, 0)], writes=[("stat", i % 2, 1)])
                        P.op("act", lambda e, so=so: e.activation(out=stat[:, so + 8:so + 9], in_=stat[:, so + 7:so + 8], func=AF.Sqrt, bias=self.eps_ap, scale=1.0),
                             reads=[("stat", i % 2, 1)], writes=[("stat", i % 2, 2)])
                        P.op("dve", lambda e, so=so: e.reciprocal(out=stat[:, so + 9:so + 10], in_=stat[:, so + 8:so + 9]),
                             reads=[("stat", i % 2, 2)], writes=[("stat", i % 2, 3)])
                        P.op("dve", lambda e, tf=tf, so=so: e.scalar_tensor_tensor(out=tf, in0=tf, scalar=stat[:, so + 6:so + 7], in1=sgln[:, 0, :],
                                                                                  op0=ALU.subtract, op1=ALU.mult),
                             reads=[pk, ("stat", i % 2, 1), ("sgln",)], writes=[pk])
                        P.op("dve", lambda e, tf=tf, so=so, i=i: e.scalar_tensor_tensor(out=vln[:, i, :], in0=tf, scalar=stat[:, so + 9:so + 10], in1=sgln[:, 1, :],
                                                                                       op0=ALU.mult, op1=ALU.add),
                             reads=[pk, ("stat", i % 2, 3), ("sgln",)], writes=[("vln", i)])
                    self.wrel()
                    ws_ = self.wget(li, S_U)
                    W = self.wring[ws_].rearrange("p (k n) -> p k n", k=8)
                    for i in range(ntq):
                        b = self.bank()
                        self.mm(ps[b][:, :], [(hT[:, kc, i * 128:(i + 1) * 128], W[:, kc, :]) for kc in range(8)], [("w", ws_)] + HT(i, i + 1), b)
                        P.op("act", lambda e, b=b, i=i: e.activation(out=gu[:, i, :], in_=ps[b][:, :], func=AF.Gelu_apprx_tanh),
                             reads=[("ps", b)], writes=[("gu", i)])
                    self.wrel()
                    for i in range(ntq):
                        b = self.bank()

                        def sgm(e, i=i, b=b):
                            ins = None
                            for gg in range(8):
                                ins = e.matmul(ps[b][:, gg * 64:(gg + 1) * 64], wsT[:, gg, :], vln[:, i, gg * 64:(gg + 1) * 64], start=True, stop=True)
                            return ins
                        P.op("pe", sgm, reads=[("wsT",), ("vln", i)], writes=[("ps", b)])

                        def sgv(e, i=i, b=b):
                            ins = None
                            for gg in range(8):
                                ins = e.scalar_tensor_tensor(out=gu[:, i, gg * 64:(gg + 1) * 64], in0=ps[b][:, gg * 64:(gg + 1) * 64], scalar=bsT[:, gg:gg + 1],
                                                             in1=gu[:, i, gg * 64:(gg + 1) * 64], op0=ALU.add, op1=ALU.mult)
                            return ins
                        P.op("dve", sgv, reads=[("ps", b), ("bsT",), ("gu", i)], writes=[("gu", i)])
                    for i in range(ntq):
                        b = self.bank()
                        pb16 = ps[b][:, :].bitcast(BF16)

                        def trb(e, i=i, pb16=pb16):
                            ins = None
                            for c in range(4):
                                ins = e.transpose(pb16[:, c * 128:(c + 1) * 128], gu[:, i, c * 128:(c + 1) * 128], identb)
                            return ins
                        P.op("pe", trb, reads=[("gu", i), ("identb",)], writes=[("ps", b)])
                        P.op("act", lambda e, i=i, pb16=pb16: e.activation(out=obT[:, :, i * 128:(i + 1) * 128], in_=pb16[:, 0:512].rearrange("p (c t) -> p c t", c=4), func=AF.Copy),
                             reads=[("ps", b)], writes=[("vln", jj) for jj in range(4)])
                    es = []
                    if isg0:
                        for S in (S_E0, S_E1):
                            sl_ = self.wget(li, S)
                            es.append(sl_)
                            P.op("dve", lambda e, sl_=sl_: e.tensor_scalar(out=self.wring[sl_], in0=self.wring[sl_], scalar1=8.0, scalar2=None, op0=ALU.mult),
                                 reads=[("w", sl_)], writes=[("w", sl_)])
                    jobs = [(i, j, h) for i, j in enumerate(qtiles) for h in range(NH)]
                    SB = [(0, 1), (2, 3), (4, 5)]
                    state = {}
                    nkv_l = L["nkv"]

                    def keylist(j):
                        if kind == "ctx":
                            return []
                        if j == 0:
                            return [(kt, ("e", 0, kt)) for kt in range(4)]
                        if j == 1:
                            return [(kt, ("e", 1, kt)) for kt in range(4)]
                        return [(kt, ("i", kt - j + 2)) for kt in range(j - 2, j + 3) if 0 <= kt < nkv_l]

                    def emit_S(n):
                        i, j, h = jobs[n]
                        hp, r0 = h // 2, (h % 2) * 64
                        kl = keylist(j)
                        nk = len(kl) + 2
                        ba, bb = SB[n % 3]
                        banks = [ba] * 4 + [bb] * 4
                        qsrc = qrT if kind == "lat" else qT

                        def fn(e):
                            ins = None
                            for ki, (kt, bt) in enumerate(kl):
                                o = ps[banks[ki]][:, (ki % 4) * 128:(ki % 4 + 1) * 128]
                                slot = kt % 8
                                ins = e.matmul(o, kT[r0:r0 + 64, hp, slot * 128:(slot + 1) * 128], qsrc[r0:r0 + 64, hp, i * 128:(i + 1) * 128], start=True, stop=True)
                            for c in range(2):
                                ki = len(kl) + c
                                o = ps[banks[ki]][:, (ki % 4) * 128:(ki % 4 + 1) * 128]
                                ins = e.matmul(o, ckT[r0:r0 + 64, hp, c * 128:(c + 1) * 128], qT[r0:r0 + 64, hp, i * 128:(i + 1) * 128], start=True, stop=True)
                            return ins
                        rds = [("q", hp), ("q", 4 + hp), ("ckT", hp), ("identb",), ("bint",)]
                        rds += [("kT", kt % 8, hp) for kt, _ in kl]
                        if isg0 and j < 2:
                            rds += [("w", es[j])]
                        wr = [("ps", ba)] + ([("ps", bb)] if nk > 4 else [])
                        P.op("pe", fn, reads=rds, writes=wr)
                        if kl:
                            if kl[0][1][0] == "i":
                                btab = bint[:, :, h, :]
                                brd = [("bint",)]
                            else:
                                btab = self.wring[es[kl[0][1][1]]].rearrange("p (o h q) -> p o h q", o=4, h=8)[:, :, h, :]
                                brd = [("w", es[kl[0][1][1]])]
                            nA = min(len(kl), 4)
                            P.op("dve", lambda e: e.tensor_tensor(out=ps[ba][:, 0:nA * 128].rearrange("p (o q) -> p o q", o=nA),
                                                                  in0=ps[ba][:, 0:nA * 128].rearrange("p (o q) -> p o q", o=nA),
                                                                  in1=btab[:, 0:nA, :], op=ALU.add),
                                 reads=[("ps", ba)] + brd, writes=[("ps", ba)])
                            if len(kl) > 4:
                                P.op("dve", lambda e: e.tensor_tensor(out=ps[bb][:, 0:128], in0=ps[bb][:, 0:128], in1=btab[:, 4, :], op=ALU.add),
                                     reads=[("ps", bb)] + brd, writes=[("ps", bb)])
                        pt = PT[n % 3]
                        n1 = min(nk, 4) * 128
                        P.op("act", lambda e: e.activation(out=pt[:, 0:n1], in_=ps[ba][:, 0:n1], func=AF.Exp, scale=0.125),
                             reads=[("ps", ba)], writes=[("pt", n % 3)])
                        if nk > 4:
                            n2 = (nk - 4) * 128
                            P.op("act", lambda e: e.activation(out=pt[:, 512:512 + n2], in_=ps[bb][:, 0:n2], func=AF.Exp, scale=0.125),
                                 reads=[("ps", bb)], writes=[("pt", n % 3)])
                        state[n] = (kl, nk)

                    def emit_PV(n):
                        i, j, h = jobs[n]
                        kl, nk = state[n]
                        ob_ = 6 + h // 4
                        oc = (h % 4) * 65
                        pt = PT[n % 3]

                        def fn(e):
                            ins = None
                            for ki in range(nk):
                                if ki < len(kl):
                                    rhs = Vr[:, kl[ki][0] % 8, h * 65:(h + 1) * 65]
                                else:
                                    rhs = cV[:, ki - len(kl), h * 65:(h + 1) * 65]
                                ins = e.matmul(ps[ob_][:, oc:oc + 65], pt[:, ki * 128:(ki + 1) * 128], rhs, start=(ki == 0), stop=(ki == nk - 1))
                            return ins
                        rds = [("pt", n % 3), ("cV", 0), ("cV", 1)] + [("V", kt % 8) for kt, _ in kl]
                        P.op("pe", fn, reads=rds, writes=[("ps", ob_)])
                        if h == NH - 1:
                            o0, o1 = 6, 7
                            so = 32 + (i % 2) * 8

                            def nrm(e):
                                ins = None
                                for hb, ob2 in enumerate((o0, o1)):
                                    ins = e.reciprocal(out=stat[:, so + hb * 4:so + hb * 4 + 4], in_=ps[ob2][:, 64:260:65])
                                return ins
                            P.op("dve", nrm, reads=[("ps", o0), ("ps", o1)], writes=[("rs", i % 2)])

                            def nrm2(e):
                                ins = None
                                for hb, ob2 in enumerate((o0, o1)):
                                    ins = e.tensor_tensor(out=oa[:, hb * 256:(hb + 1) * 256].rearrange("p (h d) -> p h d", d=64),
                                                          in0=ps[ob2][:, 0:260].rearrange("p (h d) -> p h d", d=65)[:, :, 0:64],
                                                          in1=stat[:, so + hb * 4:so + hb * 4 + 4].unsqueeze(2).broadcast_to([128, 4, 64]), op=ALU.mult)
                                return ins
                            P.op("dve", nrm2, reads=[("ps", o0), ("ps", o1), ("rs", i % 2)], writes=[("oa",)])
                            pb16 = ps[o0][:, :].bitcast(BF16)

                            def trb(e):
                                ins = None
                                for c in range(4):
                                    ins = e.transpose(pb16[:, c * 128:(c + 1) * 128], oa[:, c * 128:(c + 1) * 128], identb)
                                return ins
                            P.op("pe", trb, reads=[("oa",), ("identb",)], writes=[("ps", o0)])
                            P.op("dve", lambda e: e.tensor_copy(out=oaT[:, :, i * 128:(i + 1) * 128], in_=pb16[:, 0:512].rearrange("p (c t) -> p c t", c=4)),
                                 reads=[("ps", o0)], writes=[("gu", jj) for jj in range(4)])
                            if isg0 and j < 2:
                                self.wrel()
                    for n in range(len(jobs) + 2):
                        if n < len(jobs):
                            emit_S(n)
                        if 2 <= n:
                            emit_PV(n - 2)
                    GU = [("gu", jj) for jj in range(4)]
                    VLN = [("vln", jj) for jj in range(4)]
                    for s in range(2):
                        for which, S, dstb, dkey in ((0, S_GA0 + 3 * s, sga, "pt"), (1, S_GB0 + 3 * s, sgb, "sgb")):
                            wa = self.wget(li, S)
                            WA = self.wring[wa].rearrange("p (k n) -> p k n", k=8)
                            for fl in range(4):
                                b = self.bank()
                                self.mm(ps[b][:, 0:nq], [(WA[:, kc, fl * 128:(fl + 1) * 128], hT[:, kc, 0:nq]) for kc in range(8)], [("w", wa)] + HT(0, ntq), b)
                                key = ("pt", fl // 2) if which == 0 else ("sgb", fl)
                                P.op("act", lambda e, b=b, dstb=dstb, fl=fl: e.activation(out=dstb[:, fl, 0:nq], in_=ps[b][:, 0:nq], func=AF.Sigmoid),
                                     reads=[("ps", b)], writes=[key])
                            self.wrel()
                        wc = self.wget(li, S_PAB0 + 3 * s)
                        WC = self.wring[wc].rearrange("p (k n) -> p k n", k=8)
                        for fl in range(4):
                            f = 4 * s + fl
                            bC, bD = self.bank(), self.bank()
                            self.mm(ps[bC][:, 0:nq], [(WC[:, kc, fl * 128:(fl + 1) * 128], oaT[:, kc, 0:nq]) for kc in range(4)], [("w", wc)] + GU, bC)
                            self.mm(ps[bD][:, 0:nq], [(WC[:, 4 + kc, fl * 128:(fl + 1) * 128], obT[:, kc, 0:nq]) for kc in range(4)], [("w", wc)] + VLN, bD)
                            P.op("dve", lambda e, bC=bC, fl=fl: e.tensor_tensor(out=tt_[:, 0:nq], in0=ps[bC][:, 0:nq], in1=sga[:, fl, 0:nq], op=ALU.mult),
                                 reads=[("ps", bC), ("pt", fl // 2), ("rope",)], writes=[("tt",)])
                            P.op("dve", lambda e, bD=bD, fl=fl: e.tensor_tensor(out=ps[bD][:, 0:nq], in0=ps[bD][:, 0:nq], in1=sgb[:, fl, 0:nq], op=ALU.mult),
                                 reads=[("ps", bD), ("sgb", fl), ("rope",)], writes=[("ps", bD)])
                            P.op("dve", lambda e, bD=bD, f=f: e.tensor_tensor(out=yT[:, f, 0:nq], in0=tt_[:, 0:nq], in1=ps[bD][:, 0:nq], op=ALU.add),
                                 reads=[("ps", bD), ("tt",), ("rope",)], writes=[("q", f)])
                        self.wrel()
                    if between is not None:
                        between()
                    for hh in range(2):
                        wo = self.wget(li, S_WO0 + hh)
                        WO = self.wring[wo].rearrange("p (k n) -> p k n", k=8)
                        for i, j in enumerate(qtiles):
                            sl = self.xslot(kind, j)
                            b = self.bank()
                            self.mm(ps[b][:, :], [(yT[:, kc, i * 128:(i + 1) * 128], WO[:, kc, :]) for kc in range(8)],
                                    [("w", wo)] + [("q", f) for f in range(8)], b)
                            P.op("dve", lambda e, b=b, hh=hh: e.tensor_tensor(out=ps[b][:, :], in0=ps[b][:, :], in1=gabc[:, hh * 512:(hh + 1) * 512], op=ALU.mult),
                                 reads=[("ps", b), ("gabc",)], writes=[("ps", b)])
                            P.op("dve", lambda e, b=b, hh=hh, sl=sl: e.scalar_tensor_tensor(out=X[:, sl, hh * 512:(hh + 1) * 512], in0=X[:, sl, hh * 512:(hh + 1) * 512],
                                                                                            scalar=ALPHA, in1=ps[b][:, :], op0=ALU.mult, op1=ALU.add),
                                 reads=[("ps", b), ("x", sl)], writes=[("x", sl)])
                            if hh == 1:
                                self.layernorm(X[:, sl, :], ("x", sl), i % 2, affine=(ln1bc[:, 0, :], ln1bc[:, 1, :], ("ln1bc",)))
                        self.wrel()

                mgs = self.m_groups(L)
                m_ht(mgs[0])
                for gi, g_ in enumerate(mgs):
                    nx = (lambda k=gi + 1: m_ht(mgs[k])) if gi + 1 < len(mgs) else None
                    m_group(g_, nx)
                self.barrier()
                if self.dbg == "mid%d" % li:
                    for sl in range(21):
                        self.dma("sp", dr["dbg"][sl * 128:(sl + 1) * 128, :], X[:, sl, :], [("dbgout", sl)], "os", reads=[("x", sl)])
                    break

                self.dma("sp", raw(o_convp, F32, 176), dr["convp%d" % li], [("convp",)], "t13")
                self.dma("sp", raw(o_fbc + 4096, F32, 2048), dr["lnrow%d" % li][0:1, 2048:4096].partition_broadcast(128), [("fbc", 1)], "t14")
                def f_hf(g, hb_i):
                    kind = g["kind"]
                    tiles = g["t"]
                    nt = len(tiles)
                    ntok = nt * 128
                    wv = 6 if kind == "ctx" else 4
                    first = (kind == "ctx" or g["G"] == 0)
                    hf = hfT[hb_i % 2]
                    hfprev = hfT[(hb_i + 1) % 2]
                    hkey = ("hf", hb_i % 2)
                    pkey = ("hf", (hb_i + 1) % 2)
                    if first:
                        P.op("dve", lambda e, hf=hf: e.memset(hf[:, :, 0:1], 0.0), writes=[hkey])
                    else:
                        P.op("dve", lambda e, hf=hf, hfprev=hfprev: e.tensor_copy(out=hf[:, :, 0:1], in_=hfprev[:, :, 512:513]), reads=[pkey], writes=[hkey])
                    nxt_t = tiles[-1] + 1
                    have_r = (kind == "lat" and nxt_t < L["nmix"])
                    if have_r:
                        sl = self.xslot(kind, nxt_t)
                        b = self.bank()

                        def trh(e, sl=sl, b=b):
                            ins = None
                            for c in range(8):
                                ins = e.matmul(ps[b][:, c:c + 1], X[0:1, sl, c * 128:(c + 1) * 128], identf[0:1, 0:1], start=True, stop=True)
                            return ins
                        P.op("pe", trh, reads=[("x", sl), ("identf",)], writes=[("ps", b)])
                        P.op("dve", lambda e, b=b, wv=wv: e.tensor_tensor(out=ps[b][:, 0:8], in0=ps[b][:, 0:8], in1=vec[:, wv, :], op=ALU.mult),
                             reads=[("ps", b)] + VEC, writes=[("ps", b)])
                        P.op("dve", lambda e, b=b, wv=wv, hf=hf, ntok=ntok: e.tensor_tensor(out=hf[:, :, ntok + 1:ntok + 2].rearrange("p c o -> p (c o)"),
                                                                                         in0=ps[b][:, 0:8], in1=vec[:, wv + 1, :], op=ALU.add),
                             reads=[("ps", b)] + VEC, writes=[hkey])
                    else:
                        P.op("dve", lambda e, hf=hf, ntok=ntok: e.memset(hf[:, :, ntok + 1:ntok + 2], 0.0), writes=[hkey])
                    for i, j in enumerate(tiles):
                        sl = self.xslot(kind, j)
                        for cb in range(2):
                            b = self.bank()

                            def tr(e, sl=sl, cb=cb, b=b):
                                ins = None
                                for ci in range(4):
                                    c = cb * 4 + ci
                                    ins = e.transpose(ps[b][:, ci * 128:(ci + 1) * 128], X[:, sl, c * 128:(c + 1) * 128], identf)
                                return ins
                            P.op("pe", tr, reads=[("x", sl), ("identf",)], writes=[("ps", b)])
                            eng = "act" if cb == 0 else "dve"

                            def ev(e, i=i, cb=cb, b=b, eng=eng, wv=wv, hf=hf):
                                ins = None
                                for ci in range(4):
                                    c = cb * 4 + ci
                                    o = hf[:, c, 1 + i * 128:1 + (i + 1) * 128]
                                    i_ = ps[b][:, ci * 128:(ci + 1) * 128]
                                    if eng == "act":
                                        ins = e.activation(out=o, in_=i_, func=AF.Identity, bias=vec[:, wv + 1, c:c + 1], scale=vec[:, wv, c:c + 1])
                                    else:
                                        ins = e.tensor_scalar(out=o, in0=i_, scalar1=vec[:, wv, c:c + 1], scalar2=vec[:, wv + 1, c:c + 1], op0=ALU.mult, op1=ALU.add)
                                return ins
                            P.op(eng, ev, reads=[("ps", b)] + VEC, writes=[hkey])

                def f_group(g, hb_i, between):
                    kind = g["kind"]
                    tiles = g["t"]
                    nt = len(tiles)
                    ntok = nt * 128
                    first = (kind == "ctx" or g["G"] == 0)
                    if first:
                        w = 1 if kind == "ctx" else 0
                        r_ = 2 * li + w
                        self.dma("sp", fbc[:, 0, :], dr["modscr"][r_:r_ + 1, 5 * 1024:6 * 1024].partition_broadcast(128), [("fbc", 0)], "t15", reads=rd)
                    hf = hfT[hb_i % 2]
                    hkey = ("hf", hb_i % 2)
                    pending = []
                    for s in range(11):
                        wu = self.wget(li, S_UP0 + s)
                        W = self.wring[wu].rearrange("p (k n) -> p k n", k=8)
                        for jl in range(2):
                            jch = 2 * s + jl
                            res = []
                            for part in range(2):
                                cbk = part * 2 + jl
                                b = self.fbank_i % 4
                                bh = 4 + self.fbank_i % 4
                                self.fbank_i += 1
                                self.mm(ps[b][:, 0:ntok], [(W[:, kc, cbk * 128:(cbk + 1) * 128], hf[:, kc, 1:ntok + 1]) for kc in range(8)], [("w", wu), hkey], b)
                                hc = 0
                                self.mm(ps[bh][:, hc:hc + 2], [(W[:, kc, cbk * 128:(cbk + 1) * 128], hf[:, kc, 0:ntok + 2:ntok + 1]) for kc in range(8)], [("w", wu), hkey], bh)
                                res.append((b, hc, bh))
                            ta, tg = ta2[jch % 4], tg2[jch % 4]
                            for part, (b, hc, bh) in enumerate(res):
                                cc = jch + 22 * part
                                tx = ta if part == 0 else tg
                                tk = ("ta", jch % 4) if part == 0 else ("tg", jch % 4)

                                def c1(e, b=b, hc=hc, cc=cc, tx=tx, bh=bh, ntok=ntok):
                                    e.activation(out=tx[:, 0:ntok - 1], in_=ps[b][:, 1:ntok], func=AF.Identity, bias=convp[:, cc, 3:4], scale=convp[:, cc, 2:3])
                                    return e.activation(out=tx[:, ntok - 1:ntok], in_=ps[bh][:, hc + 1:hc + 2], func=AF.Identity, bias=convp[:, cc, 3:4], scale=convp[:, cc, 2:3])
                                P.op("act", c1, reads=[("ps", b), ("ps", bh), ("convp",)], writes=[tk])

                                def c2(e, b=b, hc=hc, cc=cc, tx=tx, bh=bh, ntok=ntok):
                                    e.scalar_tensor_tensor(out=tx[:, 1:ntok], in0=ps[b][:, 0:ntok - 1], scalar=convp[:, cc, 0:1], in1=tx[:, 1:ntok], op0=ALU.mult, op1=ALU.add)
                                    return e.scalar_tensor_tensor(out=tx[:, 0:1], in0=ps[bh][:, hc:hc + 1], scalar=convp[:, cc, 0:1], in1=tx[:, 0:1], op0=ALU.mult, op1=ALU.add)
                                P.op("dve", c2, reads=[("ps", b), ("ps", bh), ("convp",), tk], writes=[tk])
                                P.op("dve", lambda e, b=b, cc=cc, tx=tx, ntok=ntok: e.scalar_tensor_tensor(out=tx[:, 0:ntok], in0=ps[b][:, 0:ntok], scalar=convp[:, cc, 1:2],
                                                                                                         in1=tx[:, 0:ntok], op0=ALU.mult, op1=ALU.add),
                                     reads=[("ps", b), ("convp",), tk], writes=[tk])
                            def fin(jch=jch, ta=ta, tg=tg):
                                P.op("act", lambda e: e.activation(out=tg[:, 0:ntok], in_=tg[:, 0:ntok], func=AF.Silu),
                                     reads=[("tg", jch % 4)], writes=[("tg", jch % 4)])
                                P.op("pool", lambda e: e.tensor_tensor(out=prodT[:, jch, 0:ntok], in0=ta[:, 0:ntok], in1=tg[:, 0:ntok], op=ALU.mult),
                                     reads=[("ta", jch % 4), ("tg", jch % 4)], writes=[("prod", jch)])
                            for pf in pending:
                                pf()
                            pending[:] = [fin]
                        self.wrel()
                    for pf in pending:
                        pf()
                    pending[:] = []
                    if between is not None:
                        between()
                    for hh in range(2):
                        bks = [self.bank() for _ in range(nt)]
                        for i3 in range(3):
                            wd_ = self.wget(li, S_DN0 + hh * 3 + i3)
                            W = self.wring[wd_].rearrange("p (k n) -> p k n", k=8)
                            nk = 8 if i3 < 2 else 6
                            for i in range(nt):
                                def fn(e, i=i, i3=i3, nk=nk, W=W, bks=bks):
                                    ins = None
                                    for kk in range(nk):
                                        kc = i3 * 8 + kk
                                        ins = e.matmul(ps[bks[i]][:, :], prodT[:, kc, i * 128:(i + 1) * 128], W[:, kk, :], start=(kc == 0), stop=(kc == 21))
                                    return ins
                                P.op("pe", fn, reads=[("w", wd_)] + [("prod", i3 * 8 + kk) for kk in range(nk)], writes=[("ps", bks[i])])
                            self.wrel()
                        for i, j in enumerate(tiles):
                            sl = self.xslot(kind, j)
                            b = bks[i]
                            P.op("dve", lambda e, b=b, hh=hh: e.tensor_tensor(out=ps[b][:, :], in0=ps[b][:, :], in1=fbc[:, 0, hh * 512:(hh + 1) * 512], op=ALU.mult),
                                 reads=[("ps", b), ("fbc", 0)], writes=[("ps", b)])
                            P.op("dve", lambda e, b=b, hh=hh, sl=sl: e.scalar_tensor_tensor(out=X[:, sl, hh * 512:(hh + 1) * 512], in0=X[:, sl, hh * 512:(hh + 1) * 512],
                                                                                            scalar=ALPHA, in1=ps[b][:, :], op0=ALU.mult, op1=ALU.add),
                                 reads=[("ps", b), ("x", sl)], writes=[("x", sl)])
                            if hh == 1:
                                self.layernorm(X[:, sl, :], ("x", sl), i % 2, affine=(fbc[:, 1, :], fbc[:, 2, :], ("fbc", 1)))
                                if last_layer and kind == "lat" and j < self.out_tiles:
                                    self.dma("sp", dr["out"][j * 128:(j + 1) * 128, :], X[:, sl, :], [("out", j)], "os", reads=[("x", sl)])
                                if last_layer and kind == "ctx" and self.ctx_out:
                                    self.dma("sp", dr["ctxout"][j * 128:(j + 1) * 128, :], X[:, sl, :], [("cout", j)], "os", reads=[("x", sl)])
                fgs = self.f_groups(L)
                f_hf(fgs[0], 0)
                for hb_i, g_ in enumerate(fgs):
                    nx = (lambda k=hb_i + 1: f_hf(fgs[k], k)) if hb_i + 1 < len(fgs) else None
                    f_group(g_, hb_i, nx)
                self.barrier()

            P.wait_all("sp", "os")
            sems = {}
            for s in sorted(P.semnames):
                sems[s] = st.enter_context(nc.semaphore(s))
            blk = st.enter_context(nc.Block())

            @blk.tensor
            def _(e):
                P.replay("pe", e, sems)

            @blk.scalar
            def _(e):
                P.replay("act", e, sems)

            @blk.vector
            def _(e):
                P.replay("dve", e, sems)

            @blk.gpsimd
            def _(e):
                P.replay("pool", e, sems)

            @blk.sync
            def _(e):
                P.replay("sp", e, sems)
        return nc

    eps_ap = LN_EPS

    def layernorm(self, xap, xkey, par, affine):
        P = self.P
        stat = self.stat
        so = 64 + par * 24
        g_bc, b_bc, akey = affine

        def st1(e):
            e.bn_stats(stat[:, so:so + 6], xap[:, 0:512])
            return e.bn_stats(stat[:, so + 6:so + 12], xap[:, 512:1024])
        P.op("dve", st1, reads=[xkey], writes=[("lnst", par, 0)])
        P.op("dve", lambda e: e.bn_aggr(stat[:, so + 12:so + 14], stat[:, so:so + 12].rearrange("p (a b) -> p a b", b=6)),
             reads=[("lnst", par, 0)], writes=[("lnst", par, 1)])
        P.op("act", lambda e: e.activation(out=stat[:, so + 14:so + 15], in_=stat[:, so + 13:so + 14], func=AF.Sqrt, bias=self.eps_ap, scale=1.0),
             reads=[("lnst", par, 1)], writes=[("lnst", par, 2)])
        P.op("dve", lambda e: e.reciprocal(out=stat[:, so + 15:so + 16], in_=stat[:, so + 14:so + 15]), reads=[("lnst", par, 2)], writes=[("lnst", par, 3)])
        P.op("dve", lambda e: e.scalar_tensor_tensor(out=xap, in0=xap, scalar=stat[:, so + 12:so + 13], in1=g_bc, op0=ALU.subtract, op1=ALU.mult),
             reads=[xkey, ("lnst", par, 1), akey], writes=[xkey])
        P.op("dve", lambda e: e.scalar_tensor_tensor(out=xap, in0=xap, scalar=stat[:, so + 15:so + 16], in1=b_bc, op0=ALU.mult, op1=ALU.add),
             reads=[xkey, ("lnst", par, 3), akey], writes=[xkey])


def _local_index(half, n):
    tau = np.arange(n)
    return tau if half == 0 else 4095 - tau


_CACHE = {}


def _prep_static(inp):
    st = {}
    for l in range(DEPTH):
        for half in range(2):
            slots, bint = build_layer_slots(inp, l, half)
            sm = layer_small(inp, l, half)
            st[(l, half)] = (slots, bint, sm)
    return st


def _core_maps(inp, st, layer_ids, x_full, ctx_full, nxt):
    maps = []
    ident = np.eye(128, dtype=np.float32)
    for c in range(8):
        b, half = c // 2, c % 2
        idx = _local_index(half, nxt * 128)
        m = {}
        m["x"] = np.ascontiguousarray(x_full[b][idx])
        cl = ctx_full[b] if half == 0 else ctx_full[b][::-1]
        m["ctx"] = np.ascontiguousarray(cl)
        cf = np.concatenate([inp["c"][b].reshape(8, 128).T, inp["c_ctx"].reshape(8, 128).T], 1)
        m["cfm"] = np.ascontiguousarray(cf, dtype=np.float32)
        m["ident"] = ident
        m["rope"] = rope_table(half, nxt * 128)
        for li, l in enumerate(layer_ids):
            slots, bint, sm = st[(l, half)]
            m["ws%d" % li] = slots
            m["bint%d" % li] = bint
            m["wsT%d" % li] = sm["wsT"]
            m["bsT%d" % li] = sm["bsT"]
            m["convp%d" % li] = sm["convp"].reshape(128, 176)
            m["sgln%d" % li] = sm["sgln"].reshape(1, 1024)
            m["lnrow%d" % li] = sm["lnrow"].reshape(1, 4096)
            m["wada%d" % li] = sm["wada"]
            m["bada%d" % li] = sm["bada"]
        maps.append(m)
    return maps


def _assemble(res, key, ntiles):
    out = np.zeros((4, 4096, D), np.float32)
    for c in range(8):
        b, half = c // 2, c % 2
        o = np.asarray(res.results[c][key])
        if half == 0:
            out[b, 0:ntiles * 128] = o
        else:
            out[b, 4096 - ntiles * 128:] = o[::-1]
    return out


def _assemble_ctx(res):
    out = np.zeros((4, 256, D), np.float32)
    for b in range(4):
        out[b] = np.asarray(res.results[2 * b]["ctxout"])
    return out


FUSED = True


def kernel(**inputs):
    inp = {k: np.asarray(v, dtype=np.float32) for k, v in inputs.items()}
    st = _prep_static(inp)
    if FUSED:
        if "fused" not in _CACHE:
            bld = Builder([dict(nkv=21, nmix=19, nffn=19, ctx="full"), dict(nkv=19, nmix=17, nffn=16, ctx="kv")],
                          n_x_in=21, out_tiles=16, ctx_out=False)
            _CACHE["fused"] = bld.build()
        nc = _CACHE["fused"]
        maps = _core_maps(inp, st, [0, 1], inp["x"], inp["ctx"], 21)
        res = run_bass_kernel_spmd(nc, maps, core_ids=list(range(8)))
        return _assemble(res, "out", 16)
    if "single" not in _CACHE:
        bld = Builder([dict(nkv=19, nmix=17, nffn=16, ctx="full")], n_x_in=19, out_tiles=16, ctx_out=True)
        _CACHE["single"] = bld.build()
    nc = _CACHE["single"]
    x = inp["x"]
    ctx = inp["ctx"]
    for l in range(DEPTH):
        maps = _core_maps(inp, st, [l], x, ctx, 19)
        res = run_bass_kernel_spmd(nc, maps, core_ids=list(range(8)))
        x = _assemble(res, "out", 16)
        ctx = _assemble_ctx(res)
    return x
```
